# Optimizing a Trainium2 kernel written in Bass

```python
import jax, jax.numpy as jnp
from jax import lax
import numpy as np

D_MODEL = 2048
BATCH = 8
SEQ = 2048
DEPTH = 4

GRID_W = 64
CTX_LEN = 256
MIX_W = D_MODEL
GROUP_W = MIX_W // 4
HEAD_DIM = 64
ATTN_SCALE = HEAD_DIM ** -0.5
ROPE_THETA = 10000.0
EPS = 1e-6
GLA_HEADS = 4
GLA_DK = GROUP_W // GLA_HEADS // 2
GLA_DV = GROUP_W // GLA_HEADS
GLA_RANK = 16
GLA_TAU = 16.0
GLA_CHUNK = 64
WIN_HEADS = GROUP_W // HEAD_DIM
WIN_KV = 2
WINDOW = 128
BLOCK = 128
CM_GROUPS = 4
CM_CHUNK = 128
CM_GW = GROUP_W // CM_GROUPS
GA_HEADS = GROUP_W // HEAD_DIM
GA_KV = 2
D_FF = -(-8 * D_MODEL // (3 * 256)) * 256

PROJ_SIZES = (
    ("a_q", GLA_HEADS * GLA_DK), ("a_k", GLA_HEADS * GLA_DK), ("a_v", GLA_HEADS * GLA_DV),
    ("a_g", GROUP_W), ("a_lrf", GLA_RANK), ("a_lrb", GLA_RANK),
    ("b_q", WIN_HEADS * HEAD_DIM), ("b_k", WIN_KV * HEAD_DIM), ("b_v", WIN_KV * HEAD_DIM),
    ("c_u", GROUP_W), ("c_v", GROUP_W),
    ("d_q", GA_HEADS * HEAD_DIM), ("d_k", GA_KV * HEAD_DIM), ("d_v", GA_KV * HEAD_DIM),
)
IN_W = (2 * GLA_HEADS * GLA_DK + GLA_HEADS * GLA_DV + GROUP_W + 2 * GLA_RANK
        + (WIN_HEADS + 2 * WIN_KV) * HEAD_DIM + 2 * GROUP_W + (GA_HEADS + 2 * GA_KV) * HEAD_DIM)

kernel_name = "hybrid_parallel_group_dit_trunk"


def rms_norm(x, g):
    xf = x.astype(jnp.float32)
    y = xf * lax.rsqrt(jnp.mean(xf * xf, axis=-1, keepdims=True) + EPS)
    return (y * g.astype(jnp.float32)).astype(x.dtype)


def layer_norm(x, g, b):
    xf = x.astype(jnp.float32)
    xc = xf - jnp.mean(xf, axis=-1, keepdims=True)
    y = xc * lax.rsqrt(jnp.mean(xc * xc, axis=-1, keepdims=True) + EPS)
    return (y * g.astype(jnp.float32) + b.astype(jnp.float32)).astype(x.dtype)


def modulate(h, shift, scale):
    return h * (1.0 + scale) + shift


def heads(t, h):
    return t.reshape(t.shape[:-1] + (h, t.shape[-1] // h))


def split_proj(p):
    out, off = {}, 0
    for name, w in PROJ_SIZES:
        out[name] = p[..., off:off + w]
        off += w
    return out


def axial_rope_tables(length):
    rows = length // GRID_W
    row = jnp.repeat(jnp.arange(rows), GRID_W).astype(jnp.float32)
    col = jnp.tile(jnp.arange(GRID_W), rows).astype(jnp.float32)
    quarter = HEAD_DIM // 4
    inv = ROPE_THETA ** (-jnp.arange(quarter, dtype=jnp.float32) / quarter)
    ar, ac = row[:, None] * inv, col[:, None] * inv
    ang = jnp.concatenate([ar, ar, ac, ac], axis=-1)
    return jnp.cos(ang), jnp.sin(ang)


def apply_rope(x, cos, sin):
    x1, x2, x3, x4 = jnp.split(x, 4, axis=-1)
    rot = jnp.concatenate([-x2, x1, -x4, x3], axis=-1)
    return x * cos[:, None, :].astype(x.dtype) + rot * sin[:, None, :].astype(x.dtype)


def gla_chunked(q, k, v, log_a, s0):
    b_, length, h_, dk = q.shape
    dv = v.shape[-1]
    nc = length // GLA_CHUNK

    def chunks(t):
        return jnp.moveaxis(t.reshape((b_, nc, GLA_CHUNK) + t.shape[2:]), 1, 0)

    tri = jnp.tril(jnp.ones((GLA_CHUNK, GLA_CHUNK), dtype=bool))

    def step(s, inp):
        qc, kc, vc, ac = inp
        cum = jnp.cumsum(ac, axis=1)
        diff = jnp.where(tri[None, :, :, None, None], cum[:, :, None] - cum[:, None, :], -jnp.inf)
        attn = jnp.einsum('bihk,bjhk,bijhk->bhij', qc, kc, jnp.exp(diff))
        o = (jnp.einsum('bhij,bjhv->bihv', attn, vc)
             + jnp.einsum('bihk,bhkv->bihv', qc * jnp.exp(cum), s))
        last = cum[:, -1]
        s = (jnp.exp(last)[..., None] * s
             + jnp.einsum('bjhk,bjhv->bhkv', kc * jnp.exp(last[:, None] - cum), vc))
        return s, o

    s, o = lax.scan(step, s0, (chunks(q), chunks(k), chunks(v), chunks(log_a)))
    return jnp.moveaxis(o, 0, 1).reshape(b_, length, h_, dv), s


def gla_mixer(pl, pc, gate_up, gate_b, gain, need_ctx):
    def qkv(p):
        q = heads(p['a_q'], GLA_HEADS).astype(jnp.float32) * GLA_DK ** -0.5
        k = heads(p['a_k'], GLA_HEADS).astype(jnp.float32)
        v = heads(p['a_v'], GLA_HEADS).astype(jnp.float32)
        return q, k, v

    def log_decay(lr, d):
        z = jnp.einsum('blr,rk->blk', lr.astype(jnp.float32), gate_up[d].astype(jnp.float32))
        z = z + gate_b[d].astype(jnp.float32)
        return heads(jax.nn.log_sigmoid(z) / GLA_TAU, GLA_HEADS)

    def flip(t):
        return jnp.flip(t, axis=1)

    qc, kc, vc = qkv(pc)
    ql, kl, vl = qkv(pl)
    s0 = jnp.zeros((qc.shape[0], GLA_HEADS, GLA_DK, GLA_DV), jnp.float32)
    oc_f, s_f = gla_chunked(qc, kc, vc, log_decay(pc['a_lrf'], 0), s0)
    ol_f, _ = gla_chunked(ql, kl, vl, log_decay(pl['a_lrf'], 0), s_f)
    oc_b, s_b = gla_chunked(flip(qc), flip(kc), flip(vc), flip(log_decay(pc['a_lrb'], 1)), s0)
    ol_b, _ = gla_chunked(flip(ql), flip(kl), flip(vl), flip(log_decay(pl['a_lrb'], 1)), s_b)

    def finish(o, g):
        o = rms_norm(o, gain.reshape(GLA_HEADS, GLA_DV))
        return (o.reshape(o.shape[:2] + (GROUP_W,)) * jax.nn.silu(g.astype(jnp.float32))).astype(g.dtype)

    out_l = finish(ol_f + flip(ol_b), pl['a_g'])
    out_c = finish(oc_f + flip(oc_b), pc['a_g']) if need_ctx else None
    return out_l, out_c


def window_attn(q, k, v, k_ctx, v_ctx, sink):
    b_, length = q.shape[:2]
    nb = length // BLOCK
    g_ = WIN_HEADS // WIN_KV
    qb = q.reshape(b_, nb, BLOCK, WIN_KV, g_, HEAD_DIM)

    def band(t):
        tp = jnp.pad(t, ((0, 0), (BLOCK, BLOCK), (0, 0), (0, 0)))
        tp = tp.reshape(b_, nb + 2, BLOCK, WIN_KV, HEAD_DIM)
        return jnp.concatenate([tp[:, :-2], tp[:, 1:-1], tp[:, 2:]], axis=2)

    kb, vb = band(k), band(v)
    s_win = jnp.einsum('bnqhgd,bnshd->bnhgqs', qb, kb).astype(jnp.float32) * ATTN_SCALE
    qpos = jnp.arange(nb)[:, None] * BLOCK + jnp.arange(BLOCK)[None, :]
    kpos = (jnp.arange(nb)[:, None] - 1) * BLOCK + jnp.arange(3 * BLOCK)[None, :]
    valid = ((jnp.abs(qpos[:, :, None] - kpos[:, None, :]) <= WINDOW)
             & (kpos[:, None, :] >= 0) & (kpos[:, None, :] < length))
    s_win = jnp.where(valid[None, :, None, None], s_win, -jnp.inf)
    s_ctx = jnp.einsum('bnqhgd,bshd->bnhgqs', qb, k_ctx).astype(jnp.float32) * ATTN_SCALE
    s_sink = jnp.broadcast_to(sink.reshape(WIN_KV, g_)[None, None, :, :, None, None].astype(jnp.float32),
                              s_win.shape[:-1] + (1,))
    p = jax.nn.softmax(jnp.concatenate([s_win, s_ctx, s_sink], axis=-1), axis=-1)
    nw, nctx = 3 * BLOCK, k_ctx.shape[1]
    p_win = p[..., :nw].astype(v.dtype)
    p_ctx = p[..., nw:nw + nctx].astype(v.dtype)
    o = (jnp.einsum('bnhgqs,bnshd->bnqhgd', p_win, vb)
         + jnp.einsum('bnhgqs,bshd->bnqhgd', p_ctx, v_ctx))
    return o.reshape(b_, length, WIN_HEADS * HEAD_DIM)


def global_attn(q, k, v, k_ctx, v_ctx):
    b_, length = q.shape[:2]
    nb = length // BLOCK
    g_ = GA_HEADS // GA_KV
    k_all = jnp.concatenate([k_ctx, k], axis=1)
    v_all = jnp.concatenate([v_ctx, v], axis=1)
    qb = jnp.moveaxis(q.reshape(b_, nb, BLOCK, GA_KV, g_, HEAD_DIM), 1, 0)

    def one_block(qblk):
        s = jnp.einsum('bqhgd,bshd->bhgqs', qblk, k_all).astype(jnp.float32) * ATTN_SCALE
        p = jax.nn.softmax(s, axis=-1).astype(v_all.dtype)
        return jnp.einsum('bhgqs,bshd->bqhgd', p, v_all)

    o = lax.map(one_block, qb)
    return jnp.moveaxis(o, 0, 1).reshape(b_, length, GA_HEADS * HEAD_DIM)


def context_attn(q, k, v, sink):
    b_, clen, hq, hd = q.shape
    kv = k.shape[2]
    g_ = hq // kv
    s = jnp.einsum('bqhgd,bshd->bhgqs', q.reshape(b_, clen, kv, g_, hd), k).astype(jnp.float32) * ATTN_SCALE
    if sink is not None:
        sk = jnp.broadcast_to(sink.reshape(kv, g_, 1, 1).astype(jnp.float32), s.shape[:-1] + (1,))
        p = jax.nn.softmax(jnp.concatenate([s, sk], axis=-1), axis=-1)[..., :-1]
    else:
        p = jax.nn.softmax(s, axis=-1)
    return jnp.einsum('bhgqs,bshd->bqhgd', p.astype(v.dtype), v).reshape(b_, clen, hq * hd)


def chunk_mlp(u, v, ln_g, ln_b, w_s, b_s):
    b_, length = u.shape[:2]
    nc = length // CM_CHUNK
    u = jax.nn.gelu(u)
    v = layer_norm(jax.nn.gelu(v), ln_g, ln_b)
    vb = v.reshape(b_, nc, CM_CHUNK, CM_GROUPS, CM_GW)
    s = jnp.einsum('gpq,bnqgc->bnpgc', w_s, vb) + b_s.T[:, :, None]
    return u * s.reshape(b_, length, GROUP_W)


def swiglu(h, w_in, w_out):
    gate, up = jnp.split(jnp.einsum('bld,df->blf', h, w_in), 2, axis=-1)
    return jnp.einsum('blf,fd->bld', jax.nn.silu(gate) * up, w_out)


def trunk_layer(x, ctx, mod_l, mod_c, cos, sin, norm_g, w_in, gla_gate_up, gla_gate_b, win_sink,
                cm_ln_g, cm_ln_b, cm_ws, cm_bs, qk_g, mix_g, w_out, w_ffn_in, w_ffn_out, need_ctx):
    sh1, sc1, gt1, sh2, sc2, gt2 = jnp.split(mod_l[:, None, :], 6, axis=-1)
    csh1, csc1, cgt1, csh2, csc2, cgt2 = jnp.split(mod_c, 6, axis=-1)
    hl = modulate(rms_norm(x, norm_g[0]), sh1, sc1)
    hc = modulate(rms_norm(ctx, norm_g[0]), csh1, csc1)
    pl = split_proj(jnp.einsum('bld,de->ble', hl, w_in))
    pc = split_proj(jnp.einsum('bld,de->ble', hc, w_in))
    ga, gb, gc, gd = jnp.split(mix_g, 4)

    a_l, a_c = gla_mixer(pl, pc, gla_gate_up, gla_gate_b, ga, need_ctx)
    kb_c, vb_c = heads(pc['b_k'], WIN_KV), heads(pc['b_v'], WIN_KV)
    b_l = window_attn(apply_rope(heads(pl['b_q'], WIN_HEADS), cos, sin),
                      apply_rope(heads(pl['b_k'], WIN_KV), cos, sin),
                      heads(pl['b_v'], WIN_KV), kb_c, vb_c, win_sink)
    c_l = chunk_mlp(pl['c_u'], pl['c_v'], cm_ln_g, cm_ln_b, cm_ws, cm_bs)
    kd_c = rms_norm(heads(pc['d_k'], GA_KV), qk_g[1])
    vd_c = heads(pc['d_v'], GA_KV)
    d_l = global_attn(apply_rope(rms_norm(heads(pl['d_q'], GA_HEADS), qk_g[0]), cos, sin),
                      apply_rope(rms_norm(heads(pl['d_k'], GA_KV), qk_g[1]), cos, sin),
                      heads(pl['d_v'], GA_KV), kd_c, vd_c)

    y_l = jnp.concatenate([a_l, rms_norm(b_l, gb), rms_norm(c_l, gc), rms_norm(d_l, gd)], axis=-1)
    x = x + gt1 * rms_norm(jnp.einsum('ble,ed->bld', y_l, w_out), norm_g[1])
    x = x + gt2 * rms_norm(swiglu(modulate(rms_norm(x, norm_g[2]), sh2, sc2), w_ffn_in, w_ffn_out), norm_g[3])

    if need_ctx:
        b_c = context_attn(heads(pc['b_q'], WIN_HEADS), kb_c, vb_c, win_sink)
        c_c = chunk_mlp(pc['c_u'], pc['c_v'], cm_ln_g, cm_ln_b, cm_ws, cm_bs)
        d_c = context_attn(rms_norm(heads(pc['d_q'], GA_HEADS), qk_g[0]), kd_c, vd_c, None)
        y_c = jnp.concatenate([a_c, rms_norm(b_c, gb), rms_norm(c_c, gc), rms_norm(d_c, gd)], axis=-1)
        ctx = ctx + cgt1 * rms_norm(jnp.einsum('ble,ed->bld', y_c, w_out), norm_g[1])
        ctx = ctx + cgt2 * rms_norm(swiglu(modulate(rms_norm(ctx, norm_g[2]), csh2, csc2), w_ffn_in, w_ffn_out), norm_g[3])
    return x, ctx


def setup_inputs(seed: int = 0) -> dict:
    key = jax.random.key(seed)
    ks = jax.random.split(key, 20)
    f32 = jnp.float32

    def nrm(k, shape, scale):
        return jax.random.normal(k, shape, f32) * scale

    return {
        "x": nrm(ks[0], (BATCH, SEQ, D_MODEL), 1.0),
        "c": nrm(ks[1], (BATCH, D_MODEL), 1.0),
        "ctx": nrm(ks[2], (BATCH, CTX_LEN, D_MODEL), 1.0),
        "c_ctx": nrm(ks[3], (D_MODEL,), 1.0),
        "w_mod": nrm(ks[4], (DEPTH, D_MODEL, 6 * D_MODEL), 0.5 * D_MODEL ** -0.5),
        "b_mod": nrm(ks[5], (DEPTH, 6 * D_MODEL), 0.02),
        "norm_g": 1.0 + nrm(ks[6], (DEPTH, 4, D_MODEL), 0.02),
        "w_in": nrm(ks[7], (DEPTH, D_MODEL, IN_W), D_MODEL ** -0.5),
        "gla_gate_up": nrm(ks[8], (DEPTH, 2, GLA_RANK, GLA_HEADS * GLA_DK), GLA_RANK ** -0.5),
        "gla_gate_b": nrm(ks[9], (DEPTH, 2, GLA_HEADS * GLA_DK), 0.1),
        "win_sink": nrm(ks[10], (DEPTH, WIN_HEADS), 0.5),
        "cm_ln_g": 1.0 + nrm(ks[11], (DEPTH, GROUP_W), 0.02),
        "cm_ln_b": nrm(ks[12], (DEPTH, GROUP_W), 0.02),
        "cm_ws": nrm(ks[13], (DEPTH, CM_GROUPS, CM_CHUNK, CM_CHUNK), CM_CHUNK ** -0.5),
        "cm_bs": nrm(ks[14], (DEPTH, CM_GROUPS, CM_CHUNK), 0.1),
        "qk_g": 1.0 + nrm(ks[15], (DEPTH, 2, HEAD_DIM), 0.02),
        "mix_g": 1.0 + nrm(ks[16], (DEPTH, MIX_W), 0.02),
        "w_out": nrm(ks[17], (DEPTH, MIX_W, D_MODEL), MIX_W ** -0.5),
        "w_ffn_in": nrm(ks[18], (DEPTH, D_MODEL, 2 * D_FF), D_MODEL ** -0.5),
        "w_ffn_out": nrm(ks[19], (DEPTH, D_FF, D_MODEL), D_FF ** -0.5),
    }


def reference(x, c, ctx, c_ctx, w_mod, b_mod, norm_g, w_in, gla_gate_up, gla_gate_b, win_sink,
              cm_ln_g, cm_ln_b, cm_ws, cm_bs, qk_g, mix_g, w_out, w_ffn_in, w_ffn_out):
    cos, sin = axial_rope_tables(x.shape[1])
    sc = jax.nn.silu(c)
    sc_ctx = jax.nn.silu(c_ctx)
    for l in range(DEPTH):
        mod_l = jnp.einsum('bd,de->be', sc, w_mod[l]) + b_mod[l]
        mod_c = jnp.einsum('d,de->e', sc_ctx, w_mod[l]) + b_mod[l]
        x, ctx = trunk_layer(x, ctx, mod_l, mod_c, cos, sin, norm_g[l], w_in[l], gla_gate_up[l],
                             gla_gate_b[l], win_sink[l], cm_ln_g[l], cm_ln_b[l], cm_ws[l], cm_bs[l],
                             qk_g[l], mix_g[l], w_out[l], w_ffn_in[l], w_ffn_out[l],
                             need_ctx=(l < DEPTH - 1))
    return x
```

```python
from contextlib import ExitStack
import numpy as np
import concourse.bass as bass
import concourse.mybir as mybir
from concourse.bass_utils import run_bass_kernel_spmd

F32 = mybir.dt.float32
BF16 = mybir.dt.bfloat16
AF = mybir.ActivationFunctionType
ALU = mybir.AluOpType

D = 2048
TL = 2048
TCX = 256
T = TL + TCX
KC = 16
DFF = 5632
FC = 44
INW = 4128
EPS = 1e-6
NTILE = T // 128
TB5 = [(0, 512), (512, 512), (1024, 512), (1536, 512), (2048, 256)]
NDS = 40
SAME_ENGINE_WAIT = True
GLA_W = 2


def bc(ap, axis, shape):
    return ap.unsqueeze(axis).to_broadcast(list(shape))


class Res:
    __slots__ = ("w", "r", "name")

    def __init__(self, name=""):
        self.w = {}
        self.r = {}
        self.name = name


class Sched:
    ENG = ("pe", "act", "dve", "pool", "sp")

    def __init__(self, nc, es):
        self.nc = nc
        self.eng = {"pe": nc.tensor, "act": nc.scalar, "dve": nc.vector, "pool": nc.gpsimd, "sp": nc.sync}
        self.sem = {e: es.enter_context(nc.semaphore("s_" + e)) for e in self.ENG}
        self.cnt = {e: 0 for e in self.ENG}
        self.known = {e: {} for e in self.ENG}
        self.dsem = [es.enter_context(nc.semaphore("d%d" % i)) for i in range(NDS)]
        self.dcnt = [0] * NDS
        self.drr = 0
        self.deferred = []
        self.nwaits = 0

    def _semh(self, key):
        return self.sem[key[1]] if key[0] == "e" else self.dsem[key[1]]

    def _wait(self, e, deps):
        kn = self.known[e]
        for key, val in deps.items():
            if kn.get(key, 0) >= val:
                continue
            if key == ("e", e) and (e == "pe" or not SAME_ENGINE_WAIT):
                continue
            self.eng[e].wait_ge(self._semh(key), val)
            kn[key] = val
            self.nwaits += 1

    @staticmethod
    def _deps(reads, writes, dwrites):
        d = {}
        for r in reads:
            for k, v in r.w.items():
                if d.get(k, 0) < v:
                    d[k] = v
        for w in writes:
            for k, v in w.w.items():
                if d.get(k, 0) < v:
                    d[k] = v
            for k, v in w.r.items():
                if d.get(k, 0) < v:
                    d[k] = v
        for w in dwrites:
            for k, v in w.r.items():
                if d.get(k, 0) < v:
                    d[k] = v
        return d

    @staticmethod
    def _mark(ev, reads, writes, dwrites):
        k, v = ev
        for r in reads:
            if r.r.get(k, 0) < v:
                r.r[k] = v
        for w in writes:
            w.w = {k: v}
            w.r = {}
        for w in dwrites:
            if w.w.get(k, 0) < v:
                w.w[k] = v

    def op(self, e, fn, reads=(), writes=(), dwrites=()):
        self._wait(e, self._deps(reads, writes, dwrites))
        ins = fn(self.eng[e])
        self.cnt[e] += 1
        ins.then_inc(self.sem[e], 1)
        self._mark((("e", e), self.cnt[e]), reads, writes, dwrites)

    def dma(self, q, out, in_, reads=(), writes=(), dwrites=()):
        i = self.drr
        self.drr = (i + 1) % NDS
        deps = self._deps(reads, writes, dwrites)
        if self.dcnt[i] > 0:
            deps[("d", i)] = max(deps.get(("d", i), 0), self.dcnt[i])
        self._wait(q, deps)
        self.eng[q].dma_start(out=out, in_=in_).then_inc(self.dsem[i], 16)
        self.dcnt[i] += 16
        self._mark((("d", i), self.dcnt[i]), reads, writes, dwrites)

    def barrier(self):
        self.run_deferred(all_=True)
        for e in self.ENG:
            deps = {("e", o): self.cnt[o] for o in self.ENG if self.cnt[o] > 0}
            for i in range(NDS):
                if self.dcnt[i] > 0:
                    deps[("d", i)] = self.dcnt[i]
            self._wait(e, deps)

    def defer(self, fn):
        self.deferred.append(fn)

    def run_deferred(self, all_=False):
        while True:
            cur, self.deferred = self.deferred, []
            for fn in cur:
                nxt = fn()
                if nxt is not None:
                    self.deferred.append(nxt)
            if not all_ or not self.deferred:
                break


_UID = [0]


def _uid():
    _UID[0] += 1
    return _UID[0]


class Rot:
    def __init__(self, nc, es, name, shape, dt, n):
        u = _uid()
        self.t = [es.enter_context(nc.sbuf_tensor("%s_%d_%d" % (name, u, i), list(shape), dt)) for i in range(n)]
        self.r = [Res(name) for _ in range(n)]
        self.i = 0

    def next(self):
        i = self.i
        self.i = (i + 1) % len(self.t)
        return self.t[i], self.r[i]


class PsRot:
    def __init__(self, tiles, ress):
        self.t = tiles
        self.r = ress
        self.i = 0

    def next(self):
        i = self.i
        self.i = (i + 1) % len(self.t)
        return self.t[i], self.r[i]


def build(depth=4, debug=False, stop_after=None):
    nc = bass.Bass("TRN2", target_bir_lowering=False)
    es = ExitStack()

    def inp(name, shape, dt=F32):
        return nc.dram_tensor(name, list(shape), dt, kind="ExternalInput").ap()

    dbg_kind = "ExternalOutput" if debug else "Internal"

    def scr(name, shape, dt):
        return nc.dram_tensor(name, list(shape), dt, kind=dbg_kind).ap()

    xT0 = inp("xT0", [KC, 128, T])
    cvec = inp("cvec", [128, KC, 2])
    w_mod = inp("w_mod", [4, D, 6 * D])
    bm_d = inp("bm", [128, 4, 96])
    ng_d = inp("ng", [128, 4, 4, KC])
    w_in = inp("w_in", [4, D, INW])
    w_out = inp("w_out", [4, D, D])
    w_f1 = inp("w_f1", [4, D, 2 * DFF])
    w_f2 = inp("w_f2", [4, DFF, D])
    gub_d = inp("gub", [4, 33, 512])
    sink_d = inp("sinkb", [128, 4, 8])
    lng_d = inp("lng", [128, 4, 512])
    lnb_d = inp("lnb", [128, 4, 512])
    wsT_d = inp("wsT", [4, 128, 4, 128])
    bsb_d = inp("bsb", [128, 4, 4, 128])
    qkg_d = inp("qkg", [128, 4, 2])
    ga_d = inp("ga", [128, 4, 4])
    gbb_d = inp("gbb", [128, 4, 512])
    gc_d = inp("gc", [128, 4, 4])
    gdb_d = inp("gdb", [128, 4, 512])
    cst_d = inp("cst", [128, 11, 128])
    cos_d = inp("cosT", [128, TL])
    sin_d = inp("sinT", [128, TL])
    outT = nc.dram_tensor("outT", [KC, 128, TL], F32, kind="ExternalOutput").ap()

    xT = scr("xT", [KC, 128, T], F32)
    oT = scr("oT", [KC, 128, T], F32)
    yT = scr("yT", [KC, 128, T], BF16)
    actT = scr("actT", [FC, 128, T], BF16)
    aqT = scr("aqT", [4, 64, T], BF16)
    akT = scr("akT", [4, 64, T], BF16)
    ak_tok = scr("ak_tok", [T, 256], BF16)
    av_tok = scr("av_tok", [T, 512], BF16)
    agT = scr("agT", [4, 128, T], BF16)
    lrT = scr("lrT", [32, T], F32)
    bqT = scr("bqT", [8, 64, T], BF16)
    bkT = scr("bkT", [2, 64, T], BF16)
    bv_tok = scr("bv_tok", [T, 128], BF16)
    cuT = scr("cuT", [4, 128, T], BF16)
    cv_tok = scr("cv_tok", [T, 512], BF16)
    dqT = scr("dqT", [8, 64, T], BF16)
    dkT = scr("dkT", [2, 64, T], BF16)
    dv_tok = scr("dv_tok", [T, 128], BF16)
    R = {n: Res(n) for n in ("xT", "oT", "yT", "actT", "aqT", "akT", "ak_tok", "av_tok", "agT", "lrT", "bqT", "bkT",
                             "bv_tok", "cuT", "cv_tok", "dqT", "dkT", "dv_tok", "outT")}

    S = Sched(nc, es)

    def sb(name, shape, dt, st=es):
        return st.enter_context(nc.sbuf_tensor("%s_%d" % (name, _uid()), list(shape), dt))

    big = sb("big", [128, KC * T], BF16)
    r_big = Res("big")
    bigk = big[:].rearrange("p (k t) -> p k t", k=KC)
    cst_f = sb("cst_f", [128, 11, 128], F32)
    cst_b = sb("cst_b", [128, 11, 128], BF16)
    r_cst = Res("cst")
    modv = sb("modv", [128, 4, 96, 2], F32)
    r_modv = Res("modv")
    ng_s = sb("ng_s", [128, 4, 4, KC], F32)
    bm_s = sb("bm_s", [128, 4, 96], F32)
    r_small = Res("small")
    lay = {n: sb("lay_" + n, [128, KC, 2], F32) for n in ("A1", "B1", "G1", "A2", "B2", "G2")}
    r_lay = {1: Res("lay1"), 2: Res("lay2")}
    acc = sb("acc", [128, T], F32)
    r_acc = Res("acc")
    epsT = sb("epsT", [128, 1], F32)
    ps_t = [es.enter_context(nc.psum_tensor("ps%d" % i, [128, 512], F32)) for i in range(7)]
    ps_b = es.enter_context(nc.psum_tensor("psb", [128, 1024], BF16))
    ps_r = [Res("ps%d" % i) for i in range(7)]
    r_psb = Res("psb")
    psA = PsRot(ps_t[0:4], ps_r[0:4])
    psE = PsRot(ps_t[4:7], ps_r[4:7])

    IDENT, TRIF, TRIB, MASKF, MASKB, WLO, WHI, ROPE, ONES, ROPE2, ONES2 = range(11)

    S.dma("sp", cst_f[:], cst_d, writes=[r_cst])
    S.op("dve", lambda e: e.tensor_copy(out=cst_b[:], in_=cst_f[:]), reads=[r_cst], writes=[r_cst])
    S.op("dve", lambda e: e.memset(epsT[:], EPS), writes=[r_cst])
    S.dma("sp", ng_s[:], ng_d, writes=[r_small])
    S.dma("sp", bm_s[:], bm_d, writes=[r_small])
    ones_b = cst_b[:, ONES, :]
    ones_f = cst_f[:, ONES, :]
    ident_b = cst_b[:, IDENT, :]

    def rstd_from(dst, src, scale, rd, wr, n_part=128):
        S.op("act", lambda e: e.activation(out=dst, in_=src, func=AF.Ln, bias=epsT[0:n_part, :], scale=scale),
             reads=rd + [r_cst], writes=wr)
        S.op("act", lambda e: e.activation(out=dst, in_=dst, func=AF.Exp, scale=-0.5), reads=wr, writes=wr)

    scb = sb("m_scb", [128, KC, 2], BF16)
    r_sc = Res("scb")

    def mod_gen(l, wm, pm, rpm):
        pmv = pm[:, 0:192].rearrange("p (j s) -> p j s", s=2)
        for piece in range(24):
            wt, rw = wm.next()
            S.dma("pool", wt[:], w_mod[l, :, piece * 512:(piece + 1) * 512].rearrange("(k p) n -> p k n", p=128), writes=[rw])
            yield
            for j4 in range(4):
                j = piece * 4 + j4
                for kc in range(KC):
                    S.op("pe", lambda e, kc=kc, j=j, j4=j4, wt=wt: e.matmul(
                        pmv[:, j, :], wt[:, kc, j4 * 128:(j4 + 1) * 128], scb[:, kc, :],
                        start=(kc == 0), stop=(kc == KC - 1)), reads=[rw, r_sc], writes=[rpm])
        S.op("dve", lambda e: e.tensor_tensor(out=modv[:, l, :, :], in0=pmv, in1=bc(bm_s[:, l, :], 2, [128, 96, 2]), op=ALU.add),
             reads=[rpm, r_small], dwrites=[r_modv])

    def stage_mod():
        with ExitStack() as st:
            cv = sb("m_cv", [128, KC, 2], F32, st)
            wm = Rot(nc, st, "m_w", [128, KC, 512], BF16, 2)
            S.dma("sp", cv[:], cvec, writes=[r_sc])
            S.op("act", lambda e: e.activation(out=scb[:], in_=cv[:], func=AF.Silu), reads=[r_sc], writes=[r_sc])
            pm, rpm = psE.next()
            for _ in mod_gen(0, wm, pm, rpm):
                pass
            S.barrier()

    def layer_scalars(l, part):
        def g(i):
            return bc(ng_s[:, l, i, :], 2, [128, KC, 2])
        mv = lambda c: modv[:, l, c * KC:(c + 1) * KC, :]
        rd = [r_modv, r_small]
        o = 0 if part == 1 else 3
        sfx = "1" if part == 1 else "2"
        rl = r_lay[part]
        S.op("dve", lambda e: e.scalar_tensor_tensor(out=lay["A" + sfx][:], in0=mv(o + 1), scalar=1.0, in1=g(0 if part == 1 else 2),
                                                     op0=ALU.add, op1=ALU.mult), reads=rd, writes=[rl])
        S.op("dve", lambda e: e.tensor_copy(out=lay["B" + sfx][:], in_=mv(o + 0)), reads=rd, dwrites=[rl])
        S.op("dve", lambda e: e.tensor_tensor(out=lay["G" + sfx][:], in0=mv(o + 2), in1=g(1 if part == 1 else 3), op=ALU.mult),
             reads=rd, dwrites=[rl])

    def interleave(gens, width):
        active = []
        it = iter(gens)
        while True:
            while len(active) < width:
                g = next(it, None)
                if g is None:
                    break
                active.append(g)
            if not active:
                break
            for g in list(active):
                try:
                    next(g)
                except StopIteration:
                    active.remove(g)

    def stage_resnorm(x_src, r_xsrc, res_G, norm_AB, x_dst, r_xdst, final=False, tlim=T):
        NB = 128
        WIDTH = 4
        with ExitStack() as st:
            xb_p = Rot(nc, st, "rn_x", [128, KC, NB], F32, WIDTH + 1)
            ob_p = Rot(nc, st, "rn_o", [128, KC, NB], F32, WIDTH + 1)
            sq_p = Rot(nc, st, "rn_sq", [128, KC, NB], BF16, WIDTH)
            rs_p = Rot(nc, st, "rn_rs", [128, NB], F32, 2 * WIDTH)
            psR = PsRot(ps_t[0:7], ps_r[0:7])
            nblk = (TL if final else tlim) // NB

            def blk(b):
                t0 = b * NB
                s = 1 if t0 >= TL else 0
                xb, rx = xb_p.next()
                ob, ro = ob_p.next()
                S.dma("sp", xb[:], x_src[:, :, t0:t0 + NB].rearrange("k p t -> p k t"), writes=[rx])
                if res_G:
                    S.dma("sp", ob[:], oT[:, :, t0:t0 + NB].rearrange("k p t -> p k t"), writes=[ro])
                    yield
                    pr, rpr = psR.next()
                    S.op("pe", lambda e: e.matmul(pr[:, 0:NB], ones_f, acc[:, t0:t0 + NB], start=True, stop=True),
                         reads=[r_acc, r_cst], writes=[rpr])
                    yield
                    rs, rrs = rs_p.next()
                    S.op("act", lambda e: e.activation(out=rs[:], in_=pr[:, 0:NB], func=AF.Ln, bias=epsT[:, :], scale=1.0 / D),
                         reads=[rpr, r_cst], writes=[rrs])
                    yield
                    S.op("act", lambda e: e.activation(out=rs[:], in_=rs[:], func=AF.Exp, scale=-0.5), writes=[rrs])
                    yield
                    for kc in range(KC):
                        S.op("dve", lambda e, kc=kc: e.scalar_tensor_tensor(out=ob[:, kc, :], in0=ob[:, kc, :], scalar=lay[res_G][:, kc, s:s + 1],
                                                                           in1=rs[:], op0=ALU.mult, op1=ALU.mult),
                             reads=[rrs, r_lay[int(res_G[1])]], writes=[ro] if kc == 0 else (), dwrites=() if kc == 0 else [ro])
                    yield
                    S.op("pool", lambda e: e.tensor_tensor(out=xb[:, 0:10, :], in0=xb[:, 0:10, :], in1=ob[:, 0:10, :], op=ALU.add),
                         reads=[ro], writes=[rx])
                    S.op("dve", lambda e: e.tensor_tensor(out=xb[:, 10:KC, :], in0=xb[:, 10:KC, :], in1=ob[:, 10:KC, :], op=ALU.add),
                         reads=[ro], dwrites=[rx])
                yield
                if x_dst is not None:
                    S.dma("sp", x_dst[:, :, t0:t0 + NB].rearrange("k p t -> p k t"), xb[:], reads=[rx], dwrites=[r_xdst])
                if norm_AB:
                    A, B = norm_AB
                    sq, rsq = sq_p.next()
                    S.op("act", lambda e: e.activation(out=sq[:], in_=xb[:], func=AF.Square), reads=[rx], writes=[rsq])
                    yield
                    pn, rpn = psR.next()
                    for kc in range(KC):
                        S.op("pe", lambda e, kc=kc: e.matmul(pn[:, 0:NB], ones_b, sq[:, kc, :], start=(kc == 0), stop=(kc == KC - 1)),
                             reads=[rsq, r_cst], writes=[rpn])
                    yield
                    rs, rrs = rs_p.next()
                    S.op("act", lambda e: e.activation(out=rs[:], in_=pn[:, 0:NB], func=AF.Ln, bias=epsT[:, :], scale=1.0 / D),
                         reads=[rpn, r_cst], writes=[rrs])
                    yield
                    S.op("act", lambda e: e.activation(out=rs[:], in_=rs[:], func=AF.Exp, scale=-0.5), writes=[rrs])
                    yield
                    S.op("dve", lambda e: e.tensor_tensor(out=ob[:], in0=xb[:], in1=bc(rs[:], 1, [128, KC, NB]), op=ALU.mult),
                         reads=[rx, rrs], writes=[ro])
                    yield
                    for kc in range(KC):
                        eng_ = "act" if kc % 4 != 3 else "pool"
                        if eng_ == "act":
                            S.op("act", lambda e, kc=kc: e.activation(out=bigk[:, kc, t0:t0 + NB], in_=ob[:, kc, :], func=AF.Identity,
                                                                      bias=lay[B][:, kc, s:s + 1], scale=lay[A][:, kc, s:s + 1]),
                                 reads=[ro, r_lay[int(A[1])]], dwrites=[r_big])
                        else:
                            S.op("pool", lambda e, kc=kc: e.tensor_scalar(out=bigk[:, kc, t0:t0 + NB], in0=ob[:, kc, :],
                                                                          scalar1=lay[A][:, kc, s:s + 1], scalar2=lay[B][:, kc, s:s + 1],
                                                                          op0=ALU.mult, op1=ALU.add),
                                 reads=[ro, r_lay[int(A[1])]], dwrites=[r_big])
                yield

            interleave((blk(b) for b in range(nblk)), WIDTH)
            S.barrier()

    def load_w(rot, Wl, col0, ncols, nk):
        wt, rw = rot.next()
        S.dma("pool", wt[:, 0:nk, 0:ncols], Wl[:, col0:col0 + ncols].rearrange("(k p) n -> p k n", p=128), writes=[rw])
        return wt, rw

    def fm_acc(wt, rw, c0, m, t0, n, nk=KC, inview=None):
        iv = bigk if inview is None else inview
        ps, rps = psA.next()
        for kc in range(nk):
            S.op("pe", lambda e, kc=kc: e.matmul(ps[0:m, 0:n], wt[:, kc, c0:c0 + m], iv[:, kc, t0:t0 + n],
                                                  start=(kc == 0), stop=(kc == nk - 1)), reads=[rw, r_big], writes=[rps])
        S.run_deferred()
        return ps, rps

    def tm_acc(wt, rw, c0, n, tt):
        ps, rps = psA.next()
        for kc in range(KC):
            S.op("pe", lambda e, kc=kc: e.matmul(ps[:, 0:n], bigk[:, kc, tt * 128:(tt + 1) * 128], wt[:, kc, c0:c0 + n],
                                                  start=(kc == 0), stop=(kc == KC - 1)), reads=[rw, r_big], writes=[rps])
        S.run_deferred()
        return ps, rps

    def stage_inproj(l):
        Wl = w_in[l]
        TBL = TB5
        with ExitStack() as st:
            wrot = Rot(nc, st, "ip_w", [128, KC, 512], BF16, 2)
            cosT = sb("ip_cos", [128, TL], F32, st)
            sinT = sb("ip_sin", [128, TL], F32, st)
            r_cs = Res()
            qkg = sb("ip_qkg", [128, 2], F32, st)
            lng = sb("ip_lng", [128, 512], F32, st)
            lnb = sb("ip_lnb", [128, 512], F32, st)
            S.dma("sp", cosT[:], cos_d, writes=[r_cs])
            S.dma("sp", sinT[:], sin_d, dwrites=[r_cs])
            S.dma("sp", qkg[:], qkg_d[:, l, :], dwrites=[r_cs])
            S.dma("sp", lng[:], lng_d[:, l, :], dwrites=[r_cs])
            S.dma("sp", lnb[:], lnb_d[:, l, :], dwrites=[r_cs])
            stb = Rot(nc, st, "ip_sb", [128, 512], BF16, 4)
            stf = Rot(nc, st, "ip_sf", [128, 512], F32, 4)
            st1 = Rot(nc, st, "ip_s1", [128, 8], F32, 4)
            xq_p = Rot(nc, st, "ip_xq", [128, 512], BF16, 8)
            hsq_p = Rot(nc, st, "ip_hsq", [128, 512], BF16, 6)
            hob_p = Rot(nc, st, "ip_hob", [128, 512], BF16, 6)

            def evac_copy(ps, rps, m, n, dst, rdst, eng="act", func=None, dt_f32=False):
                sbt, rs = (stf if dt_f32 else stb).next()
                if eng == "act":
                    S.op("act", lambda e: e.activation(out=sbt[0:m, 0:n], in_=ps[0:m, 0:n], func=func or AF.Copy),
                         reads=[rps], writes=[rs])
                else:
                    S.op("dve", lambda e: e.tensor_copy(out=sbt[0:m, 0:n], in_=ps[0:m, 0:n]), reads=[rps], writes=[rs])
                S.dma("sp", dst, sbt[0:m, 0:n], reads=[rs], dwrites=[rdst])

            def headproc(ps, rps, t0, n, normg, dst, rdst):
                latent = t0 < TL
                xq, rxq = xq_p.next()
                if normg is None:
                    S.op("act", lambda e: e.activation(out=xq[:, 0:n], in_=ps[:, 0:n], func=AF.Copy), reads=[rps], writes=[rxq])
                    stepB_ready = True
                else:
                    sq, rsq = hsq_p.next()
                    S.op("act", lambda e: e.activation(out=sq[:, 0:n], in_=ps[:, 0:n], func=AF.Square), reads=[rps], writes=[rsq])

                def stepB():
                    if not latent:
                        S.dma("sp", dst, xq[:, 0:n], reads=[rxq], dwrites=[rdst])
                        return None
                    pr, rpr = psE.next()
                    S.op("pe", lambda e: e.matmul(pr[:, 0:n], cst_b[:, ROPE2, :], xq[:, 0:n], start=True, stop=True),
                         reads=[rxq, r_cst], writes=[rpr])
                    t1, rt1 = stf.next()
                    t2, rt2 = stf.next()
                    ob, rob = hob_p.next()
                    S.op("dve", lambda e: e.tensor_tensor(out=t1[:, 0:n], in0=xq[:, 0:n], in1=cosT[:, t0:t0 + n], op=ALU.mult),
                         reads=[rxq, r_cs], writes=[rt1])
                    S.op("dve", lambda e: e.tensor_tensor(out=t2[:, 0:n], in0=pr[:, 0:n], in1=sinT[:, t0:t0 + n], op=ALU.mult),
                         reads=[rpr, r_cs], writes=[rt2])
                    S.op("pool", lambda e: e.tensor_tensor(out=ob[:, 0:n], in0=t1[:, 0:n], in1=t2[:, 0:n], op=ALU.add),
                         reads=[rt1, rt2], writes=[rob])
                    S.dma("sp", dst, ob[:, 0:n], reads=[rob], dwrites=[rdst])
                    return None

                def stepA():
                    pn, rpn = psE.next()
                    S.op("pe", lambda e: e.matmul(pn[:, 0:n], cst_b[:, ONES2, :], sq[:, 0:n], start=True, stop=True),
                         reads=[rsq, r_cst], writes=[rpn])
                    rs, rrs = stf.next()
                    rstd_from(rs[:, 0:n], pn[:, 0:n], 1.0 / 64, [rpn], [rrs], n_part=128)
                    S.op("dve", lambda e: e.scalar_tensor_tensor(out=xq[:, 0:n], in0=ps[:, 0:n], scalar=qkg[:, normg:normg + 1],
                                                                 in1=rs[:, 0:n], op0=ALU.mult, op1=ALU.mult),
                         reads=[rps, rrs, r_cs], writes=[rxq])
                    return stepB

                S.defer(stepB if normg is None else stepA)

            def fm_heads(wt, rw, c0, nheads, dst, rdst, normg=None, plain=False, func=None, dt_f32=False, m=64):
                if m == 64:
                    for hp_ in range(nheads // 2):
                        for (t0, n) in TBL:
                            ps, rps = fm_acc(wt, rw, c0 + hp_ * 128, 128, t0, n)
                            d2 = dst[2 * hp_:2 * hp_ + 2, :, t0:t0 + n].rearrange("h p t -> (h p) t")
                            if plain:
                                evac_copy(ps, rps, 128, n, d2, rdst, func=func, dt_f32=dt_f32)
                            else:
                                headproc(ps, rps, t0, n, normg, d2, rdst)
                    return
                for h in range(nheads):
                    for (t0, n) in TBL:
                        ps, rps = fm_acc(wt, rw, c0 + h * m, m, t0, n)
                        evac_copy(ps, rps, m, n, dst[h, :, t0:t0 + n] if dst.ndim == 3 else dst[:, t0:t0 + n], rdst,
                                  func=func, dt_f32=dt_f32)

            def tm_cols(wt, rw, c0, n, dst, rdst):
                for tt in range(NTILE):
                    ps, rps = tm_acc(wt, rw, c0, n, tt)
                    evac_copy(ps, rps, 128, n, dst[tt * 128:(tt + 1) * 128, :], rdst, eng="dve")

            groups = [(0, 512), (512, 512), (1024, 512), (1536, 32), (1568, 512), (2080, 256), (2336, 512), (2848, 512),
                      (3360, 512), (3872, 256)]
            nxt = load_w(wrot, Wl, groups[0][0], groups[0][1], KC)
            for gi, (c0g, ncg) in enumerate(groups):
                wt, rw = nxt
                if gi + 1 < len(groups):
                    nxt = load_w(wrot, Wl, groups[gi + 1][0], groups[gi + 1][1], KC)
                if gi == 0:
                    fm_heads(wt, rw, 0, 4, aqT, R["aqT"], plain=True)
                    fm_heads(wt, rw, 256, 4, akT, R["akT"], plain=True)
                    tm_cols(wt, rw, 256, 256, ak_tok, R["ak_tok"])
                elif gi == 1:
                    tm_cols(wt, rw, 0, 512, av_tok, R["av_tok"])
                elif gi == 2:
                    fm_heads(wt, rw, 0, 4, agT, R["agT"], plain=True, func=AF.Silu, m=128)
                elif gi == 3:
                    fm_heads(wt, rw, 0, 1, lrT, R["lrT"], plain=True, dt_f32=True, m=32)
                elif gi == 4:
                    fm_heads(wt, rw, 0, 8, bqT, R["bqT"])
                elif gi == 5:
                    fm_heads(wt, rw, 0, 2, bkT, R["bkT"])
                    tm_cols(wt, rw, 128, 128, bv_tok, R["bv_tok"])
                elif gi == 6:
                    fm_heads(wt, rw, 0, 4, cuT, R["cuT"], plain=True, func=AF.Gelu_apprx_tanh, m=128)
                elif gi == 7:
                    for tt in range(NTILE):
                        ps, rps = tm_acc(wt, rw, 0, 512, tt)
                        g, rg = stf.next()
                        S.op("act", lambda e: e.activation(out=g[:], in_=ps[:], func=AF.Gelu_apprx_tanh), reads=[rps], writes=[rg])
                        s1, rs1 = st1.next()
                        S.op("dve", lambda e: e.bn_stats(out=s1[:, 0:6], in_=g[:]), reads=[rg], writes=[rs1])
                        S.op("dve", lambda e: e.bn_aggr(out=s1[:, 6:8], in_=s1[:, 0:6]), reads=[rs1], writes=[rs1])
                        rstd_from(s1[:, 7:8], s1[:, 7:8], 1.0, [rs1], [rs1])
                        S.op("dve", lambda e: e.scalar_tensor_tensor(out=s1[:, 6:7], in0=s1[:, 6:7], scalar=-1.0, in1=s1[:, 7:8],
                                                                     op0=ALU.mult, op1=ALU.mult), reads=[rs1], writes=[rs1])
                        S.op("act", lambda e: e.activation(out=g[:], in_=g[:], func=AF.Identity, bias=s1[:, 6:7], scale=s1[:, 7:8]),
                             reads=[rs1], writes=[rg])
                        S.op("dve", lambda e: e.tensor_tensor(out=g[:], in0=g[:], in1=lng[:], op=ALU.mult), reads=[r_cs], writes=[rg])
                        ob, rob = stb.next()
                        S.op("pool", lambda e: e.tensor_tensor(out=ob[:], in0=g[:], in1=lnb[:], op=ALU.add), reads=[rg, r_cs], writes=[rob])
                        S.dma("sp", cv_tok[tt * 128:(tt + 1) * 128, :], ob[:], reads=[rob], dwrites=[R["cv_tok"]])
                elif gi == 8:
                    fm_heads(wt, rw, 0, 8, dqT, R["dqT"], normg=0)
                elif gi == 9:
                    fm_heads(wt, rw, 0, 2, dkT, R["dkT"], normg=1)
                    tm_cols(wt, rw, 128, 128, dv_tok, R["dv_tok"])
            S.barrier()

    def stage_gla(l):
        with ExitStack() as st:
            gub = sb("g_gub", [33, 512], F32, st)
            ga = sb("g_ga", [128, 4], F32, st)
            r_g = Res()
            S.dma("sp", gub[:], gub_d[l], writes=[r_g])
            S.dma("sp", ga[:], ga_d[:, l, :], dwrites=[r_g])
            of_s = sb("g_of", [128, 4, T], BF16, st)
            r_of = Res()
            stored = set()
            gT_p = Rot(nc, st, "g_g", [128, 4, 128], BF16, 3)
            tot_p = Rot(nc, st, "g_tot", [128, 4, 128], F32, 3)
            sq_p = Rot(nc, st, "g_sq", [128, 4, 128], BF16, 3)
            rs_p = Rot(nc, st, "g_rs", [128, 4, 128], F32, 3)
            yo_p = Rot(nc, st, "g_yo", [128, 4, 128], BF16, 3)
            ps7 = ps_b[:].bitcast(F32)
            BK = [dict(A=(ps_t[0], ps_r[0]), B=(ps_t[1], ps_r[1]), C=(ps_t[2], ps_r[2]), P=(ps_t[6], ps_r[6])),
                  dict(A=(ps_t[3], ps_r[3]), B=(ps_t[4], ps_r[4]), C=(ps_t[5], ps_r[5]), P=(ps7, r_psb))]
            DS = []
            for d in (0, 1):
                o = {}
                o["Sf"] = sb("g_Sf", [64, 4, 128], F32, st)
                o["Sb"] = sb("g_Sb", [64, 4, 128], BF16, st)
                o["rS"] = Res()
                o["lr"] = Rot(nc, st, "g_lr", [33, 128], F32, 2)
                for t_, _r in zip(o["lr"].t, o["lr"].r):
                    S.op("dve", lambda e, t_=t_: e.memset(t_[32:33, :], 1.0), writes=[_r])
                o["sp"] = Rot(nc, st, "g_sp", [128, 256], F32, 2)
                o["Ep"] = Rot(nc, st, "g_Ep", [64, 4, 128], F32, 2)
                o["Em"] = Rot(nc, st, "g_Em", [64, 4, 128], F32, 2)
                o["Emt"] = Rot(nc, st, "g_Emt", [128, 256], F32, 2)
                o["qT"] = Rot(nc, st, "g_q", [64, 4, 128], BF16, 2)
                o["kT"] = Rot(nc, st, "g_k", [64, 4, 128], BF16, 2)
                o["ktok"] = Rot(nc, st, "g_kt", [128, 256], BF16, 2)
                o["v"] = Rot(nc, st, "g_v", [128, 512], BF16, 2)
                o["qt"] = Rot(nc, st, "g_qt", [64, 4, 128], BF16, 2)
                o["kt"] = Rot(nc, st, "g_ktl", [64, 4, 128], BF16, 2)
                o["ktt"] = Rot(nc, st, "g_ktt", [128, 256], BF16, 2)
                o["am"] = Rot(nc, st, "g_am", [128, 4, 128], BF16, 2)
                DS.append(o)

            def prep(d, tt, P):
                o = DS[d]
                tok = slice(tt * 128, (tt + 1) * 128)
                tri = cst_f[:, TRIF + d, :]
                lr, rlr = o["lr"].next()
                S.dma("sp", lr[0:32, :], lrT[:, tok], writes=[rlr])
                qT, rq = o["qT"].next()
                S.dma("sp", qT[:], aqT[:, :, tok].rearrange("h p t -> p h t"), writes=[rq])
                kT, rk = o["kT"].next()
                S.dma("sp", kT[:], akT[:, :, tok].rearrange("h p t -> p h t"), writes=[rk])
                ktok, rkt = o["ktok"].next()
                S.dma("sp", ktok[:], ak_tok[tok, :], writes=[rkt])
                v, rv = o["v"].next()
                S.dma("sp", v[:], av_tok[tok, :], writes=[rv])
                yield
                pz, rpz = BK[d]["P"]
                S.op("pe", lambda e: e.matmul(pz[:, 0:256], lr[:, :], gub[:, d * 256:(d + 1) * 256], start=True, stop=True),
                     reads=[rlr, r_g], writes=[rpz])
                yield
                sp, rsp = o["sp"].next()
                S.op("act", lambda e: e.activation(out=sp[:], in_=pz[:, 0:256], func=AF.Exp, scale=-1.0), reads=[rpz], writes=[rsp])
                yield
                S.op("act", lambda e: e.activation(out=sp[:], in_=sp[:], func=AF.Ln, bias=cst_f[:, ONES, 0:1]), reads=[r_cst], writes=[rsp])
                yield
                pc, rpc = BK[d]["P"]
                pcv = pc[0:64, :].rearrange("p (h t) -> p h t", h=4)
                for h in range(4):
                    S.op("pe", lambda e, h=h: e.matmul(pcv[:, h, :], sp[:, h * 64:(h + 1) * 64], tri, start=True, stop=True),
                         reads=[rsp, r_cst], writes=[rpc])
                yield
                Ep, rEp = o["Ep"].next()
                Em, rEm = o["Em"].next()
                Emt, rEmt = o["Emt"].next()
                S.op("act", lambda e: e.activation(out=Ep[:], in_=pcv, func=AF.Exp), reads=[rpc], writes=[rEp])
                S.op("act", lambda e: e.activation(out=Em[:], in_=pcv, func=AF.Exp, scale=-1.0), reads=[rpc], writes=[rEm])
                yield
                pct, rpct = BK[d]["P"]
                S.op("pe", lambda e: e.matmul(pct[:, 0:256], tri, sp[:], start=True, stop=True), reads=[rsp, r_cst], writes=[rpct])
                yield
                S.op("act", lambda e: e.activation(out=Emt[:], in_=pct[:, 0:256], func=AF.Exp, scale=-1.0), reads=[rpct], writes=[rEmt])
                yield
                qt, rqt = o["qt"].next()
                kt, rktl = o["kt"].next()
                ktt, rktt = o["ktt"].next()
                S.op("dve", lambda e: e.scalar_tensor_tensor(out=qt[:], in0=qT[:], scalar=0.125, in1=Ep[:], op0=ALU.mult, op1=ALU.mult),
                     reads=[rq, rEp], writes=[rqt])
                S.op("pool", lambda e: e.tensor_tensor(out=kt[:], in0=kT[:], in1=Em[:], op=ALU.mult), reads=[rk, rEm], writes=[rktl])
                S.op("pool", lambda e: e.tensor_tensor(out=ktt[:], in0=ktok[:], in1=Emt[:], op=ALU.mult), reads=[rkt, rEmt], writes=[rktt])
                P.update(dict(v=v, rv=rv, Ep=Ep, rEp=rEp, qt=qt, rqt=rqt, kt=kt, rktl=rktl, ktt=ktt, rktt=rktt))
                yield

            def scan(d, tt, P):
                o = DS[d]
                Sf, Sb, r_S = o["Sf"], o["Sb"], o["rS"]
                tok = slice(tt * 128, (tt + 1) * 128)
                mask = cst_b[:, MASKF + d, :]
                v, rv, Ep, rEp, qt, rqt, kt, rktl, ktt, rktt = (P[k] for k in ("v", "rv", "Ep", "rEp", "qt", "rqt", "kt", "rktl", "ktt", "rktt"))
                pa, rpa = BK[d]["A"]
                pav = pa[:, :].rearrange("p (h t) -> p h t", h=4)
                for h in range(4):
                    S.op("pe", lambda e, h=h: e.matmul(pav[:, h, :], kt[:, h, :], qt[:, h, :], start=True, stop=True),
                         reads=[rktl, rqt], writes=[rpa])
                cA, cB = (0, 1) if d == 0 else (1, 0)
                psu = {}
                for c in (cA, cB):
                    pu, rpu = BK[d]["B" if c == cA else "C"]
                    puv = pu[0:64, :].rearrange("p (h t) -> p h t", h=4)
                    for h in range(4):
                        S.op("pe", lambda e, h=h, c=c, puv=puv: e.matmul(puv[:, h, :], ktt[c * 64:(c + 1) * 64, h * 64:(h + 1) * 64],
                                                                        v[c * 64:(c + 1) * 64, h * 128:(h + 1) * 128], start=True, stop=True),
                             reads=[rktt, rv], writes=[rpu])
                    psu[c] = (puv, rpu)
                yield
                am, ram = o["am"].next()
                S.op("dve", lambda e: e.tensor_tensor(out=am[:], in0=pav, in1=bc(mask, 1, [128, 4, 128]), op=ALU.mult),
                     reads=[rpa, r_cst], writes=[ram])
                yield
                po, rpo = BK[d]["A"]
                pov = po[:, :].rearrange("p (h t) -> p h t", h=4)
                for h in range(4):
                    S.op("pe", lambda e, h=h: e.matmul(pov[:, h, :], v[:, h * 128:(h + 1) * 128], am[:, h, :], start=(h == 0), stop=False,
                                                       skip_group_check=True), reads=[rv, ram], writes=[rpo])
                for ci, c in enumerate((cA, cB)):
                    for h in range(4):
                        S.op("pe", lambda e, h=h, c=c, ci=ci: e.matmul(pov[:, h, c * 64:(c + 1) * 64], Sb[:, h, :], qt[:, h, c * 64:(c + 1) * 64],
                                                                       start=False, stop=(ci == 1 and h == 3), skip_group_check=True),
                             reads=[r_S, rqt], writes=[rpo])
                    yield
                    puv, rpu = psu[c]
                    col = (c * 64 + 63) if d == 0 else (c * 64)
                    S.op("dve", lambda e, puv=puv: e.tensor_tensor(out=Sf[:], in0=puv, in1=Sf[:], op=ALU.add), reads=[rpu], writes=[r_S])
                    S.op("dve", lambda e, col=col: e.tensor_tensor(out=Sf[:], in0=Sf[:], in1=bc(Ep[:, :, col], 2, [64, 4, 128]), op=ALU.mult),
                         reads=[rEp], writes=[r_S])
                    yield
                    S.op("pool", lambda e: e.tensor_copy(out=Sb[:], in_=Sf[:]), writes=[r_S])
                    yield
                if tt not in stored:
                    stored.add(tt)
                    S.op("act", lambda e: e.activation(out=of_s[:, :, tok], in_=pov, func=AF.Copy), reads=[rpo], dwrites=[r_of])
                    yield
                    return
                gT, rgT = gT_p.next()
                S.dma("sp", gT[:], agT[:, :, tok].rearrange("h p t -> p h t"), writes=[rgT])
                tot, rtot = tot_p.next()
                S.op("dve", lambda e: e.tensor_tensor(out=tot[:], in0=pov, in1=of_s[:, :, tok], op=ALU.add), reads=[rpo, r_of], writes=[rtot])
                yield
                sq, rsq = sq_p.next()
                S.op("act", lambda e: e.activation(out=sq[:], in_=tot[:], func=AF.Square), reads=[rtot], writes=[rsq])
                yield
                pn, rpn = BK[d]["A"]
                S.op("pe", lambda e: e.matmul(pn[:, :], ones_b, sq[:].rearrange("p h t -> p (h t)"), start=True, stop=True),
                     reads=[rsq, r_cst], writes=[rpn])
                yield
                rs, rrs = rs_p.next()
                rsf = rs[:].rearrange("p h t -> p (h t)")
                S.op("act", lambda e: e.activation(out=rsf, in_=pn[:, :], func=AF.Ln, bias=epsT[:, :], scale=1.0 / 128), reads=[rpn, r_cst], writes=[rrs])
                yield
                S.op("act", lambda e: e.activation(out=rsf, in_=rsf, func=AF.Exp, scale=-0.5), writes=[rrs])
                yield
                S.op("dve", lambda e: e.tensor_tensor(out=tot[:], in0=tot[:], in1=rs[:], op=ALU.mult), reads=[rrs], writes=[rtot])
                yield
                S.op("pool", lambda e: e.tensor_tensor(out=tot[:], in0=tot[:], in1=gT[:], op=ALU.mult), reads=[rgT], writes=[rtot])
                yield
                yo, ryo = yo_p.next()
                S.op("dve", lambda e: e.tensor_tensor(out=yo[:], in0=tot[:], in1=bc(ga[:], 2, [128, 4, 128]), op=ALU.mult),
                     reads=[rtot, r_g], writes=[ryo])
                yield
                S.dma("sp", yT[0:4, :, tok].rearrange("h p t -> p h t"), yo[:], reads=[ryo], dwrites=[R["yT"]])

            def dir_gen(d):
                o = DS[d]
                S.op("dve", lambda e: e.memset(o["Sf"][:], 0.0), writes=[o["rS"]])
                S.op("dve", lambda e: e.memset(o["Sb"][:], 0.0), dwrites=[o["rS"]])
                order = [16, 17] + list(range(16)) if d == 0 else [17, 16] + list(range(15, -1, -1))
                P = {}
                for _ in prep(d, order[0], P):
                    yield
                for k, tt in enumerate(order):
                    Pn = {}
                    active = [scan(d, tt, P)]
                    if k + 1 < len(order):
                        active.append(prep(d, order[k + 1], Pn))
                    while active:
                        for g_ in list(active):
                            try:
                                next(g_)
                            except StopIteration:
                                active.remove(g_)
                        yield
                    P = Pn

            interleave([dir_gen(0), dir_gen(1)], GLA_W)
            S.barrier()

    def stage_attn(l, which, need_ctx, modl=None, side=None):
        qsrc, ksrc, vsrc = (bqT, bkT, bv_tok) if which == "B" else (dqT, dkT, dv_tok)
        rq_, rk_, rv_ = (R["bqT"], R["bkT"], R["bv_tok"]) if which == "B" else (R["dqT"], R["dkT"], R["dv_tok"])
        ych0 = 4 if which == "B" else 12
        with ExitStack() as st:
            QT = sb("a_Q", [64, 8, T], BF16, st)
            KT = sb("a_K", [64, 2, T], BF16, st)
            VX = sb("a_V", [128, NTILE, 2, 65], BF16, st)
            gv = sb("a_gv", [128, 512], F32, st)
            sinkE = sb("a_sink", [128, 8], F32, st)
            r_in = Res()
            S.op("dve", lambda e: e.memset(VX[:], 1.0), writes=[r_in])
            for (t0, n) in TB5:
                S.dma("sp", QT[:, :, t0:t0 + n], qsrc[:, :, t0:t0 + n].rearrange("h p t -> p h t"), reads=[rq_], dwrites=[r_in])
            S.dma("sp", KT[:], ksrc.rearrange("h p t -> p h t"), reads=[rk_], dwrites=[r_in])
            r_vx = Res()
            r_sk = Res()
            for g in range(2):
                S.dma("sp", VX[:, :, g, 0:64], vsrc[:, g * 64:(g + 1) * 64].rearrange("(t p) d -> p t d", p=128),
                      reads=[rv_, r_in], dwrites=[r_vx])
            S.dma("sp", gv[:], (gbb_d if which == "B" else gdb_d)[:, l, :], dwrites=[r_in])
            if which == "B":
                S.dma("sp", sinkE[:], sink_d[:, l, :], writes=[r_sk])
                S.op("act", lambda e: e.activation(out=sinkE[:], in_=sinkE[:], func=AF.Exp), writes=[r_sk])
            pT_p = Rot(nc, st, "a_pT", [128, 4, 128], BF16, 5)
            psS = PsRot(ps_t[0:3], ps_r[0:3])
            gen = None
            sgen = side(st, (ps_b[:].bitcast(F32)[:, 256:512], r_psb)) if side is not None else None
            if modl is not None:
                wm = Rot(nc, st, "a_wm", [128, KC, 512], BF16, 2)
                gen = mod_gen(modl, wm, ps_t[3], ps_r[3])
            ob_p = Rot(nc, st, "a_ob", [128, 8, 64], F32, 2)
            s1_p = Rot(nc, st, "a_s1", [128, 20], F32, 2)
            junk_p = Rot(nc, st, "a_jk", [128, 512], BF16, 1)
            y_p = Rot(nc, st, "a_y", [128, 512], BF16, 2)
            ys_p = Rot(nc, st, "a_ys", [128, 4, 128], BF16, 2)
            qtiles = list(range(16)) + ([16, 17] if need_ctx else [])
            use_sink = (which == "B")
            items = []
            for qt in qtiles:
                if qt >= 16:
                    keys = [(16, None), (17, None)]
                elif which == "B":
                    keys = []
                    if qt > 0:
                        keys.append((qt - 1, WLO))
                    keys.append((qt, None))
                    if qt < 15:
                        keys.append((qt + 1, WHI))
                    keys += [(16, None), (17, None)]
                else:
                    keys = [(k, None) for k in range(NTILE)]
                for g in range(2):
                    for ki, (kt, mk) in enumerate(keys):
                        items.append((qt, g, ki, kt, mk, len(keys)))

            def emit_qk(it):
                qt, g, ki, kt, mk, nk = it
                ps_, rps = psS.next()
                psv = ps_[:, :].rearrange("p (h t) -> p h t", h=4)
                S.op("pe", lambda e: e.matmul(psv, KT[:, g, kt * 128:(kt + 1) * 128], QT[:, 4 * g:4 * g + 4, qt * 128:(qt + 1) * 128],
                                              start=True, stop=True), reads=[r_in], writes=[rps])
                return psv, rps

            def finalize(qt, pos):
                ob, rob = ob_p.next()
                s1, rs1 = s1_p.next()
                for g in range(2):
                    pov, rpo = pos[g]
                    if use_sink:
                        S.op("dve", lambda e, g=g, pov=pov: e.tensor_tensor(out=s1[:, 4 * g:4 * g + 4], in0=pov[:, :, 64],
                                                                            in1=sinkE[:, 4 * g:4 * g + 4], op=ALU.add),
                             reads=[rpo, r_sk], writes=[rs1])
                    else:
                        S.op("dve", lambda e, g=g, pov=pov: e.tensor_copy(out=s1[:, 4 * g:4 * g + 4], in_=pov[:, :, 64]), reads=[rpo], writes=[rs1])
                    S.op("dve", lambda e, g=g: e.reciprocal(out=s1[:, 8 + 4 * g:12 + 4 * g], in_=s1[:, 4 * g:4 * g + 4]), reads=[rs1], writes=[rs1])
                    S.op("dve", lambda e, g=g, pov=pov: e.tensor_tensor(out=ob[:, 4 * g:4 * g + 4, :], in0=pov[:, :, 0:64],
                                                                        in1=bc(s1[:, 8 + 4 * g:12 + 4 * g], 2, [128, 4, 64]), op=ALU.mult),
                         reads=[rpo, rs1], writes=[rob])
                obf = ob[:].rearrange("p h d -> p (h d)")
                jk, rjk = junk_p.next()
                S.op("act", lambda e: e.activation(out=jk[:], in_=obf, func=AF.Square, accum_out=s1[:, 16:17]), reads=[rob], writes=[rjk, rs1])
                rstd_from(s1[:, 17:18], s1[:, 16:17], 1.0 / 512, [rs1], [rs1])
                y, ry = y_p.next()
                S.op("dve", lambda e: e.scalar_tensor_tensor(out=y[:], in0=obf, scalar=s1[:, 17:18], in1=gv[:], op0=ALU.mult, op1=ALU.mult),
                     reads=[rob, rs1, r_in], writes=[ry])

                def part2():
                    ptv = ps_b[:, 0:512].rearrange("p (c t) -> p c t", c=4)
                    for c in range(4):
                        S.op("pe", lambda e, c=c: e.transpose(ptv[:, c, :], y[:, c * 128:(c + 1) * 128], ident_b), reads=[ry, r_cst], writes=[r_psb])
                    ys, rys = ys_p.next()
                    S.op("act", lambda e: e.activation(out=ys[:], in_=ptv, func=AF.Copy), writes=[r_psb, rys])
                    S.dma("sp", yT[ych0:ych0 + 4, :, qt * 128:(qt + 1) * 128].rearrange("c p t -> p c t"), ys[:], reads=[rys], dwrites=[R["yT"]])
                return part2

            LOOK = 2
            pend = []
            nxt_i = 0
            fin_q = []
            pos = []
            for idx, it in enumerate(items):
                while len(pend) < LOOK + 1 and nxt_i < len(items):
                    pend.append(emit_qk(items[nxt_i]))
                    nxt_i += 1
                psv, rps = pend.pop(0)
                qt, g, ki, kt, mk, nk = it
                if ki == 0:
                    if g == 0:
                        pos = []
                    po, rpo = psE.next()
                    pov = po[:, 0:260].rearrange("p (h d) -> p h d", h=4)
                    pos.append((pov, rpo))
                pov, rpo = pos[g]
                pT, rpT = pT_p.next()
                S.op("act", lambda e, pT=pT, psv=psv: e.activation(out=pT[:], in_=psv, func=AF.Exp, scale=0.125), reads=[rps], writes=[rpT])
                if mk is not None:
                    S.op("dve", lambda e, pT=pT, mk=mk: e.tensor_tensor(out=pT[:], in0=pT[:], in1=bc(cst_b[:, mk, :], 1, [128, 4, 128]),
                                                                        op=ALU.mult), reads=[r_cst], writes=[rpT])
                for h4 in range(4):
                    S.op("pe", lambda e, h4=h4, pT=pT, kt=kt, g=g, ki=ki, pov=pov, nk=nk: e.matmul(
                        pov[:, h4, :], pT[:, h4, :], VX[:, kt, g, :], start=(ki == 0 and h4 == 0),
                        stop=(ki == nk - 1 and h4 == 3), skip_group_check=True), reads=[rpT, r_vx, r_in], writes=[rpo])
                for f_ in fin_q:
                    f_[0] -= 1
                while fin_q and fin_q[0][0] <= 0:
                    fin_q.pop(0)[1]()
                if g == 1 and ki == nk - 1:
                    fin_q.append([3, finalize(qt, pos)])
                if gen is not None and idx % 12 == 11:
                    next(gen, None)
                if sgen is not None and idx % 2 == 0:
                    next(sgen, None)
            for f_ in fin_q:
                f_[1]()
            if gen is not None:
                for _ in gen:
                    pass
            if sgen is not None:
                for _ in sgen:
                    pass
            S.barrier()

    def cmlp_gen(l, need_ctx, st, bank):
        if True:
            wsf = sb("c_wsf", [128, 4, 128], BF16, st)
            bsb = sb("c_bsb", [128, 4, 128], F32, st)
            gc = sb("c_gc", [128, 4], F32, st)
            r_c = Res()
            S.dma("pool", wsf[:], wsT_d[l], writes=[r_c])
            S.dma("sp", bsb[:], bsb_d[:, l, :, :], dwrites=[r_c])
            S.dma("sp", gc[:], gc_d[:, l, :], dwrites=[r_c])
            W_ = 1
            cv_p = Rot(nc, st, "c_cv", [128, 512], BF16, W_ + 1)
            cu_p = Rot(nc, st, "c_cu", [128, 4, 128], BF16, W_ + 1)
            t_p = Rot(nc, st, "c_t", [128, 4, 128], F32, 1)
            sq_p = Rot(nc, st, "c_sq", [128, 4, 128], BF16, 1)
            rs_p = Rot(nc, st, "c_rs", [128, 128], F32, 1)
            yo_p = Rot(nc, st, "c_yo", [128, 4, 128], BF16, W_ + 1)

            def tile_gen(tt):
                tok = slice(tt * 128, (tt + 1) * 128)
                cv, rcv = cv_p.next()
                S.dma("sp", cv[:], cv_tok[tok, :], writes=[rcv])
                cu, rcu = cu_p.next()
                S.dma("sp", cu[:], cuT[:, :, tok].rearrange("g p t -> p g t"), writes=[rcu])
                yield
                t, rt = t_p.next()
                reg, rreg = bank
                psv = reg[:, 0:256].rearrange("p (g t) -> p g t", g=2)
                for half in range(2):
                    for gg in range(2):
                        g = half * 2 + gg
                        S.op("pe", lambda e, g=g, gg=gg: e.matmul(psv[:, gg, :], cv[:, g * 128:(g + 1) * 128], wsf[:, g, :], start=True, stop=True),
                             reads=[rcv, r_c], writes=[rreg])
                    yield
                    S.op("dve", lambda e, half=half: e.tensor_tensor(out=t[:, 2 * half:2 * half + 2, :], in0=psv, in1=bsb[:, 2 * half:2 * half + 2, :],
                                                                     op=ALU.add), reads=[r_c], writes=[rreg, rt] if half == 0 else [rreg],
                         dwrites=() if half == 0 else [rt])
                    yield
                S.op("pool", lambda e: e.tensor_tensor(out=t[:], in0=t[:], in1=cu[:], op=ALU.mult), reads=[rcu], writes=[rt])
                yield
                sq, rsq = sq_p.next()
                S.op("act", lambda e: e.activation(out=sq[:], in_=t[:], func=AF.Square), reads=[rt], writes=[rsq])
                yield
                pn, rpn = bank
                for g in range(4):
                    S.op("pe", lambda e, g=g: e.matmul(pn[:, 0:128], ones_b, sq[:, g, :], start=(g == 0), stop=(g == 3)),
                         reads=[rsq, r_cst], writes=[rpn])
                yield
                rs, rrs = rs_p.next()
                S.op("act", lambda e: e.activation(out=rs[:], in_=pn[:, 0:128], func=AF.Ln, bias=epsT[:, :], scale=1.0 / 512),
                     reads=[r_cst], writes=[rpn, rrs])
                yield
                S.op("act", lambda e: e.activation(out=rs[:], in_=rs[:], func=AF.Exp, scale=-0.5), writes=[rrs])
                yield
                S.op("dve", lambda e: e.tensor_tensor(out=t[:], in0=t[:], in1=bc(rs[:], 1, [128, 4, 128]), op=ALU.mult), reads=[rrs], writes=[rt])
                yo, ryo = yo_p.next()
                S.op("dve", lambda e: e.tensor_tensor(out=yo[:], in0=t[:], in1=bc(gc[:], 2, [128, 4, 128]), op=ALU.mult),
                     reads=[rt, r_c], writes=[ryo])
                yield
                S.dma("sp", yT[8:12, :, tok].rearrange("g p t -> p g t"), yo[:], reads=[ryo], dwrites=[R["yT"]])

            for tt in range(NTILE if need_ctx else 16):
                for _ in tile_gen(tt):
                    yield

    def evac_o(ps, rps, f, t0, n, ost_p, sqt_p):
        ost, ro = ost_p.next()
        S.op("act", lambda e: e.activation(out=ost[:, 0:n], in_=ps[:, 0:n], func=AF.Copy), reads=[rps], writes=[ro])
        sqt, rsq = sqt_p.next()
        S.op("dve", lambda e: e.tensor_tensor(out=sqt[:, 0:n], in0=ps[:, 0:n], in1=ost[:, 0:n], op=ALU.mult), reads=[rps, ro], writes=[rsq])
        if f == 0:
            S.op("pool", lambda e: e.tensor_copy(out=acc[:, t0:t0 + n], in_=sqt[:, 0:n]), reads=[rsq], dwrites=[r_acc])
        else:
            S.op("pool", lambda e: e.tensor_tensor(out=acc[:, t0:t0 + n], in0=acc[:, t0:t0 + n], in1=sqt[:, 0:n], op=ALU.add),
                 reads=[rsq], writes=[r_acc])
        S.dma("sp", oT[f, :, t0:t0 + n], ost[:, 0:n], reads=[ro], dwrites=[R["oT"]])

    def stage_outproj(l, tbl=TB5):
        with ExitStack() as st:
            for kc in range(KC):
                S.dma("sp", bigk[:, kc, :], yT[kc, :, :], reads=[R["yT"]], dwrites=[r_big])
            wrot = Rot(nc, st, "op_w", [128, KC, 512], BF16, 2)
            ost_p = Rot(nc, st, "op_o", [128, 512], F32, 3)
            sqt_p = Rot(nc, st, "op_q", [128, 512], F32, 3)
            nxt = load_w(wrot, w_out[l], 0, 512, KC)
            for gi in range(4):
                wt, rw = nxt
                if gi + 1 < 4:
                    nxt = load_w(wrot, w_out[l], (gi + 1) * 512, 512, KC)
                for f4 in range(4):
                    for (t0, n) in tbl:
                        ps, rps = fm_acc(wt, rw, f4 * 128, 128, t0, n)
                        evac_o(ps, rps, gi * 4 + f4, t0, n, ost_p, sqt_p)
            S.barrier()

    def stage_ffn_in(l, tbl=TB5):
        with ExitStack() as st:
            wg_p = Rot(nc, st, "f1_g", [128, KC, 512], BF16, 2)
            wu_p = Rot(nc, st, "f1_u", [128, KC, 512], BF16, 2)
            sg_p = Rot(nc, st, "f1_s", [128, 512], F32, 3)
            a_p = Rot(nc, st, "f1_a", [128, 512], BF16, 3)
            Wl = w_f1[l]
            nxt = (load_w(wg_p, Wl, 0, 512, KC), load_w(wu_p, Wl, DFF, 512, KC))
            for gi in range(11):
                (wg, rwg), (wu, rwu) = nxt
                if gi + 1 < 11:
                    nxt = (load_w(wg_p, Wl, (gi + 1) * 512, 512, KC), load_w(wu_p, Wl, DFF + (gi + 1) * 512, 512, KC))
                for f4 in range(4):
                    f = gi * 4 + f4
                    for (t0, n) in tbl:
                        pg, rpg = fm_acc(wg, rwg, f4 * 128, 128, t0, n)
                        pu, rpu = fm_acc(wu, rwu, f4 * 128, 128, t0, n)
                        sg, rsg = sg_p.next()
                        S.op("act", lambda e: e.activation(out=sg[:, 0:n], in_=pg[:, 0:n], func=AF.Silu), reads=[rpg], writes=[rsg])
                        a, ra = a_p.next()
                        S.op("dve", lambda e: e.tensor_tensor(out=a[:, 0:n], in0=pu[:, 0:n], in1=sg[:, 0:n], op=ALU.mult), reads=[rpu, rsg], writes=[ra])
                        S.dma("sp", actT[f, :, t0:t0 + n], a[:, 0:n], reads=[ra], dwrites=[R["actT"]])
            S.barrier()

    def stage_ffn_out(l, tlim=T):
        PT = 1152
        bigf = big[:, 0:FC * 768].rearrange("p (k t) -> p k t", k=FC)
        with ExitStack() as st:
            ex = sb("f2_ex", [128, FC, 384], BF16, st)
            wrot = Rot(nc, st, "f2_w", [128, FC, 256], BF16, 2)
            ost_p = Rot(nc, st, "f2_o", [128, 512], F32, 3)
            sqt_p = Rot(nc, st, "f2_q", [128, 512], F32, 3)
            for p in range(2):
                base = p * PT
                first = True
                for (view, tl0, g0, n) in ((bigf, 0, 0, 512), (bigf, 512, 512, 256), (ex, 0, 768, 384)):
                    for kc in range(FC):
                        S.dma("sp", view[:, kc, tl0:tl0 + n], actT[kc, :, base + g0:base + g0 + n], dwrites=() if first else [r_big],
                              writes=[r_big] if first else ())
                        first = False
                nxt = load_w(wrot, w_f2[l], 0, 256, FC)
                for gi in range(8):
                    wt, rw = nxt
                    if gi + 1 < 8:
                        nxt = load_w(wrot, w_f2[l], (gi + 1) * 256, 256, FC)
                    for f2 in range(2):
                        for (view, tl0, g0, n) in ((bigf, 0, 0, 512), (bigf, 512, 512, 256), (ex, 0, 768, 384)):
                            t_abs = base + g0
                            if t_abs >= tlim:
                                continue
                            n = min(n, tlim - t_abs)
                            ps, rps = fm_acc(wt, rw, f2 * 128, 128, tl0, n, nk=FC, inview=view)
                            evac_o(ps, rps, gi * 2 + f2, t_abs, n, ost_p, sqt_p)
            S.barrier()

    stage_mod()
    r_x0 = Res("xT0")
    layer_scalars(0, 1)
    stage_resnorm(xT0, r_x0, None, ("A1", "B1"), xT, R["xT"])
    for l in range(depth):
        last = (l == depth - 1)
        need_ctx = not last
        tlim = T if need_ctx else TL
        tbl = TB5 if need_ctx else TB5[:4]
        layer_scalars(l, 2)
        stage_inproj(l)
        if stop_after == "inproj":
            break
        stage_gla(l)
        stage_attn(l, "B", need_ctx)
        stage_attn(l, "D", need_ctx, modl=(l + 1 if not last else None), side=lambda st_, bank_: cmlp_gen(l, need_ctx, st_, bank_))
        if stop_after == "mix":
            break
        stage_outproj(l, tbl)
        stage_resnorm(xT, R["xT"], "G1", ("A2", "B2"), xT, R["xT"], tlim=tlim)
        if stop_after == "res1":
            break
        stage_ffn_in(l, tbl)
        stage_ffn_out(l, tlim)
        if not last:
            layer_scalars(l + 1, 1)
            stage_resnorm(xT, R["xT"], "G2", ("A1", "B1"), xT, R["xT"])
        else:
            stage_resnorm(xT, R["xT"], "G2", None, outT, R["outT"], final=True)
    S.barrier()
    es.close()
    return nc


def _consts():
    c = np.zeros((128, 11, 128), np.float32)
    j = np.arange(128)[:, None]
    i = np.arange(128)[None, :]
    same = (j // 64) == (i // 64)
    c[:, 0, :] = np.eye(128, dtype=np.float32)
    c[:, 1, :] = np.where(same & (j <= i), -1.0 / 16.0, 0.0)
    c[:, 2, :] = np.where(same & (j >= i), -1.0 / 16.0, 0.0)
    c[:, 3, :] = np.where(same & (j <= i), 1.0, 0.0)
    c[:, 4, :] = np.where(same & (j >= i), 1.0, 0.0)
    c[:, 5, :] = np.where(j >= i, 1.0, 0.0)
    c[:, 6, :] = np.where(j <= i, 1.0, 0.0)
    P = np.zeros((64, 64), np.float32)
    for a in range(16):
        P[a, a + 16] = -1.0
        P[a + 16, a] = 1.0
        P[a + 32, a + 48] = -1.0
        P[a + 48, a + 32] = 1.0
    c[0:64, 7, 0:64] = P.T
    c[:, 8, :] = 1.0
    c[0:64, 9, 0:64] = P.T
    c[64:128, 9, 64:128] = P.T
    c[0:64, 10, 0:64] = 1.0
    c[64:128, 10, 64:128] = 1.0
    rows = TL // 64
    row = np.repeat(np.arange(rows), 64).astype(np.float32)
    col = np.tile(np.arange(64), rows).astype(np.float32)
    inv = (np.float32(10000.0) ** (-np.arange(16, dtype=np.float32) / np.float32(16))).astype(np.float32)
    ar, ac = row[:, None] * inv, col[:, None] * inv
    ang = np.concatenate([ar, ar, ac, ac], axis=-1).astype(np.float32)
    ct = np.cos(ang).T.astype(np.float32)
    stb = np.sin(ang).T.astype(np.float32)
    return c, np.ascontiguousarray(np.concatenate([ct, ct], 0)), np.ascontiguousarray(np.concatenate([stb, stb], 0))


def _shared_inputs(c_ctx, w_mod, b_mod, norm_g, w_in, gla_gate_up, gla_gate_b, win_sink, cm_ln_g, cm_ln_b, cm_ws, cm_bs,
                   qk_g, mix_g, w_out, w_ffn_in, w_ffn_out):
    f = lambda a: np.ascontiguousarray(np.asarray(a, dtype=np.float32))
    L = w_mod.shape[0]
    cst, cosT, sinT = _consts()
    gub = np.zeros((L, 33, 512), np.float32)
    gub[:, 0:16, 0:256] = gla_gate_up[:, 0]
    gub[:, 16:32, 256:512] = gla_gate_up[:, 1]
    gub[:, 32, 0:256] = gla_gate_b[:, 0]
    gub[:, 32, 256:512] = gla_gate_b[:, 1]
    rep = lambda a: np.ascontiguousarray(np.broadcast_to(a[None], (128,) + a.shape))
    d = {
        "w_mod": f(w_mod), "w_in": f(w_in), "w_out": f(w_out), "w_f1": f(w_ffn_in), "w_f2": f(w_ffn_out),
        "bm": f(b_mod.reshape(L, 96, 128).transpose(2, 0, 1)),
        "ng": f(norm_g.reshape(L, 4, KC, 128).transpose(3, 0, 1, 2)),
        "gub": gub,
        "sinkb": f(rep(win_sink)),
        "lng": f(rep(cm_ln_g)), "lnb": f(rep(cm_ln_b)),
        "wsT": f(cm_ws.transpose(0, 3, 1, 2)),
        "bsb": f(rep(cm_bs)),
        "qkg": f(np.concatenate([qk_g.transpose(2, 0, 1)] * 2, axis=0)),
        "ga": f(mix_g[:, 0:512].reshape(L, 4, 128).transpose(2, 0, 1)),
        "gbb": f(rep(mix_g[:, 512:1024])),
        "gc": f(mix_g[:, 1024:1536].reshape(L, 4, 128).transpose(2, 0, 1)),
        "gdb": f(rep(mix_g[:, 1536:2048])),
        "cst": cst, "cosT": cosT, "sinT": sinT,
    }
    return d


def _core_inputs(xb, cb, ctxb, c_ctx):
    xt = np.concatenate([np.asarray(xb, np.float32).T, np.asarray(ctxb, np.float32).T], axis=1)
    cv = np.stack([np.asarray(cb, np.float32), np.asarray(c_ctx, np.float32)], axis=-1)
    return {"xT0": np.ascontiguousarray(xt.reshape(KC, 128, T)),
            "cvec": np.ascontiguousarray(cv.reshape(KC, 128, 2).transpose(1, 0, 2))}


def kernel(x, c, ctx, c_ctx, w_mod, b_mod, norm_g, w_in, gla_gate_up, gla_gate_b, win_sink, cm_ln_g, cm_ln_b, cm_ws, cm_bs,
           qk_g, mix_g, w_out, w_ffn_in, w_ffn_out):
    A = lambda a: np.asarray(a)
    x, c, ctx, c_ctx = A(x), A(c), A(ctx), A(c_ctx)
    shared = _shared_inputs(A(c_ctx), A(w_mod), A(b_mod), A(norm_g), A(w_in), A(gla_gate_up), A(gla_gate_b), A(win_sink),
                            A(cm_ln_g), A(cm_ln_b), A(cm_ws), A(cm_bs), A(qk_g), A(mix_g), A(w_out), A(w_ffn_in), A(w_ffn_out))
    nb = x.shape[0]
    nc = build(depth=4)
    in_maps = []
    for b in range(nb):
        m = dict(shared)
        m.update(_core_inputs(x[b], c[b], ctx[b], c_ctx))
        in_maps.append(m)
    res = run_bass_kernel_spmd(nc, in_maps, core_ids=list(range(nb)))
    out = np.empty((nb, TL, D), np.float32)
    for b in range(nb):
        o = np.asarray(res.results[b]["outT"], dtype=np.float32).reshape(D, TL)
        out[b] = o.T
    return out
```

```python
from contextlib import ExitStack
import numpy as np
import concourse.bass as bass
import concourse.mybir as mybir
from concourse.bass_utils import run_bass_kernel_spmd

F32 = mybir.dt.float32
BF16 = mybir.dt.bfloat16
AF = mybir.ActivationFunctionType
ALU = mybir.AluOpType

D = 2048
TL = 2048
TCX = 256
T = TL + TCX
KC = 16
DFF = 5632
FC = 44
INW = 4128
EPS = 1e-6
NTILE = T // 128
TB5 = [(0, 512), (512, 512), (1024, 512), (1536, 512), (2048, 256)]
NDS = 40
SAME_ENGINE_WAIT = True
GLA_W = 2


def bc(ap, axis, shape):
    return ap.unsqueeze(axis).to_broadcast(list(shape))


class Res:
    __slots__ = ("w", "r", "name")

    def __init__(self, name=""):
        self.w = {}
        self.r = {}
        self.name = name


class Sched:
    ENG = ("pe", "act", "dve", "pool", "sp")

    def __init__(self, nc, es):
        self.nc = nc
        self.eng = {"pe": nc.tensor, "act": nc.scalar, "dve": nc.vector, "pool": nc.gpsimd, "sp": nc.sync}
        self.sem = {e: es.enter_context(nc.semaphore("s_" + e)) for e in self.ENG}
        self.cnt = {e: 0 for e in self.ENG}
        self.known = {e: {} for e in self.ENG}
        self.dsem = [es.enter_context(nc.semaphore("d%d" % i)) for i in range(NDS)]
        self.dcnt = [0] * NDS
        self.drr = 0
        self.deferred = []
        self.nwaits = 0

    def _semh(self, key):
        return self.sem[key[1]] if key[0] == "e" else self.dsem[key[1]]

    def _wait(self, e, deps):
        kn = self.known[e]
        for key, val in deps.items():
            if kn.get(key, 0) >= val:
                continue
            if key == ("e", e) and (e == "pe" or not SAME_ENGINE_WAIT):
                continue
            self.eng[e].wait_ge(self._semh(key), val)
            kn[key] = val
            self.nwaits += 1

    @staticmethod
    def _deps(reads, writes, dwrites):
        d = {}
        for r in reads:
            for k, v in r.w.items():
                if d.get(k, 0) < v:
                    d[k] = v
        for w in writes:
            for k, v in w.w.items():
                if d.get(k, 0) < v:
                    d[k] = v
            for k, v in w.r.items():
                if d.get(k, 0) < v:
                    d[k] = v
        for w in dwrites:
            for k, v in w.r.items():
                if d.get(k, 0) < v:
                    d[k] = v
        return d

    @staticmethod
    def _mark(ev, reads, writes, dwrites):
        k, v = ev
        for r in reads:
            if r.r.get(k, 0) < v:
                r.r[k] = v
        for w in writes:
            w.w = {k: v}
            w.r = {}
        for w in dwrites:
            if w.w.get(k, 0) < v:
                w.w[k] = v

    def op(self, e, fn, reads=(), writes=(), dwrites=()):
        self._wait(e, self._deps(reads, writes, dwrites))
        ins = fn(self.eng[e])
        self.cnt[e] += 1
        ins.then_inc(self.sem[e], 1)
        self._mark((("e", e), self.cnt[e]), reads, writes, dwrites)

    def dma(self, q, out, in_, reads=(), writes=(), dwrites=()):
        i = self.drr
        self.drr = (i + 1) % NDS
        deps = self._deps(reads, writes, dwrites)
        if self.dcnt[i] > 0:
            deps[("d", i)] = max(deps.get(("d", i), 0), self.dcnt[i])
        self._wait(q, deps)
        self.eng[q].dma_start(out=out, in_=in_).then_inc(self.dsem[i], 16)
        self.dcnt[i] += 16
        self._mark((("d", i), self.dcnt[i]), reads, writes, dwrites)

    def barrier(self):
        self.run_deferred(all_=True)
        for e in self.ENG:
            deps = {("e", o): self.cnt[o] for o in self.ENG if self.cnt[o] > 0}
            for i in range(NDS):
                if self.dcnt[i] > 0:
                    deps[("d", i)] = self.dcnt[i]
            self._wait(e, deps)

    def defer(self, fn):
        self.deferred.append(fn)

    def run_deferred(self, all_=False):
        while True:
            cur, self.deferred = self.deferred, []
            for fn in cur:
                nxt = fn()
                if nxt is not None:
                    self.deferred.append(nxt)
            if not all_ or not self.deferred:
                break


_UID = [0]


def _uid():
    _UID[0] += 1
    return _UID[0]


class Rot:
    def __init__(self, nc, es, name, shape, dt, n):
        u = _uid()
        self.t = [es.enter_context(nc.sbuf_tensor("%s_%d_%d" % (name, u, i), list(shape), dt)) for i in range(n)]
        self.r = [Res(name) for _ in range(n)]
        self.i = 0

    def next(self):
        i = self.i
        self.i = (i + 1) % len(self.t)
        return self.t[i], self.r[i]


class PsRot:
    def __init__(self, tiles, ress):
        self.t = tiles
        self.r = ress
        self.i = 0

    def next(self):
        i = self.i
        self.i = (i + 1) % len(self.t)
        return self.t[i], self.r[i]


def build(depth=4, debug=False, stop_after=None):
    nc = bass.Bass("TRN2", target_bir_lowering=False)
    es = ExitStack()

    def inp(name, shape, dt=F32):
        return nc.dram_tensor(name, list(shape), dt, kind="ExternalInput").ap()

    dbg_kind = "ExternalOutput" if debug else "Internal"

    def scr(name, shape, dt):
        return nc.dram_tensor(name, list(shape), dt, kind=dbg_kind).ap()

    xT0 = inp("xT0", [KC, 128, T])
    cvec = inp("cvec", [128, KC, 2])
    w_mod = inp("w_mod", [4, D, 6 * D])
    bm_d = inp("bm", [128, 4, 96])
    ng_d = inp("ng", [128, 4, 4, KC])
    w_in = inp("w_in", [4, D, INW])
    w_out = inp("w_out", [4, D, D])
    w_f1 = inp("w_f1", [4, D, 2 * DFF])
    w_f2 = inp("w_f2", [4, DFF, D])
    gub_d = inp("gub", [4, 33, 512])
    sink_d = inp("sinkb", [128, 4, 8])
    lng_d = inp("lng", [128, 4, 512])
    lnb_d = inp("lnb", [128, 4, 512])
    wsT_d = inp("wsT", [4, 128, 4, 128])
    bsb_d = inp("bsb", [128, 4, 4, 128])
    qkg_d = inp("qkg", [128, 4, 2])
    ga_d = inp("ga", [128, 4, 4])
    gbb_d = inp("gbb", [128, 4, 512])
    gc_d = inp("gc", [128, 4, 4])
    gdb_d = inp("gdb", [128, 4, 512])
    cst_d = inp("cst", [128, 11, 128])
    cos_d = inp("cosT", [128, TL])
    sin_d = inp("sinT", [128, TL])
    outT = nc.dram_tensor("outT", [KC, 128, TL], F32, kind="ExternalOutput").ap()

    xT = scr("xT", [KC, 128, T], F32)
    oT = scr("oT", [KC, 128, T], F32)
    yT = scr("yT", [KC, 128, T], BF16)
    actT = scr("actT", [FC, 128, T], BF16)
    aqT = scr("aqT", [4, 64, T], BF16)
    akT = scr("akT", [4, 64, T], BF16)
    ak_tok = scr("ak_tok", [T, 256], BF16)
    av_tok = scr("av_tok", [T, 512], BF16)
    agT = scr("agT", [4, 128, T], BF16)
    lrT = scr("lrT", [32, T], F32)
    bqT = scr("bqT", [8, 64, T], BF16)
    bkT = scr("bkT", [2, 64, T], BF16)
    bv_tok = scr("bv_tok", [T, 128], BF16)
    cuT = scr("cuT", [4, 128, T], BF16)
    cv_tok = scr("cv_tok", [T, 512], BF16)
    dqT = scr("dqT", [8, 64, T], BF16)
    dkT = scr("dkT", [2, 64, T], BF16)
    dv_tok = scr("dv_tok", [T, 128], BF16)
    R = {n: Res(n) for n in ("xT", "oT", "yT", "actT", "aqT", "akT", "ak_tok", "av_tok", "agT", "lrT", "bqT", "bkT",
                             "bv_tok", "cuT", "cv_tok", "dqT", "dkT", "dv_tok", "outT")}

    S = Sched(nc, es)

    def sb(name, shape, dt, st=es):
        return st.enter_context(nc.sbuf_tensor("%s_%d" % (name, _uid()), list(shape), dt))

    big = sb("big", [128, KC * T], BF16)
    r_big = Res("big")
    bigk = big[:].rearrange("p (k t) -> p k t", k=KC)
    cst_f = sb("cst_f", [128, 11, 128], F32)
    cst_b = sb("cst_b", [128, 11, 128], BF16)
    r_cst = Res("cst")
    modv = sb("modv", [128, 4, 96, 2], F32)
    r_modv = Res("modv")
    ng_s = sb("ng_s", [128, 4, 4, KC], F32)
    bm_s = sb("bm_s", [128, 4, 96], F32)
    r_small = Res("small")
    lay = {n: sb("lay_" + n, [128, KC, 2], F32) for n in ("A1", "B1", "G1", "A2", "B2", "G2")}
    r_lay = {1: Res("lay1"), 2: Res("lay2")}
    acc = sb("acc", [128, T], F32)
    r_acc = Res("acc")
    epsT = sb("epsT", [128, 1], F32)
    ps_t = [es.enter_context(nc.psum_tensor("ps%d" % i, [128, 512], F32)) for i in range(7)]
    ps_b = es.enter_context(nc.psum_tensor("psb", [128, 1024], BF16))
    ps_r = [Res("ps%d" % i) for i in range(7)]
    r_psb = Res("psb")
    psA = PsRot(ps_t[0:4], ps_r[0:4])
    psE = PsRot(ps_t[4:7], ps_r[4:7])

    IDENT, TRIF, TRIB, MASKF, MASKB, WLO, WHI, ROPE, ONES, ROPE2, ONES2 = range(11)

    S.dma("sp", cst_f[:], cst_d, writes=[r_cst])
    S.op("dve", lambda e: e.tensor_copy(out=cst_b[:], in_=cst_f[:]), reads=[r_cst], writes=[r_cst])
    S.op("dve", lambda e: e.memset(epsT[:], EPS), writes=[r_cst])
    S.dma("sp", ng_s[:], ng_d, writes=[r_small])
    S.dma("sp", bm_s[:], bm_d, writes=[r_small])
    ones_b = cst_b[:, ONES, :]
    ones_f = cst_f[:, ONES, :]
    ident_b = cst_b[:, IDENT, :]

    def rstd_from(dst, src, scale, rd, wr, n_part=128):
        S.op("act", lambda e: e.activation(out=dst, in_=src, func=AF.Ln, bias=epsT[0:n_part, :], scale=scale),
             reads=rd + [r_cst], writes=wr)
        S.op("act", lambda e: e.activation(out=dst, in_=dst, func=AF.Exp, scale=-0.5), reads=wr, writes=wr)

    scb = sb("m_scb", [128, KC, 2], BF16)
    r_sc = Res("scb")

    def mod_gen(l, wm, pm, rpm):
        pmv = pm[:, 0:192].rearrange("p (j s) -> p j s", s=2)
        for piece in range(24):
            wt, rw = wm.next()
            S.dma("pool", wt[:], w_mod[l, :, piece * 512:(piece + 1) * 512].rearrange("(k p) n -> p k n", p=128), writes=[rw])
            yield
            for j4 in range(4):
                j = piece * 4 + j4
                for kc in range(KC):
                    S.op("pe", lambda e, kc=kc, j=j, j4=j4, wt=wt: e.matmul(
                        pmv[:, j, :], wt[:, kc, j4 * 128:(j4 + 1) * 128], scb[:, kc, :],
                        start=(kc == 0), stop=(kc == KC - 1)), reads=[rw, r_sc], writes=[rpm])
        S.op("dve", lambda e: e.tensor_tensor(out=modv[:, l, :, :], in0=pmv, in1=bc(bm_s[:, l, :], 2, [128, 96, 2]), op=ALU.add),
             reads=[rpm, r_small], dwrites=[r_modv])

    def stage_mod():
        with ExitStack() as st:
            cv = sb("m_cv", [128, KC, 2], F32, st)
            wm = Rot(nc, st, "m_w", [128, KC, 512], BF16, 2)
            S.dma("sp", cv[:], cvec, writes=[r_sc])
            S.op("act", lambda e: e.activation(out=scb[:], in_=cv[:], func=AF.Silu), reads=[r_sc], writes=[r_sc])
            pm, rpm = psE.next()
            for _ in mod_gen(0, wm, pm, rpm):
                pass
            S.barrier()

    def layer_scalars(l, part):
        def g(i):
            return bc(ng_s[:, l, i, :], 2, [128, KC, 2])
        mv = lambda c: modv[:, l, c * KC:(c + 1) * KC, :]
        rd = [r_modv, r_small]
        o = 0 if part == 1 else 3
        sfx = "1" if part == 1 else "2"
        rl = r_lay[part]
        S.op("dve", lambda e: e.scalar_tensor_tensor(out=lay["A" + sfx][:], in0=mv(o + 1), scalar=1.0, in1=g(0 if part == 1 else 2),
                                                     op0=ALU.add, op1=ALU.mult), reads=rd, writes=[rl])
        S.op("dve", lambda e: e.tensor_copy(out=lay["B" + sfx][:], in_=mv(o + 0)), reads=rd, dwrites=[rl])
        S.op("dve", lambda e: e.tensor_tensor(out=lay["G" + sfx][:], in0=mv(o + 2), in1=g(1 if part == 1 else 3), op=ALU.mult),
             reads=rd, dwrites=[rl])

    def interleave(gens, width):
        active = []
        it = iter(gens)
        while True:
            while len(active) < width:
                g = next(it, None)
                if g is None:
                    break
                active.append(g)
            if not active:
                break
            for g in list(active):
                try:
                    next(g)
                except StopIteration:
                    active.remove(g)

    def stage_resnorm(x_src, r_xsrc, res_G, norm_AB, x_dst, r_xdst, final=False, tlim=T):
        NB = 128
        WIDTH = 4
        with ExitStack() as st:
            xb_p = Rot(nc, st, "rn_x", [128, KC, NB], F32, WIDTH + 1)
            ob_p = Rot(nc, st, "rn_o", [128, KC, NB], F32, WIDTH + 1)
            sq_p = Rot(nc, st, "rn_sq", [128, KC, NB], BF16, WIDTH)
            rs_p = Rot(nc, st, "rn_rs", [128, NB], F32, 2 * WIDTH)
            psR = PsRot(ps_t[0:7], ps_r[0:7])
            nblk = (TL if final else tlim) // NB

            def blk(b):
                t0 = b * NB
                s = 1 if t0 >= TL else 0
                xb, rx = xb_p.next()
                ob, ro = ob_p.next()
                S.dma("sp", xb[:], x_src[:, :, t0:t0 + NB].rearrange("k p t -> p k t"), writes=[rx])
                if res_G:
                    S.dma("sp", ob[:], oT[:, :, t0:t0 + NB].rearrange("k p t -> p k t"), writes=[ro])
                    yield
                    pr, rpr = psR.next()
                    S.op("pe", lambda e: e.matmul(pr[:, 0:NB], ones_f, acc[:, t0:t0 + NB], start=True, stop=True),
                         reads=[r_acc, r_cst], writes=[rpr])
                    yield
                    rs, rrs = rs_p.next()
                    S.op("act", lambda e: e.activation(out=rs[:], in_=pr[:, 0:NB], func=AF.Ln, bias=epsT[:, :], scale=1.0 / D),
                         reads=[rpr, r_cst], writes=[rrs])
                    yield
                    S.op("act", lambda e: e.activation(out=rs[:], in_=rs[:], func=AF.Exp, scale=-0.5), writes=[rrs])
                    yield
                    for kc in range(KC):
                        S.op("dve", lambda e, kc=kc: e.scalar_tensor_tensor(out=ob[:, kc, :], in0=ob[:, kc, :], scalar=lay[res_G][:, kc, s:s + 1],
                                                                           in1=rs[:], op0=ALU.mult, op1=ALU.mult),
                             reads=[rrs, r_lay[int(res_G[1])]], writes=[ro] if kc == 0 else (), dwrites=() if kc == 0 else [ro])
                    yield
                    S.op("pool", lambda e: e.tensor_tensor(out=xb[:, 0:10, :], in0=xb[:, 0:10, :], in1=ob[:, 0:10, :], op=ALU.add),
                         reads=[ro], writes=[rx])
                    S.op("dve", lambda e: e.tensor_tensor(out=xb[:, 10:KC, :], in0=xb[:, 10:KC, :], in1=ob[:, 10:KC, :], op=ALU.add),
                         reads=[ro], dwrites=[rx])
                yield
                if x_dst is not None:
                    S.dma("sp", x_dst[:, :, t0:t0 + NB].rearrange("k p t -> p k t"), xb[:], reads=[rx], dwrites=[r_xdst])
                if norm_AB:
                    A, B = norm_AB
                    sq, rsq = sq_p.next()
                    S.op("act", lambda e: e.activation(out=sq[:], in_=xb[:], func=AF.Square), reads=[rx], writes=[rsq])
                    yield
                    pn, rpn = psR.next()
                    for kc in range(KC):
                        S.op("pe", lambda e, kc=kc: e.matmul(pn[:, 0:NB], ones_b, sq[:, kc, :], start=(kc == 0), stop=(kc == KC - 1)),
                             reads=[rsq, r_cst], writes=[rpn])
                    yield
                    rs, rrs = rs_p.next()
                    S.op("act", lambda e: e.activation(out=rs[:], in_=pn[:, 0:NB], func=AF.Ln, bias=epsT[:, :], scale=1.0 / D),
                         reads=[rpn, r_cst], writes=[rrs])
                    yield
                    S.op("act", lambda e: e.activation(out=rs[:], in_=rs[:], func=AF.Exp, scale=-0.5), writes=[rrs])
                    yield
                    S.op("dve", lambda e: e.tensor_tensor(out=ob[:], in0=xb[:], in1=bc(rs[:], 1, [128, KC, NB]), op=ALU.mult),
                         reads=[rx, rrs], writes=[ro])
                    yield
                    for kc in range(KC):
                        eng_ = "act" if kc % 4 != 3 else "pool"
                        if eng_ == "act":
                            S.op("act", lambda e, kc=kc: e.activation(out=bigk[:, kc, t0:t0 + NB], in_=ob[:, kc, :], func=AF.Identity,
                                                                      bias=lay[B][:, kc, s:s + 1], scale=lay[A][:, kc, s:s + 1]),
                                 reads=[ro, r_lay[int(A[1])]], dwrites=[r_big])
                        else:
                            S.op("pool", lambda e, kc=kc: e.tensor_scalar(out=bigk[:, kc, t0:t0 + NB], in0=ob[:, kc, :],
                                                                          scalar1=lay[A][:, kc, s:s + 1], scalar2=lay[B][:, kc, s:s + 1],
                                                                          op0=ALU.mult, op1=ALU.add),
                                 reads=[ro, r_lay[int(A[1])]], dwrites=[r_big])
                yield

            interleave((blk(b) for b in range(nblk)), WIDTH)
            S.barrier()

    def load_w(rot, Wl, col0, ncols, nk):
        wt, rw = rot.next()
        S.dma("pool", wt[:, 0:nk, 0:ncols], Wl[:, col0:col0 + ncols].rearrange("(k p) n -> p k n", p=128), writes=[rw])
        return wt, rw

    def fm_acc(wt, rw, c0, m, t0, n, nk=KC, inview=None):
        iv = bigk if inview is None else inview
        ps, rps = psA.next()
        for kc in range(nk):
            S.op("pe", lambda e, kc=kc: e.matmul(ps[0:m, 0:n], wt[:, kc, c0:c0 + m], iv[:, kc, t0:t0 + n],
                                                  start=(kc == 0), stop=(kc == nk - 1)), reads=[rw, r_big], writes=[rps])
        S.run_deferred()
        return ps, rps

    def tm_acc(wt, rw, c0, n, tt):
        ps, rps = psA.next()
        for kc in range(KC):
            S.op("pe", lambda e, kc=kc: e.matmul(ps[:, 0:n], bigk[:, kc, tt * 128:(tt + 1) * 128], wt[:, kc, c0:c0 + n],
                                                  start=(kc == 0), stop=(kc == KC - 1)), reads=[rw, r_big], writes=[rps])
        S.run_deferred()
        return ps, rps

    def stage_inproj(l):
        Wl = w_in[l]
        TBL = TB5
        with ExitStack() as st:
            wrot = Rot(nc, st, "ip_w", [128, KC, 512], BF16, 2)
            cosT = sb("ip_cos", [128, TL], F32, st)
            sinT = sb("ip_sin", [128, TL], F32, st)
            r_cs = Res()
            qkg = sb("ip_qkg", [128, 2], F32, st)
            lng = sb("ip_lng", [128, 512], F32, st)
            lnb = sb("ip_lnb", [128, 512], F32, st)
            S.dma("sp", cosT[:], cos_d, writes=[r_cs])
            S.dma("sp", sinT[:], sin_d, dwrites=[r_cs])
            S.dma("sp", qkg[:], qkg_d[:, l, :], dwrites=[r_cs])
            S.dma("sp", lng[:], lng_d[:, l, :], dwrites=[r_cs])
            S.dma("sp", lnb[:], lnb_d[:, l, :], dwrites=[r_cs])
            stb = Rot(nc, st, "ip_sb", [128, 512], BF16, 4)
            stf = Rot(nc, st, "ip_sf", [128, 512], F32, 4)
            st1 = Rot(nc, st, "ip_s1", [128, 8], F32, 4)
            xq_p = Rot(nc, st, "ip_xq", [128, 512], BF16, 8)
            hsq_p = Rot(nc, st, "ip_hsq", [128, 512], BF16, 6)
            hob_p = Rot(nc, st, "ip_hob", [128, 512], BF16, 6)

            def evac_copy(ps, rps, m, n, dst, rdst, eng="act", func=None, dt_f32=False):
                sbt, rs = (stf if dt_f32 else stb).next()
                if eng == "act":
                    S.op("act", lambda e: e.activation(out=sbt[0:m, 0:n], in_=ps[0:m, 0:n], func=func or AF.Copy),
                         reads=[rps], writes=[rs])
                else:
                    S.op("dve", lambda e: e.tensor_copy(out=sbt[0:m, 0:n], in_=ps[0:m, 0:n]), reads=[rps], writes=[rs])
                S.dma("sp", dst, sbt[0:m, 0:n], reads=[rs], dwrites=[rdst])

            def headproc(ps, rps, t0, n, normg, dst, rdst):
                latent = t0 < TL
                xq, rxq = xq_p.next()
                if normg is None:
                    S.op("act", lambda e: e.activation(out=xq[:, 0:n], in_=ps[:, 0:n], func=AF.Copy), reads=[rps], writes=[rxq])
                    stepB_ready = True
                else:
                    sq, rsq = hsq_p.next()
                    S.op("act", lambda e: e.activation(out=sq[:, 0:n], in_=ps[:, 0:n], func=AF.Square), reads=[rps], writes=[rsq])

                def stepB():
                    if not latent:
                        S.dma("sp", dst, xq[:, 0:n], reads=[rxq], dwrites=[rdst])
                        return None
                    pr, rpr = psE.next()
                    S.op("pe", lambda e: e.matmul(pr[:, 0:n], cst_b[:, ROPE2, :], xq[:, 0:n], start=True, stop=True),
                         reads=[rxq, r_cst], writes=[rpr])
                    t1, rt1 = stf.next()
                    t2, rt2 = stf.next()
                    ob, rob = hob_p.next()
                    S.op("dve", lambda e: e.tensor_tensor(out=t1[:, 0:n], in0=xq[:, 0:n], in1=cosT[:, t0:t0 + n], op=ALU.mult),
                         reads=[rxq, r_cs], writes=[rt1])
                    S.op("dve", lambda e: e.tensor_tensor(out=t2[:, 0:n], in0=pr[:, 0:n], in1=sinT[:, t0:t0 + n], op=ALU.mult),
                         reads=[rpr, r_cs], writes=[rt2])
                    S.op("pool", lambda e: e.tensor_tensor(out=ob[:, 0:n], in0=t1[:, 0:n], in1=t2[:, 0:n], op=ALU.add),
                         reads=[rt1, rt2], writes=[rob])
                    S.dma("sp", dst, ob[:, 0:n], reads=[rob], dwrites=[rdst])
                    return None

                def stepA():
                    pn, rpn = psE.next()
                    S.op("pe", lambda e: e.matmul(pn[:, 0:n], cst_b[:, ONES2, :], sq[:, 0:n], start=True, stop=True),
                         reads=[rsq, r_cst], writes=[rpn])
                    rs, rrs = stf.next()
                    rstd_from(rs[:, 0:n], pn[:, 0:n], 1.0 / 64, [rpn], [rrs], n_part=128)
                    S.op("dve", lambda e: e.scalar_tensor_tensor(out=xq[:, 0:n], in0=ps[:, 0:n], scalar=qkg[:, normg:normg + 1],
                                                                 in1=rs[:, 0:n], op0=ALU.mult, op1=ALU.mult),
                         reads=[rps, rrs, r_cs], writes=[rxq])
                    return stepB

                S.defer(stepB if normg is None else stepA)

            def fm_heads(wt, rw, c0, nheads, dst, rdst, normg=None, plain=False, func=None, dt_f32=False, m=64):
                if m == 64:
                    for hp_ in range(nheads // 2):
                        for (t0, n) in TBL:
                            ps, rps = fm_acc(wt, rw, c0 + hp_ * 128, 128, t0, n)
                            d2 = dst[2 * hp_:2 * hp_ + 2, :, t0:t0 + n].rearrange("h p t -> (h p) t")
                            if plain:
                                evac_copy(ps, rps, 128, n, d2, rdst, func=func, dt_f32=dt_f32)
                            else:
                                headproc(ps, rps, t0, n, normg, d2, rdst)
                    return
                for h in range(nheads):
                    for (t0, n) in TBL:
                        ps, rps = fm_acc(wt, rw, c0 + h * m, m, t0, n)
                        evac_copy(ps, rps, m, n, dst[h, :, t0:t0 + n] if dst.ndim == 3 else dst[:, t0:t0 + n], rdst,
                                  func=func, dt_f32=dt_f32)

            def tm_cols(wt, rw, c0, n, dst, rdst):
                for tt in range(NTILE):
                    ps, rps = tm_acc(wt, rw, c0, n, tt)
                    evac_copy(ps, rps, 128, n, dst[tt * 128:(tt + 1) * 128, :], rdst, eng="dve")

            groups = [(0, 512), (512, 512), (1024, 512), (1536, 32), (1568, 512), (2080, 256), (2336, 512), (2848, 512),
                      (3360, 512), (3872, 256)]
            nxt = load_w(wrot, Wl, groups[0][0], groups[0][1], KC)
            for gi, (c0g, ncg) in enumerate(groups):
                wt, rw = nxt
                if gi + 1 < len(groups):
                    nxt = load_w(wrot, Wl, groups[gi + 1][0], groups[gi + 1][1], KC)
                if gi == 0:
                    fm_heads(wt, rw, 0, 4, aqT, R["aqT"], plain=True)
                    fm_heads(wt, rw, 256, 4, akT, R["akT"], plain=True)
                    tm_cols(wt, rw, 256, 256, ak_tok, R["ak_tok"])
                elif gi == 1:
                    tm_cols(wt, rw, 0, 512, av_tok, R["av_tok"])
                elif gi == 2:
                    fm_heads(wt, rw, 0, 4, agT, R["agT"], plain=True, func=AF.Silu, m=128)
                elif gi == 3:
                    fm_heads(wt, rw, 0, 1, lrT, R["lrT"], plain=True, dt_f32=True, m=32)
                elif gi == 4:
                    fm_heads(wt, rw, 0, 8, bqT, R["bqT"])
                elif gi == 5:
                    fm_heads(wt, rw, 0, 2, bkT, R["bkT"])
                    tm_cols(wt, rw, 128, 128, bv_tok, R["bv_tok"])
                elif gi == 6:
                    fm_heads(wt, rw, 0, 4, cuT, R["cuT"], plain=True, func=AF.Gelu_apprx_tanh, m=128)
                elif gi == 7:
                    for tt in range(NTILE):
                        ps, rps = tm_acc(wt, rw, 0, 512, tt)
                        g, rg = stf.next()
                        S.op("act", lambda e: e.activation(out=g[:], in_=ps[:], func=AF.Gelu_apprx_tanh), reads=[rps], writes=[rg])
                        s1, rs1 = st1.next()
                        S.op("dve", lambda e: e.bn_stats(out=s1[:, 0:6], in_=g[:]), reads=[rg], writes=[rs1])
                        S.op("dve", lambda e: e.bn_aggr(out=s1[:, 6:8], in_=s1[:, 0:6]), reads=[rs1], writes=[rs1])
                        rstd_from(s1[:, 7:8], s1[:, 7:8], 1.0, [rs1], [rs1])
                        S.op("dve", lambda e: e.scalar_tensor_tensor(out=s1[:, 6:7], in0=s1[:, 6:7], scalar=-1.0, in1=s1[:, 7:8],
                                                                     op0=ALU.mult, op1=ALU.mult), reads=[rs1], writes=[rs1])
                        S.op("act", lambda e: e.activation(out=g[:], in_=g[:], func=AF.Identity, bias=s1[:, 6:7], scale=s1[:, 7:8]),
                             reads=[rs1], writes=[rg])
                        S.op("dve", lambda e: e.tensor_tensor(out=g[:], in0=g[:], in1=lng[:], op=ALU.mult), reads=[r_cs], writes=[rg])
                        ob, rob = stb.next()
                        S.op("pool", lambda e: e.tensor_tensor(out=ob[:], in0=g[:], in1=lnb[:], op=ALU.add), reads=[rg, r_cs], writes=[rob])
                        S.dma("sp", cv_tok[tt * 128:(tt + 1) * 128, :], ob[:], reads=[rob], dwrites=[R["cv_tok"]])
                elif gi == 8:
                    fm_heads(wt, rw, 0, 8, dqT, R["dqT"], normg=0)
                elif gi == 9:
                    fm_heads(wt, rw, 0, 2, dkT, R["dkT"], normg=1)
                    tm_cols(wt, rw, 128, 128, dv_tok, R["dv_tok"])
            S.barrier()

    def stage_gla(l):
        with ExitStack() as st:
            gub = sb("g_gub", [33, 512], F32, st)
            ga = sb("g_ga", [128, 4], F32, st)
            r_g = Res()
            S.dma("sp", gub[:], gub_d[l], writes=[r_g])
            S.dma("sp", ga[:], ga_d[:, l, :], dwrites=[r_g])
            of_s = sb("g_of", [128, 4, T], BF16, st)
            r_of = Res()
            stored = set()
            gT_p = Rot(nc, st, "g_g", [128, 4, 128], BF16, 3)
            tot_p = Rot(nc, st, "g_tot", [128, 4, 128], F32, 3)
            sq_p = Rot(nc, st, "g_sq", [128, 4, 128], BF16, 3)
            rs_p = Rot(nc, st, "g_rs", [128, 4, 128], F32, 3)
            yo_p = Rot(nc, st, "g_yo", [128, 4, 128], BF16, 3)
            ps7 = ps_b[:].bitcast(F32)
            BK = [dict(A=(ps_t[0], ps_r[0]), B=(ps_t[1], ps_r[1]), C=(ps_t[2], ps_r[2]), P=(ps_t[6], ps_r[6])),
                  dict(A=(ps_t[3], ps_r[3]), B=(ps_t[4], ps_r[4]), C=(ps_t[5], ps_r[5]), P=(ps7, r_psb))]
            DS = []
            for d in (0, 1):
                o = {}
                o["Sf"] = sb("g_Sf", [64, 4, 128], F32, st)
                o["Sb"] = sb("g_Sb", [64, 4, 128], BF16, st)
                o["rS"] = Res()
                o["lr"] = Rot(nc, st, "g_lr", [33, 128], F32, 2)
                for t_, _r in zip(o["lr"].t, o["lr"].r):
                    S.op("dve", lambda e, t_=t_: e.memset(t_[32:33, :], 1.0), writes=[_r])
                o["sp"] = Rot(nc, st, "g_sp", [128, 256], F32, 2)
                o["Ep"] = Rot(nc, st, "g_Ep", [64, 4, 128], F32, 2)
                o["Em"] = Rot(nc, st, "g_Em", [64, 4, 128], F32, 2)
                o["Emt"] = Rot(nc, st, "g_Emt", [128, 256], F32, 2)
                o["qT"] = Rot(nc, st, "g_q", [64, 4, 128], BF16, 2)
                o["kT"] = Rot(nc, st, "g_k", [64, 4, 128], BF16, 2)
                o["ktok"] = Rot(nc, st, "g_kt", [128, 256], BF16, 2)
                o["v"] = Rot(nc, st, "g_v", [128, 512], BF16, 2)
                o["qt"] = Rot(nc, st, "g_qt", [64, 4, 128], BF16, 2)
                o["kt"] = Rot(nc, st, "g_ktl", [64, 4, 128], BF16, 2)
                o["ktt"] = Rot(nc, st, "g_ktt", [128, 256], BF16, 2)
                o["am"] = Rot(nc, st, "g_am", [128, 4, 128], BF16, 2)
                DS.append(o)

            def prep(d, tt, P):
                o = DS[d]
                tok = slice(tt * 128, (tt + 1) * 128)
                tri = cst_f[:, TRIF + d, :]
                lr, rlr = o["lr"].next()
                S.dma("sp", lr[0:32, :], lrT[:, tok], writes=[rlr])
                qT, rq = o["qT"].next()
                S.dma("sp", qT[:], aqT[:, :, tok].rearrange("h p t -> p h t"), writes=[rq])
                kT, rk = o["kT"].next()
                S.dma("sp", kT[:], akT[:, :, tok].rearrange("h p t -> p h t"), writes=[rk])
                ktok, rkt = o["ktok"].next()
                S.dma("sp", ktok[:], ak_tok[tok, :], writes=[rkt])
                v, rv = o["v"].next()
                S.dma("sp", v[:], av_tok[tok, :], writes=[rv])
                yield
                pz, rpz = BK[d]["P"]
                S.op("pe", lambda e: e.matmul(pz[:, 0:256], lr[:, :], gub[:, d * 256:(d + 1) * 256], start=True, stop=True),
                     reads=[rlr, r_g], writes=[rpz])
                yield
                sp, rsp = o["sp"].next()
                S.op("act", lambda e: e.activation(out=sp[:], in_=pz[:, 0:256], func=AF.Exp, scale=-1.0), reads=[rpz], writes=[rsp])
                yield
                S.op("act", lambda e: e.activation(out=sp[:], in_=sp[:], func=AF.Ln, bias=cst_f[:, ONES, 0:1]), reads=[r_cst], writes=[rsp])
                yield
                pc, rpc = BK[d]["P"]
                pcv = pc[0:64, :].rearrange("p (h t) -> p h t", h=4)
                for h in range(4):
                    S.op("pe", lambda e, h=h: e.matmul(pcv[:, h, :], sp[:, h * 64:(h + 1) * 64], tri, start=True, stop=True),
                         reads=[rsp, r_cst], writes=[rpc])
                yield
                Ep, rEp = o["Ep"].next()
                Em, rEm = o["Em"].next()
                Emt, rEmt = o["Emt"].next()
                S.op("act", lambda e: e.activation(out=Ep[:], in_=pcv, func=AF.Exp), reads=[rpc], writes=[rEp])
                S.op("act", lambda e: e.activation(out=Em[:], in_=pcv, func=AF.Exp, scale=-1.0), reads=[rpc], writes=[rEm])
                yield
                pct, rpct = BK[d]["P"]
                S.op("pe", lambda e: e.matmul(pct[:, 0:256], tri, sp[:], start=True, stop=True), reads=[rsp, r_cst], writes=[rpct])
                yield
                S.op("act", lambda e: e.activation(out=Emt[:], in_=pct[:, 0:256], func=AF.Exp, scale=-1.0), reads=[rpct], writes=[rEmt])
                yield
                qt, rqt = o["qt"].next()
                kt, rktl = o["kt"].next()
                ktt, rktt = o["ktt"].next()
                S.op("dve", lambda e: e.scalar_tensor_tensor(out=qt[:], in0=qT[:], scalar=0.125, in1=Ep[:], op0=ALU.mult, op1=ALU.mult),
                     reads=[rq, rEp], writes=[rqt])
                S.op("pool", lambda e: e.tensor_tensor(out=kt[:], in0=kT[:], in1=Em[:], op=ALU.mult), reads=[rk, rEm], writes=[rktl])
                S.op("pool", lambda e: e.tensor_tensor(out=ktt[:], in0=ktok[:], in1=Emt[:], op=ALU.mult), reads=[rkt, rEmt], writes=[rktt])
                P.update(dict(v=v, rv=rv, Ep=Ep, rEp=rEp, qt=qt, rqt=rqt, kt=kt, rktl=rktl, ktt=ktt, rktt=rktt))
                yield

            def scan(d, tt, P):
                o = DS[d]
                Sf, Sb, r_S = o["Sf"], o["Sb"], o["rS"]
                tok = slice(tt * 128, (tt + 1) * 128)
                mask = cst_b[:, MASKF + d, :]
                v, rv, Ep, rEp, qt, rqt, kt, rktl, ktt, rktt = (P[k] for k in ("v", "rv", "Ep", "rEp", "qt", "rqt", "kt", "rktl", "ktt", "rktt"))
                pa, rpa = BK[d]["A"]
                pav = pa[:, :].rearrange("p (h t) -> p h t", h=4)
                for h in range(4):
                    S.op("pe", lambda e, h=h: e.matmul(pav[:, h, :], kt[:, h, :], qt[:, h, :], start=True, stop=True),
                         reads=[rktl, rqt], writes=[rpa])
                cA, cB = (0, 1) if d == 0 else (1, 0)
                psu = {}
                for c in (cA, cB):
                    pu, rpu = BK[d]["B" if c == cA else "C"]
                    puv = pu[0:64, :].rearrange("p (h t) -> p h t", h=4)
                    for h in range(4):
                        S.op("pe", lambda e, h=h, c=c, puv=puv: e.matmul(puv[:, h, :], ktt[c * 64:(c + 1) * 64, h * 64:(h + 1) * 64],
                                                                        v[c * 64:(c + 1) * 64, h * 128:(h + 1) * 128], start=True, stop=True),
                             reads=[rktt, rv], writes=[rpu])
                    psu[c] = (puv, rpu)
                yield
                am, ram = o["am"].next()
                S.op("dve", lambda e: e.tensor_tensor(out=am[:], in0=pav, in1=bc(mask, 1, [128, 4, 128]), op=ALU.mult),
                     reads=[rpa, r_cst], writes=[ram])
                yield
                po, rpo = BK[d]["A"]
                pov = po[:, :].rearrange("p (h t) -> p h t", h=4)
                for h in range(4):
                    S.op("pe", lambda e, h=h: e.matmul(pov[:, h, :], v[:, h * 128:(h + 1) * 128], am[:, h, :], start=(h == 0), stop=False,
                                                       skip_group_check=True), reads=[rv, ram], writes=[rpo])
                for ci, c in enumerate((cA, cB)):
                    for h in range(4):
                        S.op("pe", lambda e, h=h, c=c, ci=ci: e.matmul(pov[:, h, c * 64:(c + 1) * 64], Sb[:, h, :], qt[:, h, c * 64:(c + 1) * 64],
                                                                       start=False, stop=(ci == 1 and h == 3), skip_group_check=True),
                             reads=[r_S, rqt], writes=[rpo])
                    yield
                    puv, rpu = psu[c]
                    col = (c * 64 + 63) if d == 0 else (c * 64)
                    S.op("dve", lambda e, puv=puv: e.tensor_tensor(out=Sf[:], in0=puv, in1=Sf[:], op=ALU.add), reads=[rpu], writes=[r_S])
                    S.op("dve", lambda e, col=col: e.tensor_tensor(out=Sf[:], in0=Sf[:], in1=bc(Ep[:, :, col], 2, [64, 4, 128]), op=ALU.mult),
                         reads=[rEp], writes=[r_S])
                    yield
                    S.op("pool", lambda e: e.tensor_copy(out=Sb[:], in_=Sf[:]), writes=[r_S])
                    yield
                if tt not in stored:
                    stored.add(tt)
                    S.op("act", lambda e: e.activation(out=of_s[:, :, tok], in_=pov, func=AF.Copy), reads=[rpo], dwrites=[r_of])
                    yield
                    return
                gT, rgT = gT_p.next()
                S.dma("sp", gT[:], agT[:, :, tok].rearrange("h p t -> p h t"), writes=[rgT])
                tot, rtot = tot_p.next()
                S.op("dve", lambda e: e.tensor_tensor(out=tot[:], in0=pov, in1=of_s[:, :, tok], op=ALU.add), reads=[rpo, r_of], writes=[rtot])
                yield
                sq, rsq = sq_p.next()
                S.op("act", lambda e: e.activation(out=sq[:], in_=tot[:], func=AF.Square), reads=[rtot], writes=[rsq])
                yield
                pn, rpn = BK[d]["A"]
                S.op("pe", lambda e: e.matmul(pn[:, :], ones_b, sq[:].rearrange("p h t -> p (h t)"), start=True, stop=True),
                     reads=[rsq, r_cst], writes=[rpn])
                yield
                rs, rrs = rs_p.next()
                rsf = rs[:].rearrange("p h t -> p (h t)")
                S.op("act", lambda e: e.activation(out=rsf, in_=pn[:, :], func=AF.Ln, bias=epsT[:, :], scale=1.0 / 128), reads=[rpn, r_cst], writes=[rrs])
                yield
                S.op("act", lambda e: e.activation(out=rsf, in_=rsf, func=AF.Exp, scale=-0.5), writes=[rrs])
                yield
                S.op("dve", lambda e: e.tensor_tensor(out=tot[:], in0=tot[:], in1=rs[:], op=ALU.mult), reads=[rrs], writes=[rtot])
                yield
                S.op("pool", lambda e: e.tensor_tensor(out=tot[:], in0=tot[:], in1=gT[:], op=ALU.mult), reads=[rgT], writes=[rtot])
                yield
                yo, ryo = yo_p.next()
                S.op("dve", lambda e: e.tensor_tensor(out=yo[:], in0=tot[:], in1=bc(ga[:], 2, [128, 4, 128]), op=ALU.mult),
                     reads=[rtot, r_g], writes=[ryo])
                yield
                S.dma("sp", yT[0:4, :, tok].rearrange("h p t -> p h t"), yo[:], reads=[ryo], dwrites=[R["yT"]])

            def dir_gen(d):
                o = DS[d]
                S.op("dve", lambda e: e.memset(o["Sf"][:], 0.0), writes=[o["rS"]])
                S.op("dve", lambda e: e.memset(o["Sb"][:], 0.0), dwrites=[o["rS"]])
                order = [16, 17] + list(range(16)) if d == 0 else [17, 16] + list(range(15, -1, -1))
                P = {}
                for _ in prep(d, order[0], P):
                    yield
                for k, tt in enumerate(order):
                    Pn = {}
                    active = [scan(d, tt, P)]
                    if k + 1 < len(order):
                        active.append(prep(d, order[k + 1], Pn))
                    while active:
                        for g_ in list(active):
                            try:
                                next(g_)
                            except StopIteration:
                                active.remove(g_)
                        yield
                    P = Pn

            interleave([dir_gen(0), dir_gen(1)], GLA_W)
            S.barrier()

    def stage_attn(l, which, need_ctx, modl=None, side=None):
        qsrc, ksrc, vsrc = (bqT, bkT, bv_tok) if which == "B" else (dqT, dkT, dv_tok)
        rq_, rk_, rv_ = (R["bqT"], R["bkT"], R["bv_tok"]) if which == "B" else (R["dqT"], R["dkT"], R["dv_tok"])
        ych0 = 4 if which == "B" else 12
        with ExitStack() as st:
            QT = sb("a_Q", [128, 2, 4, T], BF16, st)
            KT = sb("a_K", [128, T], BF16, st)
            VX = sb("a_V", [128, NTILE, 2, 65], BF16, st)
            gv = sb("a_gv", [128, 512], F32, st)
            sinkE = sb("a_sink", [128, 8], F32, st)
            r_in = Res()
            S.op("dve", lambda e: e.memset(VX[:], 1.0), writes=[r_in])
            S.op("pool", lambda e: e.memset(QT[64:128, 0, :, :], 0.0), dwrites=[r_in])
            S.op("dve", lambda e: e.memset(QT[0:64, 1, :, :], 0.0), dwrites=[r_in])
            for (t0, n) in TB5:
                for g in range(2):
                    S.dma("sp", QT[64 * g:64 * g + 64, g, :, t0:t0 + n], qsrc[4 * g:4 * g + 4, :, t0:t0 + n].rearrange("h p t -> p h t"),
                          dwrites=[r_in])
            for g in range(2):
                S.dma("sp", KT[64 * g:64 * g + 64, :], ksrc[g], dwrites=[r_in])
            r_vx = Res()
            r_sk = Res()
            for g in range(2):
                S.dma("sp", VX[:, :, g, 0:64], vsrc[:, g * 64:(g + 1) * 64].rearrange("(t p) d -> p t d", p=128),
                      reads=[rv_, r_in], dwrites=[r_vx])
            S.dma("sp", gv[:], (gbb_d if which == "B" else gdb_d)[:, l, :], dwrites=[r_in])
            if which == "B":
                S.dma("sp", sinkE[:], sink_d[:, l, :], writes=[r_sk])
                S.op("act", lambda e: e.activation(out=sinkE[:], in_=sinkE[:], func=AF.Exp), writes=[r_sk])
            pT_p = Rot(nc, st, "a_pT", [128, 4, 128], BF16, 5)
            psS = PsRot(ps_t[0:3], ps_r[0:3])
            gen = None
            sgen = side(st, (ps_b[:].bitcast(F32)[:, 256:512], r_psb)) if side is not None else None
            if modl is not None:
                wm = Rot(nc, st, "a_wm", [128, KC, 512], BF16, 2)
                gen = mod_gen(modl, wm, ps_t[3], ps_r[3])
            ob_p = Rot(nc, st, "a_ob", [128, 8, 64], F32, 2)
            s1_p = Rot(nc, st, "a_s1", [128, 20], F32, 2)
            junk_p = Rot(nc, st, "a_jk", [128, 512], BF16, 1)
            y_p = Rot(nc, st, "a_y", [128, 512], BF16, 2)
            ys_p = Rot(nc, st, "a_ys", [128, 4, 128], BF16, 2)
            qtiles = list(range(16)) + ([16, 17] if need_ctx else [])
            use_sink = (which == "B")
            items = []
            for qt in qtiles:
                if qt >= 16:
                    keys = [(16, None), (17, None)]
                elif which == "B":
                    keys = []
                    if qt > 0:
                        keys.append((qt - 1, WLO))
                    keys.append((qt, None))
                    if qt < 15:
                        keys.append((qt + 1, WHI))
                    keys += [(16, None), (17, None)]
                else:
                    keys = [(k, None) for k in range(NTILE)]
                for g in range(2):
                    for ki, (kt, mk) in enumerate(keys):
                        items.append((qt, g, ki, kt, mk, len(keys)))

            def emit_qk(it):
                qt, g, ki, kt, mk, nk = it
                ps_, rps = psS.next()
                psv = ps_[:, :].rearrange("p (h t) -> p h t", h=4)
                S.op("pe", lambda e: e.matmul(psv, KT[:, kt * 128:(kt + 1) * 128], QT[:, g, :, qt * 128:(qt + 1) * 128],
                                              start=True, stop=True), reads=[r_in], writes=[rps])
                return psv, rps

            def finalize(qt, pos):
                ob, rob = ob_p.next()
                s1, rs1 = s1_p.next()
                for g in range(2):
                    pov, rpo = pos[g]
                    if use_sink:
                        S.op("dve", lambda e, g=g, pov=pov: e.tensor_tensor(out=s1[:, 4 * g:4 * g + 4], in0=pov[:, :, 64],
                                                                            in1=sinkE[:, 4 * g:4 * g + 4], op=ALU.add),
                             reads=[rpo, r_sk], writes=[rs1])
                    else:
                        S.op("dve", lambda e, g=g, pov=pov: e.tensor_copy(out=s1[:, 4 * g:4 * g + 4], in_=pov[:, :, 64]), reads=[rpo], writes=[rs1])
                    S.op("dve", lambda e, g=g: e.reciprocal(out=s1[:, 8 + 4 * g:12 + 4 * g], in_=s1[:, 4 * g:4 * g + 4]), reads=[rs1], writes=[rs1])
                    S.op("dve", lambda e, g=g, pov=pov: e.tensor_tensor(out=ob[:, 4 * g:4 * g + 4, :], in0=pov[:, :, 0:64],
                                                                        in1=bc(s1[:, 8 + 4 * g:12 + 4 * g], 2, [128, 4, 64]), op=ALU.mult),
                         reads=[rpo, rs1], writes=[rob])
                obf = ob[:].rearrange("p h d -> p (h d)")
                jk, rjk = junk_p.next()
                S.op("act", lambda e: e.activation(out=jk[:], in_=obf, func=AF.Square, accum_out=s1[:, 16:17]), reads=[rob], writes=[rjk, rs1])
                rstd_from(s1[:, 17:18], s1[:, 16:17], 1.0 / 512, [rs1], [rs1])
                y, ry = y_p.next()
                S.op("dve", lambda e: e.scalar_tensor_tensor(out=y[:], in0=obf, scalar=s1[:, 17:18], in1=gv[:], op0=ALU.mult, op1=ALU.mult),
                     reads=[rob, rs1, r_in], writes=[ry])

                def part2():
                    ptv = ps_b[:, 0:512].rearrange("p (c t) -> p c t", c=4)
                    for c in range(4):
                        S.op("pe", lambda e, c=c: e.transpose(ptv[:, c, :], y[:, c * 128:(c + 1) * 128], ident_b), reads=[ry, r_cst], writes=[r_psb])
                    ys, rys = ys_p.next()
                    S.op("act", lambda e: e.activation(out=ys[:], in_=ptv, func=AF.Copy), writes=[r_psb, rys])
                    S.dma("sp", yT[ych0:ych0 + 4, :, qt * 128:(qt + 1) * 128].rearrange("c p t -> p c t"), ys[:], reads=[rys], dwrites=[R["yT"]])
                return part2

            LOOK = 2
            pend = []
            nxt_i = 0
            fin_q = []
            pos = []
            for idx, it in enumerate(items):
                while len(pend) < LOOK + 1 and nxt_i < len(items):
                    pend.append(emit_qk(items[nxt_i]))
                    nxt_i += 1
                psv, rps = pend.pop(0)
                qt, g, ki, kt, mk, nk = it
                if ki == 0:
                    if g == 0:
                        pos = []
                    po, rpo = psE.next()
                    pov = po[:, 0:260].rearrange("p (h d) -> p h d", h=4)
                    pos.append((pov, rpo))
                pov, rpo = pos[g]
                pT, rpT = pT_p.next()
                S.op("act", lambda e, pT=pT, psv=psv: e.activation(out=pT[:], in_=psv, func=AF.Exp, scale=0.125), reads=[rps], writes=[rpT])
                if mk is not None:
                    S.op("dve", lambda e, pT=pT, mk=mk: e.tensor_tensor(out=pT[:], in0=pT[:], in1=bc(cst_b[:, mk, :], 1, [128, 4, 128]),
                                                                        op=ALU.mult), reads=[r_cst], writes=[rpT])
                for h4 in range(4):
                    S.op("pe", lambda e, h4=h4, pT=pT, kt=kt, g=g, ki=ki, pov=pov, nk=nk: e.matmul(
                        pov[:, h4, :], pT[:, h4, :], VX[:, kt, g, :], start=(ki == 0 and h4 == 0),
                        stop=(ki == nk - 1 and h4 == 3), skip_group_check=True), reads=[rpT, r_vx, r_in], writes=[rpo])
                for f_ in fin_q:
                    f_[0] -= 1
                while fin_q and fin_q[0][0] <= 0:
                    fin_q.pop(0)[1]()
                if g == 1 and ki == nk - 1:
                    fin_q.append([3, finalize(qt, pos)])
                if gen is not None and idx % 12 == 11:
                    next(gen, None)
                if sgen is not None and idx % 2 == 0:
                    next(sgen, None)
            for f_ in fin_q:
                f_[1]()
            if gen is not None:
                for _ in gen:
                    pass
            if sgen is not None:
                for _ in sgen:
                    pass
            S.barrier()

    def cmlp_gen(l, need_ctx, st, bank):
        if True:
            wsf = sb("c_wsf", [128, 4, 128], BF16, st)
            bsb = sb("c_bsb", [128, 4, 128], F32, st)
            gc = sb("c_gc", [128, 4], F32, st)
            r_c = Res()
            S.dma("pool", wsf[:], wsT_d[l], writes=[r_c])
            S.dma("sp", bsb[:], bsb_d[:, l, :, :], dwrites=[r_c])
            S.dma("sp", gc[:], gc_d[:, l, :], dwrites=[r_c])
            W_ = 1
            cv_p = Rot(nc, st, "c_cv", [128, 512], BF16, W_ + 1)
            cu_p = Rot(nc, st, "c_cu", [128, 4, 128], BF16, W_ + 1)
            t_p = Rot(nc, st, "c_t", [128, 4, 128], F32, 1)
            sq_p = Rot(nc, st, "c_sq", [128, 4, 128], BF16, 1)
            rs_p = Rot(nc, st, "c_rs", [128, 128], F32, 1)
            yo_p = Rot(nc, st, "c_yo", [128, 4, 128], BF16, W_ + 1)

            def tile_gen(tt):
                tok = slice(tt * 128, (tt + 1) * 128)
                cv, rcv = cv_p.next()
                S.dma("sp", cv[:], cv_tok[tok, :], writes=[rcv])
                cu, rcu = cu_p.next()
                S.dma("sp", cu[:], cuT[:, :, tok].rearrange("g p t -> p g t"), writes=[rcu])
                yield
                t, rt = t_p.next()
                reg, rreg = bank
                psv = reg[:, 0:256].rearrange("p (g t) -> p g t", g=2)
                for half in range(2):
                    for gg in range(2):
                        g = half * 2 + gg
                        S.op("pe", lambda e, g=g, gg=gg: e.matmul(psv[:, gg, :], cv[:, g * 128:(g + 1) * 128], wsf[:, g, :], start=True, stop=True),
                             reads=[rcv, r_c], writes=[rreg])
                    yield
                    S.op("dve", lambda e, half=half: e.tensor_tensor(out=t[:, 2 * half:2 * half + 2, :], in0=psv, in1=bsb[:, 2 * half:2 * half + 2, :],
                                                                     op=ALU.add), reads=[r_c], writes=[rreg, rt] if half == 0 else [rreg],
                         dwrites=() if half == 0 else [rt])
                    yield
                S.op("pool", lambda e: e.tensor_tensor(out=t[:], in0=t[:], in1=cu[:], op=ALU.mult), reads=[rcu], writes=[rt])
                yield
                sq, rsq = sq_p.next()
                S.op("act", lambda e: e.activation(out=sq[:], in_=t[:], func=AF.Square), reads=[rt], writes=[rsq])
                yield
                pn, rpn = bank
                for g in range(4):
                    S.op("pe", lambda e, g=g: e.matmul(pn[:, 0:128], ones_b, sq[:, g, :], start=(g == 0), stop=(g == 3)),
                         reads=[rsq, r_cst], writes=[rpn])
                yield
                rs, rrs = rs_p.next()
                S.op("act", lambda e: e.activation(out=rs[:], in_=pn[:, 0:128], func=AF.Ln, bias=epsT[:, :], scale=1.0 / 512),
                     reads=[r_cst], writes=[rpn, rrs])
                yield
                S.op("act", lambda e: e.activation(out=rs[:], in_=rs[:], func=AF.Exp, scale=-0.5), writes=[rrs])
                yield
                S.op("dve", lambda e: e.tensor_tensor(out=t[:], in0=t[:], in1=bc(rs[:], 1, [128, 4, 128]), op=ALU.mult), reads=[rrs], writes=[rt])
                yo, ryo = yo_p.next()
                S.op("dve", lambda e: e.tensor_tensor(out=yo[:], in0=t[:], in1=bc(gc[:], 2, [128, 4, 128]), op=ALU.mult),
                     reads=[rt, r_c], writes=[ryo])
                yield
                S.dma("sp", yT[8:12, :, tok].rearrange("g p t -> p g t"), yo[:], reads=[ryo], dwrites=[R["yT"]])

            for tt in range(NTILE if need_ctx else 16):
                for _ in tile_gen(tt):
                    yield

    def evac_o(ps, rps, f, t0, n, ost_p, sqt_p):
        ost, ro = ost_p.next()
        S.op("act", lambda e: e.activation(out=ost[:, 0:n], in_=ps[:, 0:n], func=AF.Copy), reads=[rps], writes=[ro])
        sqt, rsq = sqt_p.next()
        S.op("dve", lambda e: e.tensor_tensor(out=sqt[:, 0:n], in0=ps[:, 0:n], in1=ost[:, 0:n], op=ALU.mult), reads=[rps, ro], writes=[rsq])
        if f == 0:
            S.op("pool", lambda e: e.tensor_copy(out=acc[:, t0:t0 + n], in_=sqt[:, 0:n]), reads=[rsq], dwrites=[r_acc])
        else:
            S.op("pool", lambda e: e.tensor_tensor(out=acc[:, t0:t0 + n], in0=acc[:, t0:t0 + n], in1=sqt[:, 0:n], op=ALU.add),
                 reads=[rsq], writes=[r_acc])
        S.dma("sp", oT[f, :, t0:t0 + n], ost[:, 0:n], reads=[ro], dwrites=[R["oT"]])

    def stage_outproj(l, tbl=TB5):
        with ExitStack() as st:
            for kc in range(KC):
                S.dma("sp", bigk[:, kc, :], yT[kc, :, :], reads=[R["yT"]], dwrites=[r_big])
            wrot = Rot(nc, st, "op_w", [128, KC, 512], BF16, 2)
            ost_p = Rot(nc, st, "op_o", [128, 512], F32, 3)
            sqt_p = Rot(nc, st, "op_q", [128, 512], F32, 3)
            nxt = load_w(wrot, w_out[l], 0, 512, KC)
            for gi in range(4):
                wt, rw = nxt
                if gi + 1 < 4:
                    nxt = load_w(wrot, w_out[l], (gi + 1) * 512, 512, KC)
                for f4 in range(4):
                    for (t0, n) in tbl:
                        ps, rps = fm_acc(wt, rw, f4 * 128, 128, t0, n)
                        evac_o(ps, rps, gi * 4 + f4, t0, n, ost_p, sqt_p)
            S.barrier()

    def stage_ffn_in(l, tbl=TB5):
        with ExitStack() as st:
            wg_p = Rot(nc, st, "f1_g", [128, KC, 512], BF16, 2)
            wu_p = Rot(nc, st, "f1_u", [128, KC, 512], BF16, 2)
            sg_p = Rot(nc, st, "f1_s", [128, 512], F32, 3)
            a_p = Rot(nc, st, "f1_a", [128, 512], BF16, 3)
            Wl = w_f1[l]
            nxt = (load_w(wg_p, Wl, 0, 512, KC), load_w(wu_p, Wl, DFF, 512, KC))
            for gi in range(11):
                (wg, rwg), (wu, rwu) = nxt
                if gi + 1 < 11:
                    nxt = (load_w(wg_p, Wl, (gi + 1) * 512, 512, KC), load_w(wu_p, Wl, DFF + (gi + 1) * 512, 512, KC))
                for f4 in range(4):
                    f = gi * 4 + f4
                    for (t0, n) in tbl:
                        pg, rpg = fm_acc(wg, rwg, f4 * 128, 128, t0, n)
                        pu, rpu = fm_acc(wu, rwu, f4 * 128, 128, t0, n)
                        sg, rsg = sg_p.next()
                        S.op("act", lambda e: e.activation(out=sg[:, 0:n], in_=pg[:, 0:n], func=AF.Silu), reads=[rpg], writes=[rsg])
                        a, ra = a_p.next()
                        S.op("dve", lambda e: e.tensor_tensor(out=a[:, 0:n], in0=pu[:, 0:n], in1=sg[:, 0:n], op=ALU.mult), reads=[rpu, rsg], writes=[ra])
                        S.dma("sp", actT[f, :, t0:t0 + n], a[:, 0:n], reads=[ra], dwrites=[R["actT"]])
            S.barrier()

    def stage_ffn_out(l, tlim=T):
        PT = 1152
        bigf = big[:, 0:FC * 768].rearrange("p (k t) -> p k t", k=FC)
        with ExitStack() as st:
            ex = sb("f2_ex", [128, FC, 384], BF16, st)
            wrot = Rot(nc, st, "f2_w", [128, FC, 256], BF16, 2)
            ost_p = Rot(nc, st, "f2_o", [128, 512], F32, 3)
            sqt_p = Rot(nc, st, "f2_q", [128, 512], F32, 3)
            for p in range(2):
                base = p * PT
                first = True
                for (view, tl0, g0, n) in ((bigf, 0, 0, 512), (bigf, 512, 512, 256), (ex, 0, 768, 384)):
                    for kc in range(FC):
                        S.dma("sp", view[:, kc, tl0:tl0 + n], actT[kc, :, base + g0:base + g0 + n], dwrites=() if first else [r_big],
                              writes=[r_big] if first else ())
                        first = False
                nxt = load_w(wrot, w_f2[l], 0, 256, FC)
                for gi in range(8):
                    wt, rw = nxt
                    if gi + 1 < 8:
                        nxt = load_w(wrot, w_f2[l], (gi + 1) * 256, 256, FC)
                    for f2 in range(2):
                        for (view, tl0, g0, n) in ((bigf, 0, 0, 512), (bigf, 512, 512, 256), (ex, 0, 768, 384)):
                            t_abs = base + g0
                            if t_abs >= tlim:
                                continue
                            n = min(n, tlim - t_abs)
                            ps, rps = fm_acc(wt, rw, f2 * 128, 128, tl0, n, nk=FC, inview=view)
                            evac_o(ps, rps, gi * 2 + f2, t_abs, n, ost_p, sqt_p)
            S.barrier()

    stage_mod()
    r_x0 = Res("xT0")
    layer_scalars(0, 1)
    stage_resnorm(xT0, r_x0, None, ("A1", "B1"), xT, R["xT"])
    for l in range(depth):
        last = (l == depth - 1)
        need_ctx = not last
        tlim = T if need_ctx else TL
        tbl = TB5 if need_ctx else TB5[:4]
        layer_scalars(l, 2)
        stage_inproj(l)
        if stop_after == "inproj":
            break
        stage_gla(l)
        stage_attn(l, "B", need_ctx)
        stage_attn(l, "D", need_ctx, modl=(l + 1 if not last else None), side=lambda st_, bank_: cmlp_gen(l, need_ctx, st_, bank_))
        if stop_after == "mix":
            break
        stage_outproj(l, tbl)
        stage_resnorm(xT, R["xT"], "G1", ("A2", "B2"), xT, R["xT"], tlim=tlim)
        if stop_after == "res1":
            break
        stage_ffn_in(l, tbl)
        stage_ffn_out(l, tlim)
        if not last:
            layer_scalars(l + 1, 1)
            stage_resnorm(xT, R["xT"], "G2", ("A1", "B1"), xT, R["xT"])
        else:
            stage_resnorm(xT, R["xT"], "G2", None, outT, R["outT"], final=True)
    S.barrier()
    es.close()
    return nc


def _consts():
    c = np.zeros((128, 11, 128), np.float32)
    j = np.arange(128)[:, None]
    i = np.arange(128)[None, :]
    same = (j // 64) == (i // 64)
    c[:, 0, :] = np.eye(128, dtype=np.float32)
    c[:, 1, :] = np.where(same & (j <= i), -1.0 / 16.0, 0.0)
    c[:, 2, :] = np.where(same & (j >= i), -1.0 / 16.0, 0.0)
    c[:, 3, :] = np.where(same & (j <= i), 1.0, 0.0)
    c[:, 4, :] = np.where(same & (j >= i), 1.0, 0.0)
    c[:, 5, :] = np.where(j >= i, 1.0, 0.0)
    c[:, 6, :] = np.where(j <= i, 1.0, 0.0)
    P = np.zeros((64, 64), np.float32)
    for a in range(16):
        P[a, a + 16] = -1.0
        P[a + 16, a] = 1.0
        P[a + 32, a + 48] = -1.0
        P[a + 48, a + 32] = 1.0
    c[0:64, 7, 0:64] = P.T
    c[:, 8, :] = 1.0
    c[0:64, 9, 0:64] = P.T
    c[64:128, 9, 64:128] = P.T
    c[0:64, 10, 0:64] = 1.0
    c[64:128, 10, 64:128] = 1.0
    rows = TL // 64
    row = np.repeat(np.arange(rows), 64).astype(np.float32)
    col = np.tile(np.arange(64), rows).astype(np.float32)
    inv = (np.float32(10000.0) ** (-np.arange(16, dtype=np.float32) / np.float32(16))).astype(np.float32)
    ar, ac = row[:, None] * inv, col[:, None] * inv
    ang = np.concatenate([ar, ar, ac, ac], axis=-1).astype(np.float32)
    ct = np.cos(ang).T.astype(np.float32)
    stb = np.sin(ang).T.astype(np.float32)
    return c, np.ascontiguousarray(np.concatenate([ct, ct], 0)), np.ascontiguousarray(np.concatenate([stb, stb], 0))


def _shared_inputs(c_ctx, w_mod, b_mod, norm_g, w_in, gla_gate_up, gla_gate_b, win_sink, cm_ln_g, cm_ln_b, cm_ws, cm_bs,
                   qk_g, mix_g, w_out, w_ffn_in, w_ffn_out):
    f = lambda a: np.ascontiguousarray(np.asarray(a, dtype=np.float32))
    L = w_mod.shape[0]
    cst, cosT, sinT = _consts()
    gub = np.zeros((L, 33, 512), np.float32)
    gub[:, 0:16, 0:256] = gla_gate_up[:, 0]
    gub[:, 16:32, 256:512] = gla_gate_up[:, 1]
    gub[:, 32, 0:256] = gla_gate_b[:, 0]
    gub[:, 32, 256:512] = gla_gate_b[:, 1]
    rep = lambda a: np.ascontiguousarray(np.broadcast_to(a[None], (128,) + a.shape))
    d = {
        "w_mod": f(w_mod), "w_in": f(w_in), "w_out": f(w_out), "w_f1": f(w_ffn_in), "w_f2": f(w_ffn_out),
        "bm": f(b_mod.reshape(L, 96, 128).transpose(2, 0, 1)),
        "ng": f(norm_g.reshape(L, 4, KC, 128).transpose(3, 0, 1, 2)),
        "gub": gub,
        "sinkb": f(rep(win_sink)),
        "lng": f(rep(cm_ln_g)), "lnb": f(rep(cm_ln_b)),
        "wsT": f(cm_ws.transpose(0, 3, 1, 2)),
        "bsb": f(rep(cm_bs)),
        "qkg": f(np.concatenate([qk_g.transpose(2, 0, 1)] * 2, axis=0)),
        "ga": f(mix_g[:, 0:512].reshape(L, 4, 128).transpose(2, 0, 1)),
        "gbb": f(rep(mix_g[:, 512:1024])),
        "gc": f(mix_g[:, 1024:1536].reshape(L, 4, 128).transpose(2, 0, 1)),
        "gdb": f(rep(mix_g[:, 1536:2048])),
        "cst": cst, "cosT": cosT, "sinT": sinT,
    }
    return d


def _core_inputs(xb, cb, ctxb, c_ctx):
    xt = np.concatenate([np.asarray(xb, np.float32).T, np.asarray(ctxb, np.float32).T], axis=1)
    cv = np.stack([np.asarray(cb, np.float32), np.asarray(c_ctx, np.float32)], axis=-1)
    return {"xT0": np.ascontiguousarray(xt.reshape(KC, 128, T)),
            "cvec": np.ascontiguousarray(cv.reshape(KC, 128, 2).transpose(1, 0, 2))}


def kernel(x, c, ctx, c_ctx, w_mod, b_mod, norm_g, w_in, gla_gate_up, gla_gate_b, win_sink, cm_ln_g, cm_ln_b, cm_ws, cm_bs,
           qk_g, mix_g, w_out, w_ffn_in, w_ffn_out):
    A = lambda a: np.asarray(a)
    x, c, ctx, c_ctx = A(x), A(c), A(ctx), A(c_ctx)
    shared = _shared_inputs(A(c_ctx), A(w_mod), A(b_mod), A(norm_g), A(w_in), A(gla_gate_up), A(gla_gate_b), A(win_sink),
                            A(cm_ln_g), A(cm_ln_b), A(cm_ws), A(cm_bs), A(qk_g), A(mix_g), A(w_out), A(w_ffn_in), A(w_ffn_out))
    nb = x.shape[0]
    nc = build(depth=4)
    in_maps = []
    for b in range(nb):
        m = dict(shared)
        m.update(_core_inputs(x[b], c[b], ctx[b], c_ctx))
        in_maps.append(m)
    res = run_bass_kernel_spmd(nc, in_maps, core_ids=list(range(nb)))
    out = np.empty((nb, TL, D), np.float32)
    for b in range(nb):
        o = np.asarray(res.results[b]["outT"], dtype=np.float32).reshape(D, TL)
        out[b] = o.T
    return out
```

```python
from contextlib import ExitStack
import numpy as np
import concourse.bass as bass
import concourse.mybir as mybir
from concourse.bass_utils import run_bass_kernel_spmd

F32 = mybir.dt.float32
BF16 = mybir.dt.bfloat16
AF = mybir.ActivationFunctionType
ALU = mybir.AluOpType

D = 2048
TL = 2048
TCX = 256
T = TL + TCX
KC = 16
DFF = 5632
FC = 44
INW = 4128
EPS = 1e-6
NTILE = T // 128
TB5 = [(0, 512), (512, 512), (1024, 512), (1536, 512), (2048, 256)]
NDS = 40
SAME_ENGINE_WAIT = True
GLA_W = 2


def bc(ap, axis, shape):
    return ap.unsqueeze(axis).to_broadcast(list(shape))


class Res:
    __slots__ = ("w", "r", "name")

    def __init__(self, name=""):
        self.w = {}
        self.r = {}
        self.name = name


class Sched:
    ENG = ("pe", "act", "dve", "pool", "sp")

    def __init__(self, nc, es):
        self.nc = nc
        self.eng = {"pe": nc.tensor, "act": nc.scalar, "dve": nc.vector, "pool": nc.gpsimd, "sp": nc.sync}
        self.sem = {e: es.enter_context(nc.semaphore("s_" + e)) for e in self.ENG}
        self.cnt = {e: 0 for e in self.ENG}
        self.known = {e: {} for e in self.ENG}
        self.dsem = [es.enter_context(nc.semaphore("d%d" % i)) for i in range(NDS)]
        self.dcnt = [0] * NDS
        self.drr = 0
        self.deferred = []
        self.nwaits = 0

    def _semh(self, key):
        return self.sem[key[1]] if key[0] == "e" else self.dsem[key[1]]

    def _wait(self, e, deps):
        kn = self.known[e]
        for key, val in deps.items():
            if kn.get(key, 0) >= val:
                continue
            if key == ("e", e) and (e == "pe" or not SAME_ENGINE_WAIT):
                continue
            self.eng[e].wait_ge(self._semh(key), val)
            kn[key] = val
            self.nwaits += 1

    @staticmethod
    def _deps(reads, writes, dwrites):
        d = {}
        for r in reads:
            for k, v in r.w.items():
                if d.get(k, 0) < v:
                    d[k] = v
        for w in writes:
            for k, v in w.w.items():
                if d.get(k, 0) < v:
                    d[k] = v
            for k, v in w.r.items():
                if d.get(k, 0) < v:
                    d[k] = v
        for w in dwrites:
            for k, v in w.r.items():
                if d.get(k, 0) < v:
                    d[k] = v
        return d

    @staticmethod
    def _mark(ev, reads, writes, dwrites):
        k, v = ev
        for r in reads:
            if r.r.get(k, 0) < v:
                r.r[k] = v
        for w in writes:
            w.w = {k: v}
            w.r = {}
        for w in dwrites:
            if w.w.get(k, 0) < v:
                w.w[k] = v

    def op(self, e, fn, reads=(), writes=(), dwrites=()):
        self._wait(e, self._deps(reads, writes, dwrites))
        ins = fn(self.eng[e])
        self.cnt[e] += 1
        ins.then_inc(self.sem[e], 1)
        self._mark((("e", e), self.cnt[e]), reads, writes, dwrites)

    def dma(self, q, out, in_, reads=(), writes=(), dwrites=()):
        i = self.drr
        self.drr = (i + 1) % NDS
        deps = self._deps(reads, writes, dwrites)
        if self.dcnt[i] > 0:
            deps[("d", i)] = max(deps.get(("d", i), 0), self.dcnt[i])
        self._wait(q, deps)
        self.eng[q].dma_start(out=out, in_=in_).then_inc(self.dsem[i], 16)
        self.dcnt[i] += 16
        self._mark((("d", i), self.dcnt[i]), reads, writes, dwrites)

    def barrier(self):
        self.run_deferred(all_=True)
        for e in self.ENG:
            deps = {("e", o): self.cnt[o] for o in self.ENG if self.cnt[o] > 0}
            for i in range(NDS):
                if self.dcnt[i] > 0:
                    deps[("d", i)] = self.dcnt[i]
            self._wait(e, deps)

    def defer(self, fn):
        self.deferred.append(fn)

    def run_deferred(self, all_=False):
        while True:
            cur, self.deferred = self.deferred, []
            for fn in cur:
                nxt = fn()
                if nxt is not None:
                    self.deferred.append(nxt)
            if not all_ or not self.deferred:
                break


_UID = [0]


def _uid():
    _UID[0] += 1
    return _UID[0]


class Rot:
    def __init__(self, nc, es, name, shape, dt, n):
        u = _uid()
        self.t = [es.enter_context(nc.sbuf_tensor("%s_%d_%d" % (name, u, i), list(shape), dt)) for i in range(n)]
        self.r = [Res(name) for _ in range(n)]
        self.i = 0

    def next(self):
        i = self.i
        self.i = (i + 1) % len(self.t)
        return self.t[i], self.r[i]


class PsRot:
    def __init__(self, tiles, ress):
        self.t = tiles
        self.r = ress
        self.i = 0

    def next(self):
        i = self.i
        self.i = (i + 1) % len(self.t)
        return self.t[i], self.r[i]


def build(depth=4, debug=False, stop_after=None):
    nc = bass.Bass("TRN2", target_bir_lowering=False)
    es = ExitStack()

    def inp(name, shape, dt=F32):
        return nc.dram_tensor(name, list(shape), dt, kind="ExternalInput").ap()

    dbg_kind = "ExternalOutput" if debug else "Internal"

    def scr(name, shape, dt):
        return nc.dram_tensor(name, list(shape), dt, kind=dbg_kind).ap()

    xT0 = inp("xT0", [NTILE, 128, KC, 128])
    cvec = inp("cvec", [128, KC, 2])
    w_mod = inp("w_mod", [4, D, 6 * D])
    bm_d = inp("bm", [128, 4, 96])
    ng_d = inp("ng", [128, 4, 4, KC])
    w_in = inp("w_in", [4, D, INW])
    w_out = inp("w_out", [4, D, D])
    w_f1 = inp("w_f1", [4, D, 2 * DFF])
    w_f2 = inp("w_f2", [4, DFF, D])
    gub_d = inp("gub", [4, 33, 512])
    sink_d = inp("sinkb", [128, 4, 8])
    lng_d = inp("lng", [128, 4, 512])
    lnb_d = inp("lnb", [128, 4, 512])
    wsT_d = inp("wsT", [4, 128, 4, 128])
    bsb_d = inp("bsb", [128, 4, 4, 128])
    qkg_d = inp("qkg", [128, 4, 2])
    ga_d = inp("ga", [128, 4, 4])
    gbb_d = inp("gbb", [128, 4, 512])
    gc_d = inp("gc", [128, 4, 4])
    gdb_d = inp("gdb", [128, 4, 512])
    cst_d = inp("cst", [128, 11, 128])
    cos_d = inp("cosT", [128, TL])
    sin_d = inp("sinT", [128, TL])
    outT = nc.dram_tensor("outT", [TL // 128, 128, KC, 128], F32, kind="ExternalOutput").ap()

    xT = scr("xT", [NTILE, 128, KC, 128], F32)
    oT = scr("oT", [NTILE, 128, KC, 128], F32)
    yT = scr("yT", [KC, 128, T], BF16)
    actT = scr("actT", [FC, 128, T], BF16)
    aqT = scr("aqT", [4, 64, T], BF16)
    akT = scr("akT", [4, 64, T], BF16)
    ak_tok = scr("ak_tok", [T, 256], BF16)
    av_tok = scr("av_tok", [T, 512], BF16)
    agT = scr("agT", [4, 128, T], BF16)
    lrT = scr("lrT", [32, T], F32)
    bqT = scr("bqT", [8, 64, T], BF16)
    bkT = scr("bkT", [2, 64, T], BF16)
    bv_tok = scr("bv_tok", [T, 128], BF16)
    cuT = scr("cuT", [4, 128, T], BF16)
    cv_tok = scr("cv_tok", [T, 512], BF16)
    dqT = scr("dqT", [8, 64, T], BF16)
    dkT = scr("dkT", [2, 64, T], BF16)
    dv_tok = scr("dv_tok", [T, 128], BF16)
    R = {n: Res(n) for n in ("xT", "oT", "yT", "actT", "aqT", "akT", "ak_tok", "av_tok", "agT", "lrT", "bqT", "bkT",
                             "bv_tok", "cuT", "cv_tok", "dqT", "dkT", "dv_tok", "outT")}

    S = Sched(nc, es)

    def sb(name, shape, dt, st=es):
        return st.enter_context(nc.sbuf_tensor("%s_%d" % (name, _uid()), list(shape), dt))

    big = sb("big", [128, KC * T], BF16)
    r_big = Res("big")
    bigk = big[:].rearrange("p (k t) -> p k t", k=KC)
    cst_f = sb("cst_f", [128, 11, 128], F32)
    cst_b = sb("cst_b", [128, 11, 128], BF16)
    r_cst = Res("cst")
    modv = sb("modv", [128, 4, 96, 2], F32)
    r_modv = Res("modv")
    ng_s = sb("ng_s", [128, 4, 4, KC], F32)
    bm_s = sb("bm_s", [128, 4, 96], F32)
    r_small = Res("small")
    lay = {n: sb("lay_" + n, [128, KC, 2], F32) for n in ("A1", "B1", "G1", "A2", "B2", "G2")}
    r_lay = {1: Res("lay1"), 2: Res("lay2")}
    acc = sb("acc", [128, T], F32)
    r_acc = Res("acc")
    epsT = sb("epsT", [128, 1], F32)
    ps_t = [es.enter_context(nc.psum_tensor("ps%d" % i, [128, 512], F32)) for i in range(7)]
    ps_b = es.enter_context(nc.psum_tensor("psb", [128, 1024], BF16))
    ps_r = [Res("ps%d" % i) for i in range(7)]
    r_psb = Res("psb")
    psA = PsRot(ps_t[0:4], ps_r[0:4])
    psE = PsRot(ps_t[4:7], ps_r[4:7])

    IDENT, TRIF, TRIB, MASKF, MASKB, WLO, WHI, ROPE, ONES, ROPE2, ONES2 = range(11)

    S.dma("sp", cst_f[:], cst_d, writes=[r_cst])
    S.op("dve", lambda e: e.tensor_copy(out=cst_b[:], in_=cst_f[:]), reads=[r_cst], writes=[r_cst])
    S.op("dve", lambda e: e.memset(epsT[:], EPS), writes=[r_cst])
    S.dma("sp", ng_s[:], ng_d, writes=[r_small])
    S.dma("sp", bm_s[:], bm_d, writes=[r_small])
    ones_b = cst_b[:, ONES, :]
    ones_f = cst_f[:, ONES, :]
    ident_b = cst_b[:, IDENT, :]

    def rstd_from(dst, src, scale, rd, wr, n_part=128):
        S.op("act", lambda e: e.activation(out=dst, in_=src, func=AF.Ln, bias=epsT[0:n_part, :], scale=scale),
             reads=rd + [r_cst], writes=wr)
        S.op("act", lambda e: e.activation(out=dst, in_=dst, func=AF.Exp, scale=-0.5), reads=wr, writes=wr)

    scb = sb("m_scb", [128, KC, 2], BF16)
    r_sc = Res("scb")

    def mod_gen(l, wm, pm, rpm):
        pmv = pm[:, 0:192].rearrange("p (j s) -> p j s", s=2)
        for piece in range(24):
            wt, rw = wm.next()
            S.dma("pool", wt[:], w_mod[l, :, piece * 512:(piece + 1) * 512].rearrange("(k p) n -> p k n", p=128), writes=[rw])
            yield
            for j4 in range(4):
                j = piece * 4 + j4
                for kc in range(KC):
                    S.op("pe", lambda e, kc=kc, j=j, j4=j4, wt=wt: e.matmul(
                        pmv[:, j, :], wt[:, kc, j4 * 128:(j4 + 1) * 128], scb[:, kc, :],
                        start=(kc == 0), stop=(kc == KC - 1)), reads=[rw, r_sc], writes=[rpm])
        S.op("dve", lambda e: e.tensor_tensor(out=modv[:, l, :, :], in0=pmv, in1=bc(bm_s[:, l, :], 2, [128, 96, 2]), op=ALU.add),
             reads=[rpm, r_small], dwrites=[r_modv])

    def stage_mod():
        with ExitStack() as st:
            cv = sb("m_cv", [128, KC, 2], F32, st)
            wm = Rot(nc, st, "m_w", [128, KC, 512], BF16, 2)
            S.dma("sp", cv[:], cvec, writes=[r_sc])
            S.op("act", lambda e: e.activation(out=scb[:], in_=cv[:], func=AF.Silu), reads=[r_sc], writes=[r_sc])
            pm, rpm = psE.next()
            for _ in mod_gen(0, wm, pm, rpm):
                pass
            S.barrier()

    def layer_scalars(l, part):
        def g(i):
            return bc(ng_s[:, l, i, :], 2, [128, KC, 2])
        mv = lambda c: modv[:, l, c * KC:(c + 1) * KC, :]
        rd = [r_modv, r_small]
        o = 0 if part == 1 else 3
        sfx = "1" if part == 1 else "2"
        rl = r_lay[part]
        S.op("dve", lambda e: e.scalar_tensor_tensor(out=lay["A" + sfx][:], in0=mv(o + 1), scalar=1.0, in1=g(0 if part == 1 else 2),
                                                     op0=ALU.add, op1=ALU.mult), reads=rd, writes=[rl])
        S.op("dve", lambda e: e.tensor_copy(out=lay["B" + sfx][:], in_=mv(o + 0)), reads=rd, dwrites=[rl])
        S.op("dve", lambda e: e.tensor_tensor(out=lay["G" + sfx][:], in0=mv(o + 2), in1=g(1 if part == 1 else 3), op=ALU.mult),
             reads=rd, dwrites=[rl])

    def interleave(gens, width):
        active = []
        it = iter(gens)
        while True:
            while len(active) < width:
                g = next(it, None)
                if g is None:
                    break
                active.append(g)
            if not active:
                break
            for g in list(active):
                try:
                    next(g)
                except StopIteration:
                    active.remove(g)

    def stage_resnorm(x_src, r_xsrc, res_G, norm_AB, x_dst, r_xdst, final=False, tlim=T):
        NB = 128
        WIDTH = 4
        with ExitStack() as st:
            xb_p = Rot(nc, st, "rn_x", [128, KC, NB], F32, WIDTH + 1)
            ob_p = Rot(nc, st, "rn_o", [128, KC, NB], F32, WIDTH + 1)
            sq_p = Rot(nc, st, "rn_sq", [128, KC, NB], BF16, WIDTH)
            rs_p = Rot(nc, st, "rn_rs", [128, NB], F32, 2 * WIDTH)
            psR = PsRot(ps_t[0:7], ps_r[0:7])
            nblk = (TL if final else tlim) // NB

            def blk(b):
                t0 = b * NB
                s = 1 if t0 >= TL else 0
                xb, rx = xb_p.next()
                ob, ro = ob_p.next()
                S.dma("sp", xb[:], x_src[b], writes=[rx])
                if res_G:
                    S.dma("sp", ob[:], oT[b], writes=[ro])
                    yield
                    pr, rpr = psR.next()
                    S.op("pe", lambda e: e.matmul(pr[:, 0:NB], ones_f, acc[:, t0:t0 + NB], start=True, stop=True),
                         reads=[r_acc, r_cst], writes=[rpr])
                    yield
                    rs, rrs = rs_p.next()
                    S.op("act", lambda e: e.activation(out=rs[:], in_=pr[:, 0:NB], func=AF.Ln, bias=epsT[:, :], scale=1.0 / D),
                         reads=[rpr, r_cst], writes=[rrs])
                    yield
                    S.op("act", lambda e: e.activation(out=rs[:], in_=rs[:], func=AF.Exp, scale=-0.5), writes=[rrs])
                    yield
                    for kc in range(KC):
                        S.op("dve", lambda e, kc=kc: e.scalar_tensor_tensor(out=ob[:, kc, :], in0=ob[:, kc, :], scalar=lay[res_G][:, kc, s:s + 1],
                                                                           in1=rs[:], op0=ALU.mult, op1=ALU.mult),
                             reads=[rrs, r_lay[int(res_G[1])]], writes=[ro] if kc == 0 else (), dwrites=() if kc == 0 else [ro])
                    yield
                    S.op("pool", lambda e: e.tensor_tensor(out=xb[:, 0:10, :], in0=xb[:, 0:10, :], in1=ob[:, 0:10, :], op=ALU.add),
                         reads=[ro], writes=[rx])
                    S.op("dve", lambda e: e.tensor_tensor(out=xb[:, 10:KC, :], in0=xb[:, 10:KC, :], in1=ob[:, 10:KC, :], op=ALU.add),
                         reads=[ro], dwrites=[rx])
                yield
                if x_dst is not None:
                    S.dma("sp", x_dst[b], xb[:], reads=[rx], dwrites=[r_xdst])
                if norm_AB:
                    A, B = norm_AB
                    sq, rsq = sq_p.next()
                    S.op("act", lambda e: e.activation(out=sq[:], in_=xb[:], func=AF.Square), reads=[rx], writes=[rsq])
                    yield
                    pn, rpn = psR.next()
                    for kc in range(KC):
                        S.op("pe", lambda e, kc=kc: e.matmul(pn[:, 0:NB], ones_b, sq[:, kc, :], start=(kc == 0), stop=(kc == KC - 1)),
                             reads=[rsq, r_cst], writes=[rpn])
                    yield
                    rs, rrs = rs_p.next()
                    S.op("act", lambda e: e.activation(out=rs[:], in_=pn[:, 0:NB], func=AF.Ln, bias=epsT[:, :], scale=1.0 / D),
                         reads=[rpn, r_cst], writes=[rrs])
                    yield
                    S.op("act", lambda e: e.activation(out=rs[:], in_=rs[:], func=AF.Exp, scale=-0.5), writes=[rrs])
                    yield
                    S.op("dve", lambda e: e.tensor_tensor(out=ob[:], in0=xb[:], in1=bc(rs[:], 1, [128, KC, NB]), op=ALU.mult),
                         reads=[rx, rrs], writes=[ro])
                    yield
                    for kc in range(KC):
                        eng_ = "act" if kc % 4 != 3 else "pool"
                        if eng_ == "act":
                            S.op("act", lambda e, kc=kc: e.activation(out=bigk[:, kc, t0:t0 + NB], in_=ob[:, kc, :], func=AF.Identity,
                                                                      bias=lay[B][:, kc, s:s + 1], scale=lay[A][:, kc, s:s + 1]),
                                 reads=[ro, r_lay[int(A[1])]], dwrites=[r_big])
                        else:
                            S.op("pool", lambda e, kc=kc: e.tensor_scalar(out=bigk[:, kc, t0:t0 + NB], in0=ob[:, kc, :],
                                                                          scalar1=lay[A][:, kc, s:s + 1], scalar2=lay[B][:, kc, s:s + 1],
                                                                          op0=ALU.mult, op1=ALU.add),
                                 reads=[ro, r_lay[int(A[1])]], dwrites=[r_big])
                yield

            interleave((blk(b) for b in range(nblk)), WIDTH)
            S.barrier()

    def load_w(rot, Wl, col0, ncols, nk):
        wt, rw = rot.next()
        S.dma("pool", wt[:, 0:nk, 0:ncols], Wl[:, col0:col0 + ncols].rearrange("(k p) n -> p k n", p=128), writes=[rw])
        return wt, rw

    def fm_acc(wt, rw, c0, m, t0, n, nk=KC, inview=None):
        iv = bigk if inview is None else inview
        ps, rps = psA.next()
        for kc in range(nk):
            S.op("pe", lambda e, kc=kc: e.matmul(ps[0:m, 0:n], wt[:, kc, c0:c0 + m], iv[:, kc, t0:t0 + n],
                                                  start=(kc == 0), stop=(kc == nk - 1)), reads=[rw, r_big], writes=[rps])
        S.run_deferred()
        return ps, rps

    def tm_acc(wt, rw, c0, n, tt):
        ps, rps = psA.next()
        for kc in range(KC):
            S.op("pe", lambda e, kc=kc: e.matmul(ps[:, 0:n], bigk[:, kc, tt * 128:(tt + 1) * 128], wt[:, kc, c0:c0 + n],
                                                  start=(kc == 0), stop=(kc == KC - 1)), reads=[rw, r_big], writes=[rps])
        S.run_deferred()
        return ps, rps

    def stage_inproj(l):
        Wl = w_in[l]
        TBL = TB5
        with ExitStack() as st:
            wrot = Rot(nc, st, "ip_w", [128, KC, 512], BF16, 2)
            cosT = sb("ip_cos", [128, TL], F32, st)
            sinT = sb("ip_sin", [128, TL], F32, st)
            r_cs = Res()
            qkg = sb("ip_qkg", [128, 2], F32, st)
            lng = sb("ip_lng", [128, 512], F32, st)
            lnb = sb("ip_lnb", [128, 512], F32, st)
            S.dma("sp", cosT[:], cos_d, writes=[r_cs])
            S.dma("sp", sinT[:], sin_d, dwrites=[r_cs])
            S.dma("sp", qkg[:], qkg_d[:, l, :], dwrites=[r_cs])
            S.dma("sp", lng[:], lng_d[:, l, :], dwrites=[r_cs])
            S.dma("sp", lnb[:], lnb_d[:, l, :], dwrites=[r_cs])
            stb = Rot(nc, st, "ip_sb", [128, 512], BF16, 4)
            stf = Rot(nc, st, "ip_sf", [128, 512], F32, 4)
            st1 = Rot(nc, st, "ip_s1", [128, 8], F32, 4)
            xq_p = Rot(nc, st, "ip_xq", [128, 512], BF16, 8)
            hsq_p = Rot(nc, st, "ip_hsq", [128, 512], BF16, 6)
            hob_p = Rot(nc, st, "ip_hob", [128, 512], BF16, 6)

            def evac_copy(ps, rps, m, n, dst, rdst, eng="act", func=None, dt_f32=False):
                sbt, rs = (stf if dt_f32 else stb).next()
                if eng == "act":
                    S.op("act", lambda e: e.activation(out=sbt[0:m, 0:n], in_=ps[0:m, 0:n], func=func or AF.Copy),
                         reads=[rps], writes=[rs])
                else:
                    S.op("dve", lambda e: e.tensor_copy(out=sbt[0:m, 0:n], in_=ps[0:m, 0:n]), reads=[rps], writes=[rs])
                S.dma("sp", dst, sbt[0:m, 0:n], reads=[rs], dwrites=[rdst])

            def headproc(ps, rps, t0, n, normg, dst, rdst):
                latent = t0 < TL
                xq, rxq = xq_p.next()
                if normg is None:
                    S.op("act", lambda e: e.activation(out=xq[:, 0:n], in_=ps[:, 0:n], func=AF.Copy), reads=[rps], writes=[rxq])
                    stepB_ready = True
                else:
                    sq, rsq = hsq_p.next()
                    S.op("act", lambda e: e.activation(out=sq[:, 0:n], in_=ps[:, 0:n], func=AF.Square), reads=[rps], writes=[rsq])

                def stepB():
                    if not latent:
                        S.dma("sp", dst, xq[:, 0:n], reads=[rxq], dwrites=[rdst])
                        return None
                    pr, rpr = psE.next()
                    S.op("pe", lambda e: e.matmul(pr[:, 0:n], cst_b[:, ROPE2, :], xq[:, 0:n], start=True, stop=True),
                         reads=[rxq, r_cst], writes=[rpr])
                    t1, rt1 = stf.next()
                    t2, rt2 = stf.next()
                    ob, rob = hob_p.next()
                    S.op("dve", lambda e: e.tensor_tensor(out=t1[:, 0:n], in0=xq[:, 0:n], in1=cosT[:, t0:t0 + n], op=ALU.mult),
                         reads=[rxq, r_cs], writes=[rt1])
                    S.op("dve", lambda e: e.tensor_tensor(out=t2[:, 0:n], in0=pr[:, 0:n], in1=sinT[:, t0:t0 + n], op=ALU.mult),
                         reads=[rpr, r_cs], writes=[rt2])
                    S.op("pool", lambda e: e.tensor_tensor(out=ob[:, 0:n], in0=t1[:, 0:n], in1=t2[:, 0:n], op=ALU.add),
                         reads=[rt1, rt2], writes=[rob])
                    S.dma("sp", dst, ob[:, 0:n], reads=[rob], dwrites=[rdst])
                    return None

                def stepA():
                    pn, rpn = psE.next()
                    S.op("pe", lambda e: e.matmul(pn[:, 0:n], cst_b[:, ONES2, :], sq[:, 0:n], start=True, stop=True),
                         reads=[rsq, r_cst], writes=[rpn])
                    rs, rrs = stf.next()
                    rstd_from(rs[:, 0:n], pn[:, 0:n], 1.0 / 64, [rpn], [rrs], n_part=128)
                    S.op("dve", lambda e: e.scalar_tensor_tensor(out=xq[:, 0:n], in0=ps[:, 0:n], scalar=qkg[:, normg:normg + 1],
                                                                 in1=rs[:, 0:n], op0=ALU.mult, op1=ALU.mult),
                         reads=[rps, rrs, r_cs], writes=[rxq])
                    return stepB

                S.defer(stepB if normg is None else stepA)

            def fm_heads(wt, rw, c0, nheads, dst, rdst, normg=None, plain=False, func=None, dt_f32=False, m=64):
                if m == 64:
                    for hp_ in range(nheads // 2):
                        for (t0, n) in TBL:
                            ps, rps = fm_acc(wt, rw, c0 + hp_ * 128, 128, t0, n)
                            d2 = dst[2 * hp_:2 * hp_ + 2, :, t0:t0 + n].rearrange("h p t -> (h p) t")
                            if plain:
                                evac_copy(ps, rps, 128, n, d2, rdst, func=func, dt_f32=dt_f32)
                            else:
                                headproc(ps, rps, t0, n, normg, d2, rdst)
                    return
                for h in range(nheads):
                    for (t0, n) in TBL:
                        ps, rps = fm_acc(wt, rw, c0 + h * m, m, t0, n)
                        evac_copy(ps, rps, m, n, dst[h, :, t0:t0 + n] if dst.ndim == 3 else dst[:, t0:t0 + n], rdst,
                                  func=func, dt_f32=dt_f32)

            def tm_cols(wt, rw, c0, n, dst, rdst):
                for tt in range(NTILE):
                    ps, rps = tm_acc(wt, rw, c0, n, tt)
                    evac_copy(ps, rps, 128, n, dst[tt * 128:(tt + 1) * 128, :], rdst, eng="dve")

            groups = [(0, 512), (512, 512), (1024, 512), (1536, 32), (1568, 512), (2080, 256), (2336, 512), (2848, 512),
                      (3360, 512), (3872, 256)]
            nxt = load_w(wrot, Wl, groups[0][0], groups[0][1], KC)
            for gi, (c0g, ncg) in enumerate(groups):
                wt, rw = nxt
                if gi + 1 < len(groups):
                    nxt = load_w(wrot, Wl, groups[gi + 1][0], groups[gi + 1][1], KC)
                if gi == 0:
                    fm_heads(wt, rw, 0, 4, aqT, R["aqT"], plain=True)
                    fm_heads(wt, rw, 256, 4, akT, R["akT"], plain=True)
                    tm_cols(wt, rw, 256, 256, ak_tok, R["ak_tok"])
                elif gi == 1:
                    tm_cols(wt, rw, 0, 512, av_tok, R["av_tok"])
                elif gi == 2:
                    fm_heads(wt, rw, 0, 4, agT, R["agT"], plain=True, func=AF.Silu, m=128)
                elif gi == 3:
                    fm_heads(wt, rw, 0, 1, lrT, R["lrT"], plain=True, dt_f32=True, m=32)
                elif gi == 4:
                    fm_heads(wt, rw, 0, 8, bqT, R["bqT"])
                elif gi == 5:
                    fm_heads(wt, rw, 0, 2, bkT, R["bkT"])
                    tm_cols(wt, rw, 128, 128, bv_tok, R["bv_tok"])
                elif gi == 6:
                    fm_heads(wt, rw, 0, 4, cuT, R["cuT"], plain=True, func=AF.Gelu_apprx_tanh, m=128)
                elif gi == 7:
                    for tt in range(NTILE):
                        ps, rps = tm_acc(wt, rw, 0, 512, tt)
                        g, rg = stf.next()
                        S.op("act", lambda e: e.activation(out=g[:], in_=ps[:], func=AF.Gelu_apprx_tanh), reads=[rps], writes=[rg])
                        s1, rs1 = st1.next()
                        S.op("dve", lambda e: e.bn_stats(out=s1[:, 0:6], in_=g[:]), reads=[rg], writes=[rs1])
                        S.op("dve", lambda e: e.bn_aggr(out=s1[:, 6:8], in_=s1[:, 0:6]), reads=[rs1], writes=[rs1])
                        rstd_from(s1[:, 7:8], s1[:, 7:8], 1.0, [rs1], [rs1])
                        S.op("dve", lambda e: e.scalar_tensor_tensor(out=s1[:, 6:7], in0=s1[:, 6:7], scalar=-1.0, in1=s1[:, 7:8],
                                                                     op0=ALU.mult, op1=ALU.mult), reads=[rs1], writes=[rs1])
                        S.op("act", lambda e: e.activation(out=g[:], in_=g[:], func=AF.Identity, bias=s1[:, 6:7], scale=s1[:, 7:8]),
                             reads=[rs1], writes=[rg])
                        S.op("dve", lambda e: e.tensor_tensor(out=g[:], in0=g[:], in1=lng[:], op=ALU.mult), reads=[r_cs], writes=[rg])
                        ob, rob = stb.next()
                        S.op("pool", lambda e: e.tensor_tensor(out=ob[:], in0=g[:], in1=lnb[:], op=ALU.add), reads=[rg, r_cs], writes=[rob])
                        S.dma("sp", cv_tok[tt * 128:(tt + 1) * 128, :], ob[:], reads=[rob], dwrites=[R["cv_tok"]])
                elif gi == 8:
                    fm_heads(wt, rw, 0, 8, dqT, R["dqT"], normg=0)
                elif gi == 9:
                    fm_heads(wt, rw, 0, 2, dkT, R["dkT"], normg=1)
                    tm_cols(wt, rw, 128, 128, dv_tok, R["dv_tok"])
            S.barrier()

    def stage_gla(l):
        with ExitStack() as st:
            gub = sb("g_gub", [33, 512], F32, st)
            ga = sb("g_ga", [128, 4], F32, st)
            r_g = Res()
            S.dma("sp", gub[:], gub_d[l], writes=[r_g])
            S.dma("sp", ga[:], ga_d[:, l, :], dwrites=[r_g])
            of_s = sb("g_of", [128, 4, T], BF16, st)
            r_of = Res()
            stored = set()
            gT_p = Rot(nc, st, "g_g", [128, 4, 128], BF16, 3)
            tot_p = Rot(nc, st, "g_tot", [128, 4, 128], F32, 3)
            sq_p = Rot(nc, st, "g_sq", [128, 4, 128], BF16, 3)
            rs_p = Rot(nc, st, "g_rs", [128, 4, 128], F32, 3)
            yo_p = Rot(nc, st, "g_yo", [128, 4, 128], BF16, 3)
            ps7 = ps_b[:].bitcast(F32)
            BK = [dict(A=(ps_t[0], ps_r[0]), B=(ps_t[1], ps_r[1]), C=(ps_t[2], ps_r[2]), P=(ps_t[6], ps_r[6])),
                  dict(A=(ps_t[3], ps_r[3]), B=(ps_t[4], ps_r[4]), C=(ps_t[5], ps_r[5]), P=(ps7, r_psb))]
            DS = []
            for d in (0, 1):
                o = {}
                o["Sf"] = sb("g_Sf", [64, 4, 128], F32, st)
                o["Sb"] = sb("g_Sb", [64, 4, 128], BF16, st)
                o["rS"] = Res()
                o["lr"] = Rot(nc, st, "g_lr", [33, 128], F32, 2)
                for t_, _r in zip(o["lr"].t, o["lr"].r):
                    S.op("dve", lambda e, t_=t_: e.memset(t_[32:33, :], 1.0), writes=[_r])
                o["sp"] = Rot(nc, st, "g_sp", [128, 256], F32, 2)
                o["Ep"] = Rot(nc, st, "g_Ep", [64, 4, 128], F32, 2)
                o["Em"] = Rot(nc, st, "g_Em", [64, 4, 128], F32, 2)
                o["Emt"] = Rot(nc, st, "g_Emt", [128, 256], F32, 2)
                o["qT"] = Rot(nc, st, "g_q", [64, 4, 128], BF16, 2)
                o["kT"] = Rot(nc, st, "g_k", [64, 4, 128], BF16, 2)
                o["ktok"] = Rot(nc, st, "g_kt", [128, 256], BF16, 2)
                o["v"] = Rot(nc, st, "g_v", [128, 512], BF16, 2)
                o["qt"] = Rot(nc, st, "g_qt", [64, 4, 128], BF16, 2)
                o["kt"] = Rot(nc, st, "g_ktl", [64, 4, 128], BF16, 2)
                o["ktt"] = Rot(nc, st, "g_ktt", [128, 256], BF16, 2)
                o["am"] = Rot(nc, st, "g_am", [128, 4, 128], BF16, 2)
                DS.append(o)

            def prep(d, tt, P):
                o = DS[d]
                tok = slice(tt * 128, (tt + 1) * 128)
                tri = cst_f[:, TRIF + d, :]
                lr, rlr = o["lr"].next()
                S.dma("sp", lr[0:32, :], lrT[:, tok], writes=[rlr])
                qT, rq = o["qT"].next()
                S.dma("sp", qT[:], aqT[:, :, tok].rearrange("h p t -> p h t"), writes=[rq])
                kT, rk = o["kT"].next()
                S.dma("sp", kT[:], akT[:, :, tok].rearrange("h p t -> p h t"), writes=[rk])
                ktok, rkt = o["ktok"].next()
                S.dma("sp", ktok[:], ak_tok[tok, :], writes=[rkt])
                v, rv = o["v"].next()
                S.dma("sp", v[:], av_tok[tok, :], writes=[rv])
                yield
                pz, rpz = BK[d]["P"]
                S.op("pe", lambda e: e.matmul(pz[:, 0:256], lr[:, :], gub[:, d * 256:(d + 1) * 256], start=True, stop=True),
                     reads=[rlr, r_g], writes=[rpz])
                yield
                sp, rsp = o["sp"].next()
                S.op("act", lambda e: e.activation(out=sp[:], in_=pz[:, 0:256], func=AF.Exp, scale=-1.0), reads=[rpz], writes=[rsp])
                yield
                S.op("act", lambda e: e.activation(out=sp[:], in_=sp[:], func=AF.Ln, bias=cst_f[:, ONES, 0:1]), reads=[r_cst], writes=[rsp])
                yield
                pc, rpc = BK[d]["P"]
                pcv = pc[0:64, :].rearrange("p (h t) -> p h t", h=4)
                for h in range(4):
                    S.op("pe", lambda e, h=h: e.matmul(pcv[:, h, :], sp[:, h * 64:(h + 1) * 64], tri, start=True, stop=True),
                         reads=[rsp, r_cst], writes=[rpc])
                yield
                Ep, rEp = o["Ep"].next()
                Em, rEm = o["Em"].next()
                Emt, rEmt = o["Emt"].next()
                S.op("act", lambda e: e.activation(out=Ep[:], in_=pcv, func=AF.Exp), reads=[rpc], writes=[rEp])
                S.op("act", lambda e: e.activation(out=Em[:], in_=pcv, func=AF.Exp, scale=-1.0), reads=[rpc], writes=[rEm])
                yield
                pct, rpct = BK[d]["P"]
                S.op("pe", lambda e: e.matmul(pct[:, 0:256], tri, sp[:], start=True, stop=True), reads=[rsp, r_cst], writes=[rpct])
                yield
                S.op("act", lambda e: e.activation(out=Emt[:], in_=pct[:, 0:256], func=AF.Exp, scale=-1.0), reads=[rpct], writes=[rEmt])
                yield
                qt, rqt = o["qt"].next()
                kt, rktl = o["kt"].next()
                ktt, rktt = o["ktt"].next()
                S.op("dve", lambda e: e.scalar_tensor_tensor(out=qt[:], in0=qT[:], scalar=0.125, in1=Ep[:], op0=ALU.mult, op1=ALU.mult),
                     reads=[rq, rEp], writes=[rqt])
                S.op("pool", lambda e: e.tensor_tensor(out=kt[:], in0=kT[:], in1=Em[:], op=ALU.mult), reads=[rk, rEm], writes=[rktl])
                S.op("pool", lambda e: e.tensor_tensor(out=ktt[:], in0=ktok[:], in1=Emt[:], op=ALU.mult), reads=[rkt, rEmt], writes=[rktt])
                P.update(dict(v=v, rv=rv, Ep=Ep, rEp=rEp, qt=qt, rqt=rqt, kt=kt, rktl=rktl, ktt=ktt, rktt=rktt))
                yield

            def scan(d, tt, P):
                o = DS[d]
                Sf, Sb, r_S = o["Sf"], o["Sb"], o["rS"]
                tok = slice(tt * 128, (tt + 1) * 128)
                mask = cst_b[:, MASKF + d, :]
                v, rv, Ep, rEp, qt, rqt, kt, rktl, ktt, rktt = (P[k] for k in ("v", "rv", "Ep", "rEp", "qt", "rqt", "kt", "rktl", "ktt", "rktt"))
                pa, rpa = BK[d]["A"]
                pav = pa[:, :].rearrange("p (h t) -> p h t", h=4)
                for h in range(4):
                    S.op("pe", lambda e, h=h: e.matmul(pav[:, h, :], kt[:, h, :], qt[:, h, :], start=True, stop=True),
                         reads=[rktl, rqt], writes=[rpa])
                cA, cB = (0, 1) if d == 0 else (1, 0)
                psu = {}
                for c in (cA, cB):
                    pu, rpu = BK[d]["B" if c == cA else "C"]
                    puv = pu[0:64, :].rearrange("p (h t) -> p h t", h=4)
                    for h in range(4):
                        S.op("pe", lambda e, h=h, c=c, puv=puv: e.matmul(puv[:, h, :], ktt[c * 64:(c + 1) * 64, h * 64:(h + 1) * 64],
                                                                        v[c * 64:(c + 1) * 64, h * 128:(h + 1) * 128], start=True, stop=True),
                             reads=[rktt, rv], writes=[rpu])
                    psu[c] = (puv, rpu)
                yield
                am, ram = o["am"].next()
                S.op("dve", lambda e: e.tensor_tensor(out=am[:], in0=pav, in1=bc(mask, 1, [128, 4, 128]), op=ALU.mult),
                     reads=[rpa, r_cst], writes=[ram])
                yield
                po, rpo = BK[d]["A"]
                pov = po[:, :].rearrange("p (h t) -> p h t", h=4)
                for h in range(4):
                    S.op("pe", lambda e, h=h: e.matmul(pov[:, h, :], v[:, h * 128:(h + 1) * 128], am[:, h, :], start=(h == 0), stop=False,
                                                       skip_group_check=True), reads=[rv, ram], writes=[rpo])
                for ci, c in enumerate((cA, cB)):
                    for h in range(4):
                        S.op("pe", lambda e, h=h, c=c, ci=ci: e.matmul(pov[:, h, c * 64:(c + 1) * 64], Sb[:, h, :], qt[:, h, c * 64:(c + 1) * 64],
                                                                       start=False, stop=(ci == 1 and h == 3), skip_group_check=True),
                             reads=[r_S, rqt], writes=[rpo])
                    yield
                    puv, rpu = psu[c]
                    col = (c * 64 + 63) if d == 0 else (c * 64)
                    S.op("dve", lambda e, puv=puv: e.tensor_tensor(out=Sf[:], in0=puv, in1=Sf[:], op=ALU.add), reads=[rpu], writes=[r_S])
                    S.op("dve", lambda e, col=col: e.tensor_tensor(out=Sf[:], in0=Sf[:], in1=bc(Ep[:, :, col], 2, [64, 4, 128]), op=ALU.mult),
                         reads=[rEp], writes=[r_S])
                    yield
                    S.op("pool", lambda e: e.tensor_copy(out=Sb[:], in_=Sf[:]), writes=[r_S])
                    yield
                if tt not in stored:
                    stored.add(tt)
                    S.op("act", lambda e: e.activation(out=of_s[:, :, tok], in_=pov, func=AF.Copy), reads=[rpo], dwrites=[r_of])
                    yield
                    return
                gT, rgT = gT_p.next()
                S.dma("sp", gT[:], agT[:, :, tok].rearrange("h p t -> p h t"), writes=[rgT])
                tot, rtot = tot_p.next()
                S.op("dve", lambda e: e.tensor_tensor(out=tot[:], in0=pov, in1=of_s[:, :, tok], op=ALU.add), reads=[rpo, r_of], writes=[rtot])
                yield
                sq, rsq = sq_p.next()
                S.op("act", lambda e: e.activation(out=sq[:], in_=tot[:], func=AF.Square), reads=[rtot], writes=[rsq])
                yield
                pn, rpn = BK[d]["A"]
                S.op("pe", lambda e: e.matmul(pn[:, :], ones_b, sq[:].rearrange("p h t -> p (h t)"), start=True, stop=True),
                     reads=[rsq, r_cst], writes=[rpn])
                yield
                rs, rrs = rs_p.next()
                rsf = rs[:].rearrange("p h t -> p (h t)")
                S.op("act", lambda e: e.activation(out=rsf, in_=pn[:, :], func=AF.Ln, bias=epsT[:, :], scale=1.0 / 128), reads=[rpn, r_cst], writes=[rrs])
                yield
                S.op("act", lambda e: e.activation(out=rsf, in_=rsf, func=AF.Exp, scale=-0.5), writes=[rrs])
                yield
                S.op("dve", lambda e: e.tensor_tensor(out=tot[:], in0=tot[:], in1=rs[:], op=ALU.mult), reads=[rrs], writes=[rtot])
                yield
                S.op("pool", lambda e: e.tensor_tensor(out=tot[:], in0=tot[:], in1=gT[:], op=ALU.mult), reads=[rgT], writes=[rtot])
                yield
                yo, ryo = yo_p.next()
                S.op("dve", lambda e: e.tensor_tensor(out=yo[:], in0=tot[:], in1=bc(ga[:], 2, [128, 4, 128]), op=ALU.mult),
                     reads=[rtot, r_g], writes=[ryo])
                yield
                S.dma("sp", yT[0:4, :, tok].rearrange("h p t -> p h t"), yo[:], reads=[ryo], dwrites=[R["yT"]])

            def dir_gen(d):
                o = DS[d]
                S.op("dve", lambda e: e.memset(o["Sf"][:], 0.0), writes=[o["rS"]])
                S.op("dve", lambda e: e.memset(o["Sb"][:], 0.0), dwrites=[o["rS"]])
                order = [16, 17] + list(range(16)) if d == 0 else [17, 16] + list(range(15, -1, -1))
                P = {}
                for _ in prep(d, order[0], P):
                    yield
                for k, tt in enumerate(order):
                    Pn = {}
                    active = [scan(d, tt, P)]
                    if k + 1 < len(order):
                        active.append(prep(d, order[k + 1], Pn))
                    while active:
                        for g_ in list(active):
                            try:
                                next(g_)
                            except StopIteration:
                                active.remove(g_)
                        yield
                    P = Pn

            interleave([dir_gen(0), dir_gen(1)], GLA_W)
            S.barrier()

    def stage_attn(l, which, need_ctx, modl=None, side=None):
        qsrc, ksrc, vsrc = (bqT, bkT, bv_tok) if which == "B" else (dqT, dkT, dv_tok)
        rq_, rk_, rv_ = (R["bqT"], R["bkT"], R["bv_tok"]) if which == "B" else (R["dqT"], R["dkT"], R["dv_tok"])
        ych0 = 4 if which == "B" else 12
        with ExitStack() as st:
            QT = sb("a_Q", [128, 2, 4, T], BF16, st)
            KT = sb("a_K", [128, T], BF16, st)
            VX = sb("a_V", [128, NTILE, 2, 65], BF16, st)
            gv = sb("a_gv", [128, 512], F32, st)
            sinkE = sb("a_sink", [128, 8], F32, st)
            r_in = Res()
            S.op("dve", lambda e: e.memset(VX[:], 1.0), writes=[r_in])
            S.op("pool", lambda e: e.memset(QT[64:128, 0, :, :], 0.0), dwrites=[r_in])
            S.op("dve", lambda e: e.memset(QT[0:64, 1, :, :], 0.0), dwrites=[r_in])
            for (t0, n) in TB5:
                for g in range(2):
                    S.dma("sp", QT[64 * g:64 * g + 64, g, :, t0:t0 + n], qsrc[4 * g:4 * g + 4, :, t0:t0 + n].rearrange("h p t -> p h t"),
                          dwrites=[r_in])
            for g in range(2):
                S.dma("sp", KT[64 * g:64 * g + 64, :], ksrc[g], dwrites=[r_in])
            r_vx = Res()
            r_sk = Res()
            for g in range(2):
                S.dma("sp", VX[:, :, g, 0:64], vsrc[:, g * 64:(g + 1) * 64].rearrange("(t p) d -> p t d", p=128),
                      reads=[rv_, r_in], dwrites=[r_vx])
            S.dma("sp", gv[:], (gbb_d if which == "B" else gdb_d)[:, l, :], dwrites=[r_in])
            if which == "B":
                S.dma("sp", sinkE[:], sink_d[:, l, :], writes=[r_sk])
                S.op("act", lambda e: e.activation(out=sinkE[:], in_=sinkE[:], func=AF.Exp), writes=[r_sk])
            pT_p = Rot(nc, st, "a_pT", [128, 4, 128], BF16, 5)
            psS = PsRot(ps_t[0:3], ps_r[0:3])
            gen = None
            sgen = side(st, (ps_b[:].bitcast(F32)[:, 256:512], r_psb)) if side is not None else None
            if modl is not None:
                wm = Rot(nc, st, "a_wm", [128, KC, 512], BF16, 2)
                gen = mod_gen(modl, wm, ps_t[3], ps_r[3])
            ob_p = Rot(nc, st, "a_ob", [128, 8, 64], F32, 2)
            s1_p = Rot(nc, st, "a_s1", [128, 20], F32, 2)
            junk_p = Rot(nc, st, "a_jk", [128, 512], BF16, 1)
            y_p = Rot(nc, st, "a_y", [128, 512], BF16, 2)
            ys_p = Rot(nc, st, "a_ys", [128, 4, 128], BF16, 2)
            qtiles = list(range(16)) + ([16, 17] if need_ctx else [])
            use_sink = (which == "B")
            items = []
            for qt in qtiles:
                if qt >= 16:
                    keys = [(16, None), (17, None)]
                elif which == "B":
                    keys = []
                    if qt > 0:
                        keys.append((qt - 1, WLO))
                    keys.append((qt, None))
                    if qt < 15:
                        keys.append((qt + 1, WHI))
                    keys += [(16, None), (17, None)]
                else:
                    keys = [(k, None) for k in range(NTILE)]
                for g in range(2):
                    for ki, (kt, mk) in enumerate(keys):
                        items.append((qt, g, ki, kt, mk, len(keys)))

            def emit_qk(it):
                qt, g, ki, kt, mk, nk = it
                ps_, rps = psS.next()
                psv = ps_[:, :].rearrange("p (h t) -> p h t", h=4)
                S.op("pe", lambda e: e.matmul(psv, KT[:, kt * 128:(kt + 1) * 128], QT[:, g, :, qt * 128:(qt + 1) * 128],
                                              start=True, stop=True), reads=[r_in], writes=[rps])
                return psv, rps

            def finalize(qt, pos):
                ob, rob = ob_p.next()
                s1, rs1 = s1_p.next()
                for g in range(2):
                    pov, rpo = pos[g]
                    if use_sink:
                        S.op("dve", lambda e, g=g, pov=pov: e.tensor_tensor(out=s1[:, 4 * g:4 * g + 4], in0=pov[:, :, 64],
                                                                            in1=sinkE[:, 4 * g:4 * g + 4], op=ALU.add),
                             reads=[rpo, r_sk], writes=[rs1])
                    else:
                        S.op("dve", lambda e, g=g, pov=pov: e.tensor_copy(out=s1[:, 4 * g:4 * g + 4], in_=pov[:, :, 64]), reads=[rpo], writes=[rs1])
                    S.op("dve", lambda e, g=g: e.reciprocal(out=s1[:, 8 + 4 * g:12 + 4 * g], in_=s1[:, 4 * g:4 * g + 4]), reads=[rs1], writes=[rs1])
                    S.op("dve", lambda e, g=g, pov=pov: e.tensor_tensor(out=ob[:, 4 * g:4 * g + 4, :], in0=pov[:, :, 0:64],
                                                                        in1=bc(s1[:, 8 + 4 * g:12 + 4 * g], 2, [128, 4, 64]), op=ALU.mult),
                         reads=[rpo, rs1], writes=[rob])
                obf = ob[:].rearrange("p h d -> p (h d)")
                jk, rjk = junk_p.next()
                S.op("act", lambda e: e.activation(out=jk[:], in_=obf, func=AF.Square, accum_out=s1[:, 16:17]), reads=[rob], writes=[rjk, rs1])
                rstd_from(s1[:, 17:18], s1[:, 16:17], 1.0 / 512, [rs1], [rs1])
                y, ry = y_p.next()
                S.op("dve", lambda e: e.scalar_tensor_tensor(out=y[:], in0=obf, scalar=s1[:, 17:18], in1=gv[:], op0=ALU.mult, op1=ALU.mult),
                     reads=[rob, rs1, r_in], writes=[ry])

                def part2():
                    ptv = ps_b[:, 0:512].rearrange("p (c t) -> p c t", c=4)
                    for c in range(4):
                        S.op("pe", lambda e, c=c: e.transpose(ptv[:, c, :], y[:, c * 128:(c + 1) * 128], ident_b), reads=[ry, r_cst], writes=[r_psb])
                    ys, rys = ys_p.next()
                    S.op("act", lambda e: e.activation(out=ys[:], in_=ptv, func=AF.Copy), writes=[r_psb, rys])
                    S.dma("sp", yT[ych0:ych0 + 4, :, qt * 128:(qt + 1) * 128].rearrange("c p t -> p c t"), ys[:], reads=[rys], dwrites=[R["yT"]])
                return part2

            LOOK = 2
            pend = []
            nxt_i = 0
            fin_q = []
            pos = []
            for idx, it in enumerate(items):
                while len(pend) < LOOK + 1 and nxt_i < len(items):
                    pend.append(emit_qk(items[nxt_i]))
                    nxt_i += 1
                psv, rps = pend.pop(0)
                qt, g, ki, kt, mk, nk = it
                if ki == 0:
                    if g == 0:
                        pos = []
                    po, rpo = psE.next()
                    pov = po[:, 0:260].rearrange("p (h d) -> p h d", h=4)
                    pos.append((pov, rpo))
                pov, rpo = pos[g]
                pT, rpT = pT_p.next()
                S.op("act", lambda e, pT=pT, psv=psv: e.activation(out=pT[:], in_=psv, func=AF.Exp, scale=0.125), reads=[rps], writes=[rpT])
                if mk is not None:
                    S.op("dve", lambda e, pT=pT, mk=mk: e.tensor_tensor(out=pT[:], in0=pT[:], in1=bc(cst_b[:, mk, :], 1, [128, 4, 128]),
                                                                        op=ALU.mult), reads=[r_cst], writes=[rpT])
                for h4 in range(4):
                    S.op("pe", lambda e, h4=h4, pT=pT, kt=kt, g=g, ki=ki, pov=pov, nk=nk: e.matmul(
                        pov[:, h4, :], pT[:, h4, :], VX[:, kt, g, :], start=(ki == 0 and h4 == 0),
                        stop=(ki == nk - 1 and h4 == 3), skip_group_check=True), reads=[rpT, r_vx, r_in], writes=[rpo])
                for f_ in fin_q:
                    f_[0] -= 1
                while fin_q and fin_q[0][0] <= 0:
                    fin_q.pop(0)[1]()
                if g == 1 and ki == nk - 1:
                    fin_q.append([3, finalize(qt, pos)])
                if gen is not None and idx % 12 == 11:
                    next(gen, None)
                if sgen is not None and idx % 2 == 0:
                    next(sgen, None)
            for f_ in fin_q:
                f_[1]()
            if gen is not None:
                for _ in gen:
                    pass
            if sgen is not None:
                for _ in sgen:
                    pass
            S.barrier()

    def cmlp_gen(l, need_ctx, st, bank):
        if True:
            wsf = sb("c_wsf", [128, 4, 128], BF16, st)
            bsb = sb("c_bsb", [128, 4, 128], F32, st)
            gc = sb("c_gc", [128, 4], F32, st)
            r_c = Res()
            S.dma("pool", wsf[:], wsT_d[l], writes=[r_c])
            S.dma("sp", bsb[:], bsb_d[:, l, :, :], dwrites=[r_c])
            S.dma("sp", gc[:], gc_d[:, l, :], dwrites=[r_c])
            W_ = 1
            cv_p = Rot(nc, st, "c_cv", [128, 512], BF16, W_ + 1)
            cu_p = Rot(nc, st, "c_cu", [128, 4, 128], BF16, W_ + 1)
            t_p = Rot(nc, st, "c_t", [128, 4, 128], F32, 1)
            sq_p = Rot(nc, st, "c_sq", [128, 4, 128], BF16, 1)
            rs_p = Rot(nc, st, "c_rs", [128, 128], F32, 1)
            yo_p = Rot(nc, st, "c_yo", [128, 4, 128], BF16, W_ + 1)

            def tile_gen(tt):
                tok = slice(tt * 128, (tt + 1) * 128)
                cv, rcv = cv_p.next()
                S.dma("sp", cv[:], cv_tok[tok, :], writes=[rcv])
                cu, rcu = cu_p.next()
                S.dma("sp", cu[:], cuT[:, :, tok].rearrange("g p t -> p g t"), writes=[rcu])
                yield
                t, rt = t_p.next()
                reg, rreg = bank
                psv = reg[:, 0:256].rearrange("p (g t) -> p g t", g=2)
                for half in range(2):
                    for gg in range(2):
                        g = half * 2 + gg
                        S.op("pe", lambda e, g=g, gg=gg: e.matmul(psv[:, gg, :], cv[:, g * 128:(g + 1) * 128], wsf[:, g, :], start=True, stop=True),
                             reads=[rcv, r_c], writes=[rreg])
                    yield
                    S.op("dve", lambda e, half=half: e.tensor_tensor(out=t[:, 2 * half:2 * half + 2, :], in0=psv, in1=bsb[:, 2 * half:2 * half + 2, :],
                                                                     op=ALU.add), reads=[r_c], writes=[rreg, rt] if half == 0 else [rreg],
                         dwrites=() if half == 0 else [rt])
                    yield
                S.op("pool", lambda e: e.tensor_tensor(out=t[:], in0=t[:], in1=cu[:], op=ALU.mult), reads=[rcu], writes=[rt])
                yield
                sq, rsq = sq_p.next()
                S.op("act", lambda e: e.activation(out=sq[:], in_=t[:], func=AF.Square), reads=[rt], writes=[rsq])
                yield
                pn, rpn = bank
                for g in range(4):
                    S.op("pe", lambda e, g=g: e.matmul(pn[:, 0:128], ones_b, sq[:, g, :], start=(g == 0), stop=(g == 3)),
                         reads=[rsq, r_cst], writes=[rpn])
                yield
                rs, rrs = rs_p.next()
                S.op("act", lambda e: e.activation(out=rs[:], in_=pn[:, 0:128], func=AF.Ln, bias=epsT[:, :], scale=1.0 / 512),
                     reads=[r_cst], writes=[rpn, rrs])
                yield
                S.op("act", lambda e: e.activation(out=rs[:], in_=rs[:], func=AF.Exp, scale=-0.5), writes=[rrs])
                yield
                S.op("dve", lambda e: e.tensor_tensor(out=t[:], in0=t[:], in1=bc(rs[:], 1, [128, 4, 128]), op=ALU.mult), reads=[rrs], writes=[rt])
                yo, ryo = yo_p.next()
                S.op("dve", lambda e: e.tensor_tensor(out=yo[:], in0=t[:], in1=bc(gc[:], 2, [128, 4, 128]), op=ALU.mult),
                     reads=[rt, r_c], writes=[ryo])
                yield
                S.dma("sp", yT[8:12, :, tok].rearrange("g p t -> p g t"), yo[:], reads=[ryo], dwrites=[R["yT"]])

            for tt in range(NTILE if need_ctx else 16):
                for _ in tile_gen(tt):
                    yield

    def evac_o(ps, rps, f, t0, n, ost_p, sqt_p):
        ost, ro = ost_p.next()
        S.op("act", lambda e: e.activation(out=ost[:, 0:n], in_=ps[:, 0:n], func=AF.Copy), reads=[rps], writes=[ro])
        sqt, rsq = sqt_p.next()
        S.op("dve", lambda e: e.tensor_tensor(out=sqt[:, 0:n], in0=ps[:, 0:n], in1=ost[:, 0:n], op=ALU.mult), reads=[rps, ro], writes=[rsq])
        if f == 0:
            S.op("pool", lambda e: e.tensor_copy(out=acc[:, t0:t0 + n], in_=sqt[:, 0:n]), reads=[rsq], dwrites=[r_acc])
        else:
            S.op("pool", lambda e: e.tensor_tensor(out=acc[:, t0:t0 + n], in0=acc[:, t0:t0 + n], in1=sqt[:, 0:n], op=ALU.add),
                 reads=[rsq], writes=[r_acc])
        S.dma("sp", oT[t0 // 128:(t0 + n) // 128, :, f, :].rearrange("b p t -> p b t"), ost[:, 0:n].rearrange("p (b t) -> p b t", t=128),
              reads=[ro], dwrites=[R["oT"]])

    def stage_outproj(l, tbl=TB5):
        with ExitStack() as st:
            for kc in range(KC):
                S.dma("sp", bigk[:, kc, :], yT[kc, :, :], reads=[R["yT"]], dwrites=[r_big])
            wrot = Rot(nc, st, "op_w", [128, KC, 512], BF16, 2)
            ost_p = Rot(nc, st, "op_o", [128, 512], F32, 3)
            sqt_p = Rot(nc, st, "op_q", [128, 512], F32, 3)
            nxt = load_w(wrot, w_out[l], 0, 512, KC)
            for gi in range(4):
                wt, rw = nxt
                if gi + 1 < 4:
                    nxt = load_w(wrot, w_out[l], (gi + 1) * 512, 512, KC)
                for f4 in range(4):
                    for (t0, n) in tbl:
                        ps, rps = fm_acc(wt, rw, f4 * 128, 128, t0, n)
                        evac_o(ps, rps, gi * 4 + f4, t0, n, ost_p, sqt_p)
            S.barrier()

    def stage_ffn_in(l, tbl=TB5):
        with ExitStack() as st:
            wg_p = Rot(nc, st, "f1_g", [128, KC, 512], BF16, 2)
            wu_p = Rot(nc, st, "f1_u", [128, KC, 512], BF16, 2)
            sg_p = Rot(nc, st, "f1_s", [128, 512], F32, 3)
            a_p = Rot(nc, st, "f1_a", [128, 512], BF16, 3)
            Wl = w_f1[l]
            nxt = (load_w(wg_p, Wl, 0, 512, KC), load_w(wu_p, Wl, DFF, 512, KC))
            for gi in range(11):
                (wg, rwg), (wu, rwu) = nxt
                if gi + 1 < 11:
                    nxt = (load_w(wg_p, Wl, (gi + 1) * 512, 512, KC), load_w(wu_p, Wl, DFF + (gi + 1) * 512, 512, KC))
                for f4 in range(4):
                    f = gi * 4 + f4
                    for (t0, n) in tbl:
                        pg, rpg = fm_acc(wg, rwg, f4 * 128, 128, t0, n)
                        pu, rpu = fm_acc(wu, rwu, f4 * 128, 128, t0, n)
                        sg, rsg = sg_p.next()
                        S.op("act", lambda e: e.activation(out=sg[:, 0:n], in_=pg[:, 0:n], func=AF.Silu), reads=[rpg], writes=[rsg])
                        a, ra = a_p.next()
                        S.op("dve", lambda e: e.tensor_tensor(out=a[:, 0:n], in0=pu[:, 0:n], in1=sg[:, 0:n], op=ALU.mult), reads=[rpu, rsg], writes=[ra])
                        S.dma("sp", actT[f, :, t0:t0 + n], a[:, 0:n], reads=[ra], dwrites=[R["actT"]])
            S.barrier()

    def stage_ffn_out(l, tlim=T):
        PT = 1152
        bigf = big[:, 0:FC * 768].rearrange("p (k t) -> p k t", k=FC)
        with ExitStack() as st:
            ex = sb("f2_ex", [128, FC, 384], BF16, st)
            wrot = Rot(nc, st, "f2_w", [128, FC, 256], BF16, 2)
            ost_p = Rot(nc, st, "f2_o", [128, 512], F32, 3)
            sqt_p = Rot(nc, st, "f2_q", [128, 512], F32, 3)
            for p in range(2):
                base = p * PT
                first = True
                for (view, tl0, g0, n) in ((bigf, 0, 0, 512), (bigf, 512, 512, 256), (ex, 0, 768, 384)):
                    for kc in range(FC):
                        S.dma("sp", view[:, kc, tl0:tl0 + n], actT[kc, :, base + g0:base + g0 + n], dwrites=() if first else [r_big],
                              writes=[r_big] if first else ())
                        first = False
                nxt = load_w(wrot, w_f2[l], 0, 256, FC)
                for gi in range(8):
                    wt, rw = nxt
                    if gi + 1 < 8:
                        nxt = load_w(wrot, w_f2[l], (gi + 1) * 256, 256, FC)
                    for f2 in range(2):
                        for (view, tl0, g0, n) in ((bigf, 0, 0, 512), (bigf, 512, 512, 256), (ex, 0, 768, 384)):
                            t_abs = base + g0
                            if t_abs >= tlim:
                                continue
                            n = min(n, tlim - t_abs)
                            ps, rps = fm_acc(wt, rw, f2 * 128, 128, tl0, n, nk=FC, inview=view)
                            evac_o(ps, rps, gi * 2 + f2, t_abs, n, ost_p, sqt_p)
            S.barrier()

    stage_mod()
    r_x0 = Res("xT0")
    layer_scalars(0, 1)
    stage_resnorm(xT0, r_x0, None, ("A1", "B1"), xT, R["xT"])
    for l in range(depth):
        last = (l == depth - 1)
        need_ctx = not last
        tlim = T if need_ctx else TL
        tbl = TB5 if need_ctx else TB5[:4]
        layer_scalars(l, 2)
        stage_inproj(l)
        if stop_after == "inproj":
            break
        stage_gla(l)
        stage_attn(l, "B", need_ctx)
        stage_attn(l, "D", need_ctx, modl=(l + 1 if not last else None), side=lambda st_, bank_: cmlp_gen(l, need_ctx, st_, bank_))
        if stop_after == "mix":
            break
        stage_outproj(l, tbl)
        stage_resnorm(xT, R["xT"], "G1", ("A2", "B2"), xT, R["xT"], tlim=tlim)
        if stop_after == "res1":
            break
        stage_ffn_in(l, tbl)
        stage_ffn_out(l, tlim)
        if not last:
            layer_scalars(l + 1, 1)
            stage_resnorm(xT, R["xT"], "G2", ("A1", "B1"), xT, R["xT"])
        else:
            stage_resnorm(xT, R["xT"], "G2", None, outT, R["outT"], final=True)
    S.barrier()
    es.close()
    return nc


def _consts():
    c = np.zeros((128, 11, 128), np.float32)
    j = np.arange(128)[:, None]
    i = np.arange(128)[None, :]
    same = (j // 64) == (i // 64)
    c[:, 0, :] = np.eye(128, dtype=np.float32)
    c[:, 1, :] = np.where(same & (j <= i), -1.0 / 16.0, 0.0)
    c[:, 2, :] = np.where(same & (j >= i), -1.0 / 16.0, 0.0)
    c[:, 3, :] = np.where(same & (j <= i), 1.0, 0.0)
    c[:, 4, :] = np.where(same & (j >= i), 1.0, 0.0)
    c[:, 5, :] = np.where(j >= i, 1.0, 0.0)
    c[:, 6, :] = np.where(j <= i, 1.0, 0.0)
    P = np.zeros((64, 64), np.float32)
    for a in range(16):
        P[a, a + 16] = -1.0
        P[a + 16, a] = 1.0
        P[a + 32, a + 48] = -1.0
        P[a + 48, a + 32] = 1.0
    c[0:64, 7, 0:64] = P.T
    c[:, 8, :] = 1.0
    c[0:64, 9, 0:64] = P.T
    c[64:128, 9, 64:128] = P.T
    c[0:64, 10, 0:64] = 1.0
    c[64:128, 10, 64:128] = 1.0
    rows = TL // 64
    row = np.repeat(np.arange(rows), 64).astype(np.float32)
    col = np.tile(np.arange(64), rows).astype(np.float32)
    inv = (np.float32(10000.0) ** (-np.arange(16, dtype=np.float32) / np.float32(16))).astype(np.float32)
    ar, ac = row[:, None] * inv, col[:, None] * inv
    ang = np.concatenate([ar, ar, ac, ac], axis=-1).astype(np.float32)
    ct = np.cos(ang).T.astype(np.float32)
    stb = np.sin(ang).T.astype(np.float32)
    return c, np.ascontiguousarray(np.concatenate([ct, ct], 0)), np.ascontiguousarray(np.concatenate([stb, stb], 0))


def _shared_inputs(c_ctx, w_mod, b_mod, norm_g, w_in, gla_gate_up, gla_gate_b, win_sink, cm_ln_g, cm_ln_b, cm_ws, cm_bs,
                   qk_g, mix_g, w_out, w_ffn_in, w_ffn_out):
    f = lambda a: np.ascontiguousarray(np.asarray(a, dtype=np.float32))
    L = w_mod.shape[0]
    cst, cosT, sinT = _consts()
    gub = np.zeros((L, 33, 512), np.float32)
    gub[:, 0:16, 0:256] = gla_gate_up[:, 0]
    gub[:, 16:32, 256:512] = gla_gate_up[:, 1]
    gub[:, 32, 0:256] = gla_gate_b[:, 0]
    gub[:, 32, 256:512] = gla_gate_b[:, 1]
    rep = lambda a: np.ascontiguousarray(np.broadcast_to(a[None], (128,) + a.shape))
    d = {
        "w_mod": f(w_mod), "w_in": f(w_in), "w_out": f(w_out), "w_f1": f(w_ffn_in), "w_f2": f(w_ffn_out),
        "bm": f(b_mod.reshape(L, 96, 128).transpose(2, 0, 1)),
        "ng": f(norm_g.reshape(L, 4, KC, 128).transpose(3, 0, 1, 2)),
        "gub": gub,
        "sinkb": f(rep(win_sink)),
        "lng": f(rep(cm_ln_g)), "lnb": f(rep(cm_ln_b)),
        "wsT": f(cm_ws.transpose(0, 3, 1, 2)),
        "bsb": f(rep(cm_bs)),
        "qkg": f(np.concatenate([qk_g.transpose(2, 0, 1)] * 2, axis=0)),
        "ga": f(mix_g[:, 0:512].reshape(L, 4, 128).transpose(2, 0, 1)),
        "gbb": f(rep(mix_g[:, 512:1024])),
        "gc": f(mix_g[:, 1024:1536].reshape(L, 4, 128).transpose(2, 0, 1)),
        "gdb": f(rep(mix_g[:, 1536:2048])),
        "cst": cst, "cosT": cosT, "sinT": sinT,
    }
    return d


def _core_inputs(xb, cb, ctxb, c_ctx):
    xt = np.concatenate([np.asarray(xb, np.float32).T, np.asarray(ctxb, np.float32).T], axis=1)
    cv = np.stack([np.asarray(cb, np.float32), np.asarray(c_ctx, np.float32)], axis=-1)
    return {"xT0": np.ascontiguousarray(xt.reshape(KC, 128, NTILE, 128).transpose(2, 1, 0, 3)),
            "cvec": np.ascontiguousarray(cv.reshape(KC, 128, 2).transpose(1, 0, 2))}


def kernel(x, c, ctx, c_ctx, w_mod, b_mod, norm_g, w_in, gla_gate_up, gla_gate_b, win_sink, cm_ln_g, cm_ln_b, cm_ws, cm_bs,
           qk_g, mix_g, w_out, w_ffn_in, w_ffn_out):
    A = lambda a: np.asarray(a)
    x, c, ctx, c_ctx = A(x), A(c), A(ctx), A(c_ctx)
    shared = _shared_inputs(A(c_ctx), A(w_mod), A(b_mod), A(norm_g), A(w_in), A(gla_gate_up), A(gla_gate_b), A(win_sink),
                            A(cm_ln_g), A(cm_ln_b), A(cm_ws), A(cm_bs), A(qk_g), A(mix_g), A(w_out), A(w_ffn_in), A(w_ffn_out))
    nb = x.shape[0]
    nc = build(depth=4)
    in_maps = []
    for b in range(nb):
        m = dict(shared)
        m.update(_core_inputs(x[b], c[b], ctx[b], c_ctx))
        in_maps.append(m)
    res = run_bass_kernel_spmd(nc, in_maps, core_ids=list(range(nb)))
    out = np.empty((nb, TL, D), np.float32)
    for b in range(nb):
        o = np.asarray(res.results[b]["outT"], dtype=np.float32).reshape(TL // 128, 128, KC, 128)
        out[b] = o.transpose(0, 3, 2, 1).reshape(TL, D)
    return out
```
